# Optimizing a Trainium2 kernel written in Bass

```python
import math
import jax, jax.numpy as jnp
from jax import lax
import numpy as np

D_MODEL = 1024
BATCH = 4
SEQ = 8192
DEPTH = 1

HEAD_DIM = 64
RWKV_HEADS = 8
RWKV_WIDTH = RWKV_HEADS * HEAD_DIM
ATT_HEADS = 8
ATT_WIDTH = ATT_HEADS * HEAD_DIM
IDX_HEADS = 8
IDX_DIM = 64
DECAY_RANK = 64
AAA_RANK = 64
GATE_RANK = 128
MAX_TOPK = 256
Q_BLOCK = 128
ROPE_THETA = 10000.0
D_FF = ((8 * D_MODEL // 3 + 255) // 256) * 256
NORM_EPS = 1e-6
LNX_EPS = 64e-5

RWKV_COLS = 3 * RWKV_WIDTH + DECAY_RANK + AAA_RANK + GATE_RANK
ATT_COLS = 3 * ATT_WIDTH
IDX_COLS = IDX_HEADS * IDX_DIM + IDX_DIM + IDX_HEADS
GATE_COLS = 2 * D_MODEL
IN_COLS = RWKV_COLS + ATT_COLS + IDX_COLS + GATE_COLS

kernel_name = "hybrid_rwkv7_dsa_gated_block"


def rmsnorm(x, g):
    xf = x.astype(jnp.float32)
    y = xf * lax.rsqrt(jnp.mean(xf * xf, axis=-1, keepdims=True) + NORM_EPS)
    return (y * g.astype(jnp.float32)).astype(x.dtype)


def rope(t):
    S, D = t.shape[1], t.shape[-1]
    half = D // 2
    inv = 1.0 / (ROPE_THETA ** (jnp.arange(half, dtype=jnp.float32) * 2.0 / D))
    ang = jnp.arange(S, dtype=jnp.float32)[:, None] * inv[None, :]
    cos = jnp.cos(ang)[None, :, None, :]
    sin = jnp.sin(ang)[None, :, None, :]
    tf = t.astype(jnp.float32)
    t1, t2 = tf[..., :half], tf[..., half:]
    return jnp.concatenate([t1 * cos - t2 * sin, t1 * sin + t2 * cos], axis=-1).astype(t.dtype)


def token_shift(p, mu):
    prev = jnp.pad(p, ((0, 0), (1, 0), (0, 0)))[:, :-1]
    return p + (prev - p) * mu


def rwkv7_mix(p, w_decay_up, w0, a_up, a0, g_up, k_k, k_a, r_k, lnx_g, lnx_b):
    B, S, _ = p.shape
    H, N = RWKV_HEADS, HEAD_DIM
    c = np.cumsum([RWKV_WIDTH, RWKV_WIDTH, RWKV_WIDTH, DECAY_RANK, AAA_RANK])
    r, k, v, wd, ad, gd = jnp.split(p, c, axis=-1)
    w = -jax.nn.softplus(-(w0 + jnp.tanh(wd) @ w_decay_up)) - 0.5
    a = jax.nn.sigmoid(a0 + ad @ a_up)
    g = jax.nn.sigmoid(gd) @ g_up
    kk = (k * k_k).reshape(B, S, H, N).astype(jnp.float32)
    kk = kk / jnp.maximum(jnp.linalg.norm(kk, axis=-1, keepdims=True), 1e-12)
    k = k * (1.0 + (a - 1.0) * k_a)
    heads = lambda t: t.reshape(B, S, H, N).astype(jnp.float32)
    r_h, k_h, v_h, a_h = heads(r), heads(k), heads(v), heads(a)
    decay = jnp.exp(-jnp.exp(heads(w)))
    tm = lambda t: jnp.moveaxis(t, 1, 0)

    def step(state, inp):
        r_t, d_t, k_t, v_t, kk_t, a_t = inp
        sa = jnp.einsum('bhij,bhj->bhi', state, -kk_t)
        state = (state * d_t[:, :, None, :]
                 + sa[..., None] * (kk_t * a_t)[:, :, None, :]
                 + v_t[..., None] * k_t[:, :, None, :])
        y_t = jnp.einsum('bhij,bhj->bhi', state, r_t)
        return state, y_t

    state0 = jnp.zeros((B, H, N, N), jnp.float32)
    _, y = lax.scan(step, state0, (tm(r_h), tm(decay), tm(k_h), tm(v_h), tm(kk), tm(a_h)))
    y = jnp.moveaxis(y, 0, 1)
    mean = jnp.mean(y, axis=-1, keepdims=True)
    var = jnp.mean(jnp.square(y - mean), axis=-1, keepdims=True)
    y = (y - mean) * lax.rsqrt(var + LNX_EPS)
    y = y * lnx_g.reshape(H, N) + lnx_b.reshape(H, N)
    bonus = jnp.sum(r_h * k_h * r_k.astype(jnp.float32), axis=-1, keepdims=True) * v_h
    y = (y + bonus).reshape(B, S, RWKV_WIDTH).astype(p.dtype)
    return y * g


def dsa_attention(q, k, v, q_idx, k_idx, w_idx):
    B, S, H, Dh = q.shape
    topk = min(MAX_TOPK, S // 4)
    nb = S // Q_BLOCK
    blk = lambda t: jnp.moveaxis(t.reshape((B, nb, Q_BLOCK) + t.shape[2:]), 1, 0)
    key_pos = jnp.arange(S)
    idx_scale = IDX_DIM ** -0.5 * IDX_HEADS ** -0.5

    def one_block(args):
        qb, qib, wib, bi = args
        t = bi * Q_BLOCK + jnp.arange(Q_BLOCK)
        sc = jax.nn.relu(jnp.einsum('bqhd,bsd->bqhs', qib, k_idx).astype(jnp.float32))
        score = jnp.einsum('bqhs,bqh->bqs', sc, wib.astype(jnp.float32)) * idx_scale
        causal = key_pos[None, None, :] <= t[None, :, None]
        score = jnp.where(causal, score, -jnp.inf)
        _, sel = lax.top_k(score, topk)
        valid = sel <= t[None, :, None]
        kg = jax.vmap(lambda kb, ib: kb[ib])(k, sel)
        vg = jax.vmap(lambda vb, ib: vb[ib])(v, sel)
        logits = jnp.einsum('bqhd,bqkhd->bqhk', qb, kg).astype(jnp.float32) * Dh ** -0.5
        logits = jnp.where(valid[:, :, None, :], logits, -jnp.inf)
        prob = jax.nn.softmax(logits, axis=-1).astype(vg.dtype)
        return jnp.einsum('bqhk,bqkhd->bqhd', prob, vg)

    out = lax.map(one_block, (blk(q), blk(q_idx), blk(w_idx), jnp.arange(nb)))
    return jnp.moveaxis(out, 0, 1).reshape(B, S, H * Dh)


def setup_inputs(seed: int = 0) -> dict:
    key = jax.random.key(seed)
    ks = jax.random.split(key, 24)
    nrm = lambda k, shape, s: jax.random.normal(k, shape, jnp.float32) * s
    return {
        "x": jax.random.normal(ks[0], (BATCH, SEQ, D_MODEL), jnp.float32),
        "norm1_g": 1.0 + nrm(ks[1], (D_MODEL,), 0.05),
        "w_in": nrm(ks[2], (D_MODEL, IN_COLS), D_MODEL ** -0.5),
        "tshift_mu": jax.random.uniform(ks[3], (RWKV_COLS,), jnp.float32),
        "w_decay_up": nrm(ks[4], (DECAY_RANK, RWKV_WIDTH), 0.5 * DECAY_RANK ** -0.5),
        "w0": nrm(ks[5], (RWKV_WIDTH,), 0.5),
        "a_up": nrm(ks[6], (AAA_RANK, RWKV_WIDTH), 0.5 * AAA_RANK ** -0.5),
        "a0": nrm(ks[7], (RWKV_WIDTH,), 0.5),
        "g_up": nrm(ks[8], (GATE_RANK, RWKV_WIDTH), GATE_RANK ** -0.5),
        "k_k": 0.85 + nrm(ks[9], (RWKV_WIDTH,), 0.05),
        "k_a": 1.0 + nrm(ks[10], (RWKV_WIDTH,), 0.05),
        "r_k": nrm(ks[11], (RWKV_HEADS, HEAD_DIM), 0.1),
        "lnx_g": 1.0 + nrm(ks[12], (RWKV_WIDTH,), 0.05),
        "lnx_b": nrm(ks[13], (RWKV_WIDTH,), 0.01),
        "w_o_rwkv": nrm(ks[14], (RWKV_WIDTH, D_MODEL), RWKV_WIDTH ** -0.5),
        "w_o_att": nrm(ks[15], (ATT_WIDTH, D_MODEL), ATT_WIDTH ** -0.5),
        "w_out": nrm(ks[16], (D_MODEL, D_MODEL), D_MODEL ** -0.5),
        "norm2_g": 1.0 + nrm(ks[17], (D_MODEL,), 0.05),
        "w_ffn_in": nrm(ks[18], (D_MODEL, 2 * D_FF), D_MODEL ** -0.5),
        "w_ffn_out": nrm(ks[19], (D_FF, D_MODEL), D_FF ** -0.5),
        "normf_g": 1.0 + nrm(ks[20], (D_MODEL,), 0.05),
    }


def reference(x, norm1_g, w_in, tshift_mu, w_decay_up, w0, a_up, a0, g_up, k_k, k_a, r_k,
              lnx_g, lnx_b, w_o_rwkv, w_o_att, w_out, norm2_g, w_ffn_in, w_ffn_out, normf_g):
    B, S, _ = x.shape
    h = x
    for _layer in range(DEPTH):
        u = rmsnorm(h, norm1_g)
        proj = u @ w_in
        c = np.cumsum([RWKV_COLS, ATT_COLS, IDX_COLS])
        p_rwkv, p_att, p_idx, p_gate = jnp.split(proj, c, axis=-1)

        y_a = rwkv7_mix(token_shift(p_rwkv, tshift_mu), w_decay_up, w0, a_up, a0, g_up,
                        k_k, k_a, r_k, lnx_g, lnx_b)

        q, k, v = jnp.split(p_att, 3, axis=-1)
        hv = lambda t: t.reshape(B, S, ATT_HEADS, HEAD_DIM)
        q, k, v = rope(hv(q)), rope(hv(k)), hv(v)
        ci = np.cumsum([IDX_HEADS * IDX_DIM, IDX_DIM])
        q_idx, k_idx, w_idx = jnp.split(p_idx, ci, axis=-1)
        q_idx = rope(q_idx.reshape(B, S, IDX_HEADS, IDX_DIM))
        k_idx = rope(k_idx[:, :, None, :])[:, :, 0, :]
        y_b = dsa_attention(q, k, v, q_idx, k_idx, w_idx)

        g_a, g_b = jnp.split(jax.nn.sigmoid(p_gate), 2, axis=-1)
        merged = g_a * (y_a @ w_o_rwkv) + g_b * (y_b @ w_o_att)
        h = h + merged @ w_out

        z = rmsnorm(h, norm2_g) @ w_ffn_in
        z_gate, z_up = jnp.split(z, 2, axis=-1)
        h = h + (jax.nn.silu(z_gate) * z_up) @ w_ffn_out
    return rmsnorm(h, normf_g)
```

```python
import math
from contextlib import ExitStack
import numpy as np
import concourse.bass as bass
import concourse.mybir as mybir
from concourse.bass_utils import run_bass_kernel_spmd

F32 = mybir.dt.float32
BF16 = mybir.dt.bfloat16
ALU = mybir.AluOpType
AF = mybir.ActivationFunctionType
AX = mybir.AxisListType

D = 1024
KC = 8
NEG = -1.0e30
EPOCH = 4000
NDSLOT = 16
NBISECT = 12
CDEC = -math.exp(-0.5)


class T:
    def __init__(self, t):
        self.t = t
        self.w = None
        self.r = {}

    def __getitem__(self, k):
        return self.t[k]


class Eng:
    def __init__(self, kb, key, obj):
        self.kb, self.key, self.obj = kb, key, obj
        self.n = 0
        self.sems = []
        self.known = {}
        self.dn = 0
        self.dsems = None
        self.dcount = [0] * NDSLOT

    def sem_for(self, seq):
        ep = (seq - 1) // EPOCH
        while len(self.sems) <= ep:
            self.sems.append(self.kb.newsem())
        return self.sems[ep], (seq - 1) % EPOCH + 1


class KB:
    def __init__(self, nc):
        self.nc = nc
        self.es = ExitStack()
        self.nsem = 0
        self.pe = Eng(self, "pe", nc.tensor)
        self.act = Eng(self, "act", nc.scalar)
        self.dve = Eng(self, "dve", nc.vector)
        self.pool = Eng(self, "pool", nc.gpsimd)
        self.sp = Eng(self, "sp", nc.sync)
        self.engs = [self.pe, self.act, self.dve, self.pool, self.sp]
        self.dmatoks = {}
        self.psb = []
        self.psi = 0
        self.nrot = 8
        self.nm = 0

    def newsem(self):
        self.nsem += 1
        return self.es.enter_context(self.nc.semaphore("s%d" % self.nsem))

    def sb(self, shape, dt=F32, name=None):
        self.nm += 1
        return T(self.es.enter_context(self.nc.sbuf_tensor("%s_%d" % (name or "t", self.nm), list(shape), dt)))

    def init_psum(self):
        for i in range(8):
            self.psb.append(T(self.es.enter_context(self.nc.psum_tensor("ps%d" % i, [128, 512], F32))))
            self.psb[-1].excl = True

    def psum(self):
        b = self.psb[self.psi % self.nrot]
        self.psi += 1
        return b

    def psum_acc(self):
        self.pai = getattr(self, "pai", 0) + 1
        return self.psb[6 + self.pai % 2]

    def resolve(self, tok):
        key, seq = tok
        if isinstance(key, tuple):
            return self.dmatoks[key], seq
        e = getattr(self, key)
        return e.sem_for(seq)

    def _wait(self, E, tok):
        key, seq = tok
        if E.known.get(key, 0) >= seq:
            return
        E.known[key] = seq
        sem, val = self.resolve(tok)
        E.obj.wait_ge(sem, val)

    def _deps(self, E, r, w):
        for b in r:
            if b.w is not None:
                self._wait(E, b.w)
            if getattr(b, "excl", False):
                for k, tok in b.r.items():
                    if k != E.key:
                        self._wait(E, tok)
        for b in w:
            if b.w is not None and b.w[0] != E.key:
                self._wait(E, b.w)
            for k, tok in b.r.items():
                if k != E.key:
                    self._wait(E, tok)

    def op(self, E, fn, r=(), w=()):
        self._deps(E, r, w)
        ins = fn(E.obj)
        E.n += 1
        sem, val = E.sem_for(E.n)
        ins.then_inc(sem, 1)
        tok = (E.key, E.n)
        for b in r:
            b.r[E.key] = tok
        for b in w:
            b.w = tok
            b.r = {}

    def dma(self, Q, out, in_, r=(), w=()):
        self._deps(Q, r, w)
        if Q.dsems is None:
            Q.dsems = [self.newsem() for _ in range(NDSLOT)]
            for i in range(NDSLOT):
                self.dmatoks[("dma", Q.key, i)] = Q.dsems[i]
        slot = Q.dn % NDSLOT
        Q.dn += 1
        if Q.dcount[slot] > 0:
            self._wait(Q, (("dma", Q.key, slot), Q.dcount[slot] * 16))
        Q.dcount[slot] += 1
        Q.obj.dma_start(out=out, in_=in_).then_inc(Q.dsems[slot], 16)
        tok = (("dma", Q.key, slot), Q.dcount[slot] * 16)
        for b in r:
            b.r[tok[0]] = tok
        for b in w:
            b.w = tok
            b.r = {}

    def barrier(self):
        toks = []
        for e in self.engs:
            if e.n > 0:
                toks.append((e.key, e.n))
            if e.dsems is not None:
                for i in range(NDSLOT):
                    if e.dcount[i] > 0:
                        toks.append((("dma", e.key, i), e.dcount[i] * 16))
        for e in self.engs:
            for tok in toks:
                if tok[0] != e.key:
                    self._wait(e, tok)

    def mm(self, ps, out, lhsT, rhs, start, stop, r):
        self.op(self.pe, lambda e: e.matmul(out, lhsT, rhs, start=start, stop=stop), r=r, w=[ps])

    def tr(self, ps, out, in_, ident, r):
        self.op(self.pe, lambda e: e.transpose(out, in_, ident), r=r, w=[ps])

    def acopy(self, out, in_, r, w, func=AF.Copy, scale=1.0, bias=None, accum=None):
        def f(e):
            kw = {}
            if bias is not None:
                kw["bias"] = bias
            if accum is not None:
                kw["accum_out"] = accum
            return e.activation(out=out, in_=in_, func=func, scale=scale, **kw)
        self.op(self.act, f, r=r, w=w)

    def tt(self, E, out, a, b, op, r, w):
        self.op(E, lambda e: e.tensor_tensor(out=out, in0=a, in1=b, op=op), r=r, w=w)

    def ts(self, E, out, a, s1, op0, r, w, s2=None, op1=None, accum=None):
        def f(e):
            kw = {}
            if accum is not None:
                kw["accum_out"] = accum
            return e.tensor_scalar(out=out, in0=a, scalar1=s1, scalar2=s2, op0=op0,
                                   op1=(op1 if op1 is not None else ALU.bypass), **kw)
        self.op(E, f, r=r, w=w)

    def stt(self, E, out, a, s, b, op0, op1, r, w):
        self.op(E, lambda e: e.scalar_tensor_tensor(out=out, in0=a, scalar=s, in1=b, op0=op0, op1=op1), r=r, w=w)


def bc(ap, shape):
    return ap.to_broadcast(list(shape))


class Ctx:
    pass


def load_cast(kb, C, dst, dst_ap_fn, src_ap_fn, ncols, rows_shape, step=2048):
    sap, dap = src_ap_fn(0, ncols), dst_ap_fn(0, ncols)
    for k in range(sap.shape[1]):
        kb.dma(kb.pool, dap[:, k, :], sap[:, k, :], w=[dst])


def rms_uT(kb, C, xt, gcol, uT, dst=None):
    kb.acopy(C.junk[:, 0:D], xt[:, :], r=[xt], w=[C.junk, C.ss], func=AF.Square, accum=C.ss[:, :])
    kb.acopy(C.rt[:, :], C.ss[:, :], r=[C.ss], w=[C.rt], func=AF.Sqrt, scale=1.0 / D, bias=C.epsn[:, :])
    kb.op(kb.dve, lambda e: e.reciprocal(out=C.rstd[:, :], in_=C.rt[:, :]), r=[C.rt], w=[C.rstd])
    kb.ts(kb.dve, C.xn[:, :], xt[:, :], C.rstd[:, :], ALU.mult, r=[xt, C.rstd], w=[C.xn])
    ps = kb.psum()
    psv = ps[:, :].bitcast(BF16)
    for kc in range(KC):
        kb.tr(ps, psv[:, kc * 128:(kc + 1) * 128], C.xn[:, kc * 128:(kc + 1) * 128], C.ident[:, :], r=[C.xn, C.ident])
    kb.tt(kb.dve, (uT[:, :, :] if dst is None else dst), psv.rearrange("p (k t) -> p k t", t=128), bc(gcol[:, :].unsqueeze(2), [128, KC, 128]),
          ALU.mult, r=[ps, gcol], w=[uT])


def rope_tm(kb, C, ps, nh, cs, out, out_t):
    pv = ps[:, 0:nh * 64].rearrange("p (h two d) -> p h two d", two=2, d=32)
    cosb = bc(cs[:, 0:1, :].unsqueeze(1), [128, nh, 2, 32])
    sinb = bc(cs[:, 1:2, :].unsqueeze(1), [128, nh, 2, 32])
    A = C.ropeA[:, 0:nh * 64].rearrange("p (h two d) -> p h two d", two=2, d=32)
    Bm = C.ropeB[:, 0:nh * 64].rearrange("p (h two d) -> p h two d", two=2, d=32)
    ov = out.rearrange("p (h two d) -> p h two d", two=2, d=32)
    kb.tt(kb.dve, A, pv, cosb, ALU.mult, r=[ps, cs], w=[C.ropeA])
    kb.tt(kb.dve, Bm, pv, sinb, ALU.mult, r=[ps, cs], w=[C.ropeB])
    kb.tt(kb.pool, ov[:, :, 0:1, :], A[:, :, 0:1, :], Bm[:, :, 1:2, :], ALU.subtract, r=[C.ropeA, C.ropeB], w=[out_t])
    kb.tt(kb.pool, ov[:, :, 1:2, :], A[:, :, 1:2, :], Bm[:, :, 0:1, :], ALU.add, r=[C.ropeA, C.ropeB], w=[out_t])


def build(S, debug=False):
    NT = S // 128
    NQ = S // 256
    SO = S // 2
    nc = bass.Bass("TRN2", target_bir_lowering=False)
    kb = KB(nc)
    C = Ctx()

    def din(name, shape, dt=F32):
        return nc.dram_tensor(name, list(shape), dt, kind="ExternalInput").ap()

    def dscr(name, shape, dt):
        return nc.dram_tensor(name, list(shape), dt, kind=("ExternalOutput" if debug else "Internal")).ap()

    xfull = din("xfull", [S, D]); xown = din("xown", [SO, D])
    g1col = din("g1col", [128, 8]); g2col = din("g2col", [128, 8]); gfrow = din("gfrow", [1, D])
    w_rw = din("w_rw", [D, 1792]); w_kv = din("w_kv", [D, 1088]); w_qi = din("w_qi", [D, 1032]); w_gate = din("w_gate", [D, 2048])
    mucol = din("mucol", [128, 14]); wdu = din("wdu", [64, 512]); w0row = din("w0row", [1, 512])
    aup = din("aup", [64, 512]); a0col = din("a0col", [128, 4]); gup = din("gup", [128, 512])
    kkcol = din("kkcol", [128, 4]); kacol = din("kacol", [128, 4]); rkcol = din("rkcol", [128, 4])
    lnxg = din("lnxg", [1, 512]); lnxb = din("lnxb", [1, 512])
    w_orw = din("w_orw", [512, D]); w_oatt = din("w_oatt", [512, D]); w_out = din("w_out", [D, D])
    w_ffi = din("w_ffi", [D, 5632]); w_ffo = din("w_ffo", [2816, D])
    cs_full = din("cs_full", [S, 64]); cs_own = din("cs_own", [SO, 64])
    identd = din("identd", [128, 128]); LLd = din("LLd", [128, 256]); maskUd = din("maskUd", [128, 256]); maskLd = din("maskLd", [128, 128])
    bonesd = din("bonesd", [128, 128]); bones2d = din("bones2d", [128, 2]); maskaddd = din("maskaddd", [128, 256])
    pard = din("pard", [128, 2]); pow2d = din("pow2d", [128, NBISECT])
    out = nc.dram_tensor("out", [SO, D], F32, kind="ExternalOutput").ap()

    KTs = dscr("KTs", [4, 128, S], BF16)
    Vs = dscr("Vs", [4, 128, NT, 130], BF16)
    KIs = dscr("KIs", [64, S], BF16)
    YAs = dscr("YAs", [SO, 512], BF16)
    YBs = dscr("YBs", [SO, 512], BF16)
    H1s = dscr("H1s", [SO, D], F32)
    YAfull = dscr("YAfull", [S, 512], F32) if debug else None

    kb.init_psum()
    sp, pe, act, dve, pool = kb.sp, kb.pe, kb.act, kb.dve, kb.pool

    C.stgi = 0
    C.junk = kb.sb([128, 1024], BF16, "junk")
    C.ss = kb.sb([128, 1], F32); C.rt = kb.sb([128, 1], F32); C.rstd = kb.sb([128, 1], F32)
    C.xn = kb.sb([128, D], BF16, "xn")
    C.epsn = kb.sb([128, 1], F32)
    C.ident = kb.sb([128, 128], BF16, "ident")
    kb.op(dve, lambda e: e.memset(C.epsn[:, :], 1e-6), w=[C.epsn])
    C.kmaxb = kb.sb([128, 1], F32)

    def ld_const(dst, src_ap, shape, dt):
        if dt == F32:
            kb.dma(sp, dst[:, :], src_ap, w=[dst])
        else:
            kb.dma(pool, dst[:, :], src_ap, w=[dst])

    g1c = kb.sb([128, 8], F32); g2c = kb.sb([128, 8], F32); par = kb.sb([128, 2], F32)
    xt = [kb.sb([128, D], F32, "xt") for _ in range(2)]
    uT = kb.sb([128, KC, 128], BF16, "uT")
    STG = lambda: [kb.sb([128, 1024], F32, "stg") for _ in range(2)]
    with ExitStack() as t0:
        es_save = kb.es
        kb.es = t0
        ld_const(C.ident, identd[:, :], [128, 128], BF16)
        ld_const(g1c, g1col[:, :], [128, 8], F32)
        ld_const(g2c, g2col[:, :], [128, 8], F32)
        ld_const(par, pard[:, :], [128, 2], F32)
        kb.barrier()
        kb.es = es_save

    with ExitStack() as p1:
        es_save = kb.es
        kb.es = p1
        C.ropeA = kb.sb([128, 512], F32); C.ropeB = kb.sb([128, 512], F32)
        Wrw = kb.sb([128, KC, 1792], BF16, "Wrw")
        Wkv = kb.sb([128, KC, 1088], BF16, "Wkv")
        load_cast(kb, C, Wrw, lambda c0, n: Wrw[:, :, c0:c0 + n],
                  lambda c0, n: w_rw.rearrange("(k p) n -> p k n", p=128)[:, :, c0:c0 + n], 1792, None, step=128)
        load_cast(kb, C, Wkv, lambda c0, n: Wkv[:, :, c0:c0 + n],
                  lambda c0, n: w_kv.rearrange("(k p) n -> p k n", p=128)[:, :, c0:c0 + n], 1088, None, step=128)
        muc = kb.sb([128, 14], F32); ld_const(muc, mucol[:, :], [128, 14], F32)
        wdus = kb.sb([64, 512], BF16); ld_const(wdus, wdu[:, :], [64, 512], BF16)
        aups = kb.sb([128, 512], BF16)
        kb.dma(pool, aups[64:128, :], aup[:, :], w=[aups])
        gups = kb.sb([128, 512], BF16); ld_const(gups, gup[:, :], [128, 512], BF16)
        w0b = kb.sb([128, 512], F32); kb.dma(sp, w0b[:, :], w0row.partition_broadcast(128), w=[w0b])
        lgb = kb.sb([128, 512], F32); kb.dma(sp, lgb[:, :], lnxg.partition_broadcast(128), w=[lgb])
        lbb = kb.sb([128, 512], F32); kb.dma(sp, lbb[:, :], lnxb.partition_broadcast(128), w=[lbb])
        a0c = kb.sb([128, 4], F32); ld_const(a0c, a0col[:, :], [128, 4], F32)
        kkc = kb.sb([128, 4], F32); ld_const(kkc, kkcol[:, :], [128, 4], F32)
        kac = kb.sb([128, 4], F32); ld_const(kac, kacol[:, :], [128, 4], F32)
        rkc = kb.sb([128, 4], F32); ld_const(rkc, rkcol[:, :], [128, 4], F32)
        LL = kb.sb([128, 256], F32); ld_const(LL, LLd[:, :], [128, 256], F32)
        maskU = kb.sb([128, 256], F32); ld_const(maskU, maskUd[:, :], [128, 256], F32)
        maskL = kb.sb([128, 128], F32); ld_const(maskL, maskLd[:, :], [128, 128], F32)
        bones = kb.sb([128, 128], F32); ld_const(bones, bonesd[:, :], [128, 128], F32)
        bones2 = kb.sb([128, 2], BF16); ld_const(bones2, bones2d[:, :], [128, 2], BF16)

        pbuf = kb.sb([128, 14, 257], F32, "pbuf")
        kb.op(pool, lambda e: e.memset(pbuf[:, :, :], 0.0), w=[pbuf])
        psh = kb.sb([128, 14, 256], F32, "psh")
        uT2 = kb.sb([128, KC, 256], BF16, "uT2")
        x4 = xt + [kb.sb([128, D], F32, "x4") for _ in range(2)]
        wa_bf = kb.sb([128, 128], BF16); sg_bf = kb.sb([128, 128], BF16)
        zT = kb.sb([128, 512], F32); sigT = kb.sb([128, 512], F32)
        Epe = kb.sb([128, 4, 2, 128], F32); Eneg = kb.sb([128, 4, 128], F32)
        a_sb = kb.sb([128, 4, 128], F32); g_sb = kb.sb([128, 512], F32)
        kk = kb.sb([128, 4, 128], F32); kk2 = kb.sb([128, 4, 128], F32); sq = kb.sb([128, 512], F32)
        kap = kb.sb([128, 4, 128], F32); tmpa = kb.sb([128, 4, 128], F32); kmod = kb.sb([128, 4, 128], F32)
        bb = kb.sb([128, 4, 128], F32); rkr0 = kb.sb([128, 4, 128], F32)
        krt = kb.sb([128, 4, 2, 128], BF16); ktl = kb.sb([128, 4, 128], BF16); btl = kb.sb([128, 4, 128], BF16)
        rkr = kb.sb([128, 4, 128], BF16); v_bf = kb.sb([128, 4, 128], BF16)
        Vtm = kb.sb([128, 512], BF16); Ktm = kb.sb([128, 512], BF16); Btm = kb.sb([128, 512], BF16)
        s_sb = kb.sb([128, 8], F32)
        Tst = kb.sb([128, 4, 64], F32, "Tst"); Tbf = kb.sb([128, 4, 64], BF16, "Tbf")
        kb.op(pool, lambda e: e.memset(Tst[:, :, :], 0.0), w=[Tst])
        kb.op(pool, lambda e: e.memset(Tbf[:, :, :], 0.0), w=[Tbf])
        Ttmp = kb.sb([128, 4, 64], F32)
        Hh = []
        for h in range(8):
            o = Ctx()
            o.Akr = kb.sb([128, 256], BF16); o.Arb = kb.sb([128, 128], BF16)
            o.T3 = [kb.sb([128, 384], BF16) for _ in range(2)]
            o.X = kb.sb([128, 64], BF16)
            Hh.append(o)
        Uneg = kb.sb([128, 512], BF16)
        Yall = kb.sb([128, 512], F32); Ysq = kb.sb([128, 512], F32)
        st8 = [kb.sb([128, 8], F32) for _ in range(6)]
        Yn = kb.sb([128, 512], F32); tmpY = kb.sb([128, 512], F32)
        ya = [kb.sb([128, 512], F32) for _ in range(2)]
        ya_sel = kb.sb([128, 512], BF16)
        cs = [kb.sb([128, 2, 32], F32) for _ in range(2)]
        kr_bf = kb.sb([128, 512], BF16); krT = kb.sb([128, 4, 128], BF16); vb = kb.sb([128, 8, 65], BF16)
        kb.op(pool, lambda e: e.memset(vb[:, :, 64:65], 1.0), w=[vb])
        ki_bf = kb.sb([128, 64], BF16); kiT = kb.sb([64, 128], BF16)

        identf = kb.sb([128, 128], F32); ld_const(identf, identd[:, :], [128, 128], F32)
        ones_r = kb.sb([1, 128], F32)
        kb.op(pool, lambda e: e.memset(ones_r[:, :], 1.0), w=[ones_r])
        kmx = kb.sb([128, 1], F32); k8 = kb.sb([128, 8], F32); k1 = kb.sb([128, 1], F32)
        kb.op(pool, lambda e: e.memset(kmx[:, :], 0.0), w=[kmx])
        for i in range(NT):
            cst = cs[i % 2]
            tsl = slice((i % 2) * 128, (i % 2) * 128 + 128)
            kb.dma(sp, cst[:, :, :], cs_full[i * 128:(i + 1) * 128, :].rearrange("p (a d) -> p a d", d=32), w=[cst])
            if i % 2 == 0:
                for t in range(2):
                    x_t = x4[((i // 2) % 2) * 2 + t]
                    kb.dma(sp, x_t[:, :], xfull[(i + t) * 128:(i + t + 1) * 128, :], w=[x_t])
                for t in range(2):
                    x_t = x4[((i // 2) % 2) * 2 + t]
                    rms_uT(kb, C, x_t, g1c, uT2, dst=uT2[:, :, t * 128:(t + 1) * 128])
                for g0 in range(0, 14, 2):
                    ps = kb.psum()
                    for m in range(g0, g0 + 2):
                        for kc in range(KC):
                            kb.mm(ps, ps[:, (m - g0) * 256:(m - g0 + 1) * 256], Wrw[:, kc, m * 128:(m + 1) * 128], uT2[:, kc, :],
                                  kc == 0, kc == KC - 1, r=[Wrw, uT2])
                    kb.acopy(pbuf[:, g0:g0 + 2, 1:257], ps[:, :].rearrange("p (m t) -> p m t", t=256), r=[ps], w=[pbuf])
                kb.tt(dve, psh[:, :, :], pbuf[:, :, 0:256], pbuf[:, :, 1:257], ALU.subtract, r=[pbuf], w=[psh])
                kb.tt(dve, psh[:, :, :], psh[:, :, :], bc(muc[:, :].unsqueeze(2), [128, 14, 256]), ALU.mult, r=[psh, muc], w=[psh])
                kb.tt(dve, psh[:, :, :], psh[:, :, :], pbuf[:, :, 1:257], ALU.add, r=[psh, pbuf], w=[psh])
                kb.op(pool, lambda e: e.tensor_copy(out=pbuf[:, :, 0:1], in_=pbuf[:, :, 256:257]), r=[pbuf], w=[pbuf])
            r_ = psh[:, 0:4, tsl]; k_ = psh[:, 4:8, tsl]; v_ = psh[:, 8:12, tsl]
            kb.acopy(wa_bf[0:64, :], psh[0:64, 12, tsl], r=[psh], w=[wa_bf], func=AF.Tanh)
            kb.acopy(wa_bf[64:128, :], psh[64:128, 12, tsl], r=[psh], w=[wa_bf])
            kb.acopy(sg_bf[:, :], psh[:, 13, tsl], r=[psh], w=[sg_bf], func=AF.Sigmoid)
            psk = kb.psum(); psv_ = kb.psum(); psi = kb.psum()
            for (pst, c0, n) in ((psk, 0, 512), (psv_, 512, 512), (psi, 1024, 64)):
                for kc in range(KC):
                    kb.mm(pst, pst[:, 0:n], uT2[:, kc, tsl], Wkv[:, kc, c0:c0 + n], kc == 0, kc == KC - 1, r=[uT2, Wkv])
            kb.acopy(vb[:, :, 0:64], psv_[:, :].rearrange("p (h d) -> p h d", d=64), r=[psv_], w=[vb])
            kb.dma(pool, Vs[:, :, i, :].rearrange("m p f -> p m f"), vb[:, :, :].rearrange("p (m a) d -> p m (a d)", a=2), r=[vb])
            kb.acopy(tmpY[:, :], psk[:, :], r=[psk], w=[tmpY], func=AF.Square)
            kb.op(dve, lambda e: e.tensor_reduce(out=k8[:, :], in_=tmpY[:, :].rearrange("p (h d) -> p h d", d=64), axis=AX.X, op=ALU.add), r=[tmpY], w=[k8])
            kb.op(dve, lambda e: e.tensor_reduce(out=k1[:, :], in_=k8[:, :], axis=AX.X, op=ALU.max), r=[k8], w=[k1])
            kb.tt(dve, kmx[:, :], kmx[:, :], k1[:, :], ALU.max, r=[kmx, k1], w=[kmx])
            rope_tm(kb, C, psk, 8, cst, kr_bf[:, :], kr_bf)
            ps = kb.psum(); psb_ = ps[:, :].bitcast(BF16)
            for m in range(4):
                kb.tr(ps, psb_[:, m * 128:(m + 1) * 128], kr_bf[:, m * 128:(m + 1) * 128], C.ident[:, :], r=[kr_bf, C.ident])
            kb.acopy(krT[:, :, :], psb_[:, 0:512].rearrange("p (m t) -> p m t", t=128), r=[ps], w=[krT])
            kb.dma(pool, KTs[:, :, i * 128:(i + 1) * 128].rearrange("m p s -> p m s"), krT[:, :, :], r=[krT])
            rope_tm(kb, C, psi, 1, cst, ki_bf[:, :], ki_bf)
            ps = kb.psum(); psb_ = ps[:, :].bitcast(BF16)
            kb.tr(ps, psb_[0:64, 0:128], ki_bf[:, 0:64], C.ident[:, :], r=[ki_bf, C.ident])
            kb.acopy(kiT[:, :], psb_[0:64, 0:128], r=[ps], w=[kiT])
            kb.dma(pool, KIs[:, i * 128:(i + 1) * 128], kiT[:, :], r=[kiT])
            ps = kb.psum()
            for m in range(4):
                kb.mm(ps, ps[:, m * 128:(m + 1) * 128], aups[64:128, m * 128:(m + 1) * 128], wa_bf[64:128, :], True, True, r=[aups, wa_bf])
            for m in range(4):
                kb.acopy(a_sb[:, m, :], ps[:, m * 128:(m + 1) * 128], r=[ps, a0c], w=[a_sb], func=AF.Sigmoid, bias=a0c[:, m:m + 1])
            ps = kb.psum()
            kb.mm(ps, ps[:, :], wa_bf[0:64, :], wdus[:, :], True, True, r=[wa_bf, wdus])
            kb.tt(dve, zT[:, :], ps[:, :], w0b[:, :], ALU.add, r=[ps, w0b], w=[zT])
            kb.acopy(sigT[:, :], zT[:, :], r=[zT], w=[sigT], func=AF.Sigmoid)
            for half in range(2):
                ps = kb.psum()
                for mm_ in range(2):
                    m = half * 2 + mm_
                    kb.mm(ps, ps[:, mm_ * 256:(mm_ + 1) * 256], sigT[:, m * 128:(m + 1) * 128], LL[:, :], True, True, r=[sigT, LL])
                pv = ps[:, :].rearrange("p (m a t) -> p m a t", a=2, t=128)
                kb.acopy(Epe[:, half * 2:half * 2 + 2, :, :], pv, r=[ps], w=[Epe], func=AF.Exp)
                kb.acopy(Eneg[:, half * 2:half * 2 + 2, :], pv[:, :, 0, :], r=[ps], w=[Eneg], func=AF.Exp, scale=-1.0)
            ps = kb.psum()
            kb.mm(ps, ps[:, :], sg_bf[:, :], gups[:, :], True, True, r=[sg_bf, gups])
            kb.acopy(g_sb[:, :], ps[:, :], r=[ps], w=[g_sb])
            kb.tt(dve, kk[:, :, :], k_, bc(kkc[:, :].unsqueeze(2), [128, 4, 128]), ALU.mult, r=[psh, kkc], w=[kk])
            kb.tt(dve, kk2[:, :, :], kk[:, :, :], kk[:, :, :], ALU.mult, r=[kk], w=[kk2])
            ps = kb.psum()
            for m in range(4):
                kb.mm(ps, ps[:, m * 128:(m + 1) * 128], bones[:, :], kk2[:, m, :], True, True, r=[bones, kk2])
            kb.acopy(sq[:, :], ps[:, :], r=[ps], w=[sq], func=AF.Sqrt)
            kb.ts(dve, sq[:, :], sq[:, :], 1e-12, ALU.max, r=[sq], w=[sq])
            kb.op(dve, lambda e: e.reciprocal(out=sq[:, :], in_=sq[:, :]), r=[sq], w=[sq])
            kb.tt(dve, kap[:, :, :], kk[:, :, :], sq[:, :].rearrange("p (m t) -> p m t", t=128), ALU.mult, r=[kk, sq], w=[kap])
            kb.stt(dve, tmpa[:, :, :], a_sb[:, :, :], -1.0, bc(kac[:, :].unsqueeze(2), [128, 4, 128]), ALU.add, ALU.mult, r=[a_sb, kac], w=[tmpa])
            kb.stt(dve, kmod[:, :, :], tmpa[:, :, :], 1.0, k_, ALU.add, ALU.mult, r=[tmpa, psh], w=[kmod])
            kb.tt(dve, bb[:, :, :], kap[:, :, :], a_sb[:, :, :], ALU.mult, r=[kap, a_sb], w=[bb])
            kb.tt(dve, krt[:, :, 1, :], r_, Epe[:, :, 0, :], ALU.mult, r=[psh, Epe], w=[krt])
            kb.tt(dve, krt[:, :, 0, :], kap[:, :, :], Epe[:, :, 1, :], ALU.mult, r=[kap, Epe], w=[krt])
            kb.tt(dve, ktl[:, :, :], kmod[:, :, :], Eneg[:, :, :], ALU.mult, r=[kmod, Eneg], w=[ktl])
            kb.tt(dve, btl[:, :, :], bb[:, :, :], Eneg[:, :, :], ALU.mult, r=[bb, Eneg], w=[btl])
            kb.acopy(v_bf[:, :, :], v_, r=[psh], w=[v_bf])
            for src, dst in ((v_bf, Vtm), (ktl, Ktm), (btl, Btm)):
                ps = kb.psum()
                psv = ps[:, :].bitcast(BF16)
                for m in range(4):
                    kb.tr(ps, psv[:, m * 128:(m + 1) * 128], src[:, m, :], C.ident[:, :], r=[src, C.ident])
                kb.acopy(dst[:, :], psv[:, 0:512], r=[ps], w=[dst])
            hp = lambda h: (h // 2, slice((h % 2) * 64, (h % 2) * 64 + 64))
            for h in range(8):
                m, P = hp(h); o = Hh[h]
                psA = kb.psum()
                kb.mm(psA, psA[:, 0:256], ktl[P, m, :], krt[P, m, :, :].rearrange("p a t -> p (a t)"), True, True, r=[ktl, krt])
                kb.tt(dve, o.Akr[:, :], psA[:, 0:256], maskU[:, :], ALU.mult, r=[psA, maskU], w=[o.Akr])
                psB = kb.psum()
                kb.mm(psB, psB[:, 0:256], btl[P, m, :], krt[P, m, :, :].rearrange("p a t -> p (a t)"), True, True, r=[btl, krt])
                kb.mm(psB, psB[:, 256:384], krt[P, m, 0, :], btl[P, m, :], True, True, r=[btl, krt])
                kb.stt(dve, o.T3[0][:, 128:256], psB[:, 0:128], -1.0, maskU[:, 0:128], ALU.mult, ALU.mult, r=[psB, maskU], w=[o.T3[0]])
                kb.stt(dve, o.T3[0][:, 256:384], psB[:, 256:384], -1.0, maskL[:, :], ALU.mult, ALU.mult, r=[psB, maskL], w=[o.T3[0]])
                kb.tt(dve, o.Arb[:, :], psB[:, 128:256], maskU[:, 128:256], ALU.mult, r=[psB, maskU], w=[o.Arb])
            for lev in range(7):
                for h in range(8):
                    o = Hh[h]
                    Tp = o.T3[lev % 2]; Tn = o.T3[(lev + 1) % 2]
                    ps = kb.psum()
                    if lev == 0:
                        kb.mm(ps, ps[:, 128:256], Tp[:, 256:384], Tp[:, 128:256], True, True, r=[Tp])
                        kb.mm(ps, ps[:, 256:384], Tp[:, 128:256], Tp[:, 256:384], True, True, r=[Tp])
                        kb.acopy(Tn[:, 128:384], ps[:, 128:384], r=[ps], w=[Tn])
                        kb.tt(dve, Tn[:, 0:128], Tp[:, 128:256], C.ident[:, :], ALU.add, r=[Tp, C.ident], w=[Tn])
                        continue
                    if lev < 6:
                        kb.mm(ps, ps[:, 0:256], Tp[:, 256:384], Tp[:, 0:256], True, True, r=[Tp])
                        kb.mm(ps, ps[:, 256:384], Tp[:, 128:256], Tp[:, 256:384], True, True, r=[Tp])
                        kb.acopy(Tn[:, 128:384], ps[:, 128:384], r=[ps], w=[Tn])
                    else:
                        kb.mm(ps, ps[:, 0:128], Tp[:, 256:384], Tp[:, 0:128], True, True, r=[Tp])
                    kb.tt(dve, Tn[:, 0:128], ps[:, 0:128], Tp[:, 0:128], ALU.add, r=[ps, Tp], w=[Tn])
            for h in range(8):
                m, P = hp(h); o = Hh[h]
                hs = slice(h * 64, (h + 1) * 64)
                ps = kb.psum()
                kb.mm(ps, ps[:, 0:64], o.Akr[:, 0:128], Vtm[:, hs], True, False, r=[o.Akr, Vtm])
                kb.mm(ps, ps[:, 0:64], krt[P, m, 0, :], Tbf[P, m, :], False, True, r=[krt, Tbf])
                kb.acopy(o.X[:, :], ps[:, 0:64], r=[ps], w=[o.X])
            for h in range(8):
                m, P = hp(h); o = Hh[h]
                Wf = o.T3[1]
                hs = slice(h * 64, (h + 1) * 64)
                ps = kb.psum()
                kb.mm(ps, ps[:, 0:64], Wf[:, 0:128], o.X[:, :], True, True, r=[Wf, o.X])
                kb.acopy(Uneg[:, hs], ps[:, 0:64], r=[ps], w=[Uneg], scale=-1.0)
            for h in range(8):
                m, P = hp(h); o = Hh[h]
                hs = slice(h * 64, (h + 1) * 64)
                ps = kb.psum()
                kb.mm(ps, ps[:, 0:64], o.Akr[:, 128:256], Vtm[:, hs], True, False, r=[o.Akr, Vtm])
                kb.mm(ps, ps[:, 0:64], krt[P, m, 1, :], Tbf[P, m, :], False, False, r=[krt, Tbf])
                kb.mm(ps, ps[:, 0:64], o.Arb[:, :], Uneg[:, hs], False, True, r=[o.Arb, Uneg])
                kb.acopy(Yall[:, hs], ps[:, 0:64], r=[ps], w=[Yall])
            for m in range(4):
                ms = slice(m * 128, (m + 1) * 128)
                ps = kb.psum()
                kb.mm(ps, ps[:, 0:128], Ktm[:, ms], Vtm[:, ms], True, False, r=[Ktm, Vtm])
                kb.mm(ps, ps[:, 0:128], Btm[:, ms], Uneg[:, ms], False, True, r=[Btm, Uneg])
                for hh in range(2):
                    P = slice(hh * 64, hh * 64 + 64)
                    kb.tt(dve, Ttmp[P, m, :], ps[P, hh * 64:hh * 64 + 64], Tst[P, m, :], ALU.add, r=[ps, Tst], w=[Ttmp])
                kb.ts(dve, Tst[:, m, :], Ttmp[:, m, :], Epe[:, m, 0, 127:128], ALU.mult, r=[Ttmp, Epe], w=[Tst])
                kb.op(pool, lambda e, m=m: e.tensor_copy(out=Tbf[:, m, :], in_=Tst[:, m, :]), r=[Tst], w=[Tbf])
            kb.tt(pool, rkr0[:, :, :], r_, kmod[:, :, :], ALU.mult, r=[psh, kmod], w=[rkr0])
            kb.tt(pool, rkr[:, :, :], rkr0[:, :, :], bc(rkc[:, :].unsqueeze(2), [128, 4, 128]), ALU.mult, r=[rkr0, rkc], w=[rkr])
            ps = kb.psum()
            for m in range(4):
                kb.mm(ps, ps[:, 2 * m:2 * m + 2], rkr[:, m, :], bones2[:, :], True, True, r=[rkr, bones2])
            kb.acopy(s_sb[:, :], ps[:, 0:8], r=[ps], w=[s_sb])
            Y3 = Yall[:, :].rearrange("p (h d) -> p h d", d=64)
            sm, sqs, mean, var, rstd8, msq = st8
            kb.op(dve, lambda e: e.tensor_reduce(out=sm[:, :], in_=Y3, axis=AX.X, op=ALU.add), r=[Yall], w=[sm])
            kb.tt(pool, Ysq[:, :], Yall[:, :], Yall[:, :], ALU.mult, r=[Yall], w=[Ysq])
            kb.op(dve, lambda e: e.tensor_reduce(out=sqs[:, :], in_=Ysq[:, :].rearrange("p (h d) -> p h d", d=64), axis=AX.X, op=ALU.add), r=[Ysq], w=[sqs])
            kb.ts(dve, mean[:, :], sm[:, :], 1.0 / 64, ALU.mult, r=[sm], w=[mean])
            kb.tt(dve, msq[:, :], mean[:, :], mean[:, :], ALU.mult, r=[mean], w=[msq])
            kb.stt(dve, var[:, :], sqs[:, :], 1.0 / 64, msq[:, :], ALU.mult, ALU.subtract, r=[sqs, msq], w=[var])
            kb.ts(dve, var[:, :], var[:, :], 64e-5, ALU.add, r=[var], w=[var])
            kb.acopy(rstd8[:, :], var[:, :], r=[var], w=[rstd8], func=AF.Sqrt)
            kb.op(dve, lambda e: e.reciprocal(out=rstd8[:, :], in_=rstd8[:, :]), r=[rstd8], w=[rstd8])
            Yn3 = Yn[:, :].rearrange("p (h d) -> p h d", d=64)
            kb.tt(dve, Yn3, Y3, bc(mean[:, :].unsqueeze(2), [128, 8, 64]), ALU.subtract, r=[Yall, mean], w=[Yn])
            kb.tt(dve, Yn3, Yn3, bc(rstd8[:, :].unsqueeze(2), [128, 8, 64]), ALU.mult, r=[Yn, rstd8], w=[Yn])
            kb.tt(pool, Yn[:, :], Yn[:, :], lgb[:, :], ALU.mult, r=[Yn, lgb], w=[Yn])
            kb.tt(pool, Yn[:, :], Yn[:, :], lbb[:, :], ALU.add, r=[Yn, lbb], w=[Yn])
            kb.tt(dve, tmpY[:, :].rearrange("p (h d) -> p h d", d=64), Vtm[:, :].rearrange("p (h d) -> p h d", d=64),
                  bc(s_sb[:, :].unsqueeze(2), [128, 8, 64]), ALU.mult, r=[Vtm, s_sb], w=[tmpY])
            kb.tt(dve, Yn[:, :], Yn[:, :], tmpY[:, :], ALU.add, r=[Yn, tmpY], w=[Yn])
            yat = ya[i % 2]
            kb.tt(dve, yat[:, :], Yn[:, :], g_sb[:, :], ALU.mult, r=[Yn, g_sb], w=[yat])
            if debug:
                kb.dma(pool, YAfull[i * 128:(i + 1) * 128, :], yat[:, :], r=[yat])
            if i % 2 == 1:
                kb.ts(dve, tmpY[:, :], ya[0][:, :], par[:, 1:2], ALU.mult, r=[ya[0], par], w=[tmpY])
                kb.stt(dve, ya_sel[:, :], ya[1][:, :], par[:, 0:1], tmpY[:, :], ALU.mult, ALU.add, r=[ya[1], par, tmpY], w=[ya_sel])
                kb.dma(pool, YAs[(i // 2) * 128:(i // 2 + 1) * 128, :], ya_sel[:, :], r=[ya_sel])
        ps = kb.psum()
        kb.tr(ps, ps[0:1, 0:128], kmx[:, 0:1], identf[:, :], r=[kmx, identf])
        kb.op(dve, lambda e: e.tensor_reduce(out=k1[0:1, :], in_=ps[0:1, 0:128], axis=AX.X, op=ALU.max), r=[ps], w=[k1])
        ps2 = kb.psum()
        kb.mm(ps2, ps2[:, 0:1], ones_r[:, :], k1[0:1, :], True, True, r=[ones_r, k1])
        kb.op(dve, lambda e: e.tensor_copy(out=C.kmaxb[:, :], in_=ps2[:, 0:1]), r=[ps2], w=[C.kmaxb])
        kb.barrier()
        kb.es = es_save

    with ExitStack() as p2:
        es_save = kb.es
        kb.es = p2
        kb.nrot = 6
        C.ropeA = kb.sb([128, 512], F32); C.ropeB = kb.sb([128, 512], F32)
        Wqi = kb.sb([128, KC, 1032], BF16, "Wqi")
        maskadd = kb.sb([128, 256], F32); pow2 = kb.sb([128, NBISECT], F32)
        with ExitStack() as t2:
            kb.es = t2
            load_cast(kb, C, Wqi, lambda c0, n: Wqi[:, :, c0:c0 + n],
                      lambda c0, n: w_qi.rearrange("(k p) n -> p k n", p=128)[:, :, c0:c0 + n], 1032, None, step=128)
            ld_const(maskadd, maskaddd[:, :], [128, 256], F32)
            ld_const(pow2, pow2d[:, :], [128, NBISECT], F32)
            kb.barrier()
            kb.es = p2
        score = kb.sb([128, S], F32, "score")
        maskT = [kb.sb([128, NT, 128], BF16, "maskT")] * 2
        ki_sb = kb.sb([128, S], BF16, "ki_sb")
        m01 = kb.sb([128, S], BF16, "m01")
        pTu = [kb.sb([128, 512], BF16, "pTu") for _ in range(4)]
        pTm = [kb.sb([128, 512], BF16, "pTm") for _ in range(4)]
        kt_sb = [kb.sb([128, S], BF16, "kt_sb") for _ in range(2)]
        v_sb = [kb.sb([128, NT, 2, 65], BF16, "v_sb") for _ in range(2)]
        cso = [kb.sb([128, 2, 32], F32) for _ in range(2)]
        q_bf = kb.sb([128, 512], BF16); qi_bf = kb.sb([128, 512], BF16)
        qT = [kb.sb([128, 4, 2, 128], BF16) for _ in range(2)]; qiT = kb.sb([128, 4, 128], BF16)
        for qt_ in qT:
            kb.op(pool, lambda e, qt_=qt_: e.memset(qt_[:, :, :, :], 0.0), w=[qt_])
        w_sb = kb.sb([128, 8], F32)
        rl = [kb.sb([128, 512], F32) for _ in range(3)]
        Bt = kb.sb([128, 1], F32); lo = kb.sb([128, 1], F32); mid = kb.sb([128, 1], F32); cnt = kb.sb([128, 1], F32)
        stp = kb.sb([128, 1], F32); Wtab = kb.sb([128, NBISECT], F32); W2tab = kb.sb([128, NBISECT], F32); w0t = kb.sb([128, 1], F32)
        q8 = kb.sb([128, 8], F32); q1 = kb.sb([128, 1], F32); mq = kb.sb([128, 1], F32); qg = kb.sb([1, 1], F32)
        negm = [kb.sb([128, 1], F32) for _ in range(2)]
        identf = kb.sb([128, 128], F32); kb.dma(sp, identf[:, :], identd[:, :], w=[identf])
        ones_r = kb.sb([1, 128], F32)
        kb.op(pool, lambda e: e.memset(ones_r[:, :], 1.0), w=[ones_r])
        rinv8 = kb.sb([128, 8], F32)
        ybu = kb.sb([128, 8, 65], F32)
        yb = kb.sb([128, 512], BF16)
        cn = Ctx(); cn.kt = 0; cn.rl = 0; cn.pb = 0; cn.pt = 0; cn.h = 0

        def front(j):
            par = j % 2
            Lk = 256 * (j + 1)
            nblk = Lk // 128
            x_t = xt[j % 2]; cst = cso[par]
            kb.dma(sp, x_t[:, :], xown[j * 128:(j + 1) * 128, :], w=[x_t])
            kb.dma(sp, cst[:, :, :], cs_own[j * 128:(j + 1) * 128, :].rearrange("p (a d) -> p a d", d=32), w=[cst])
            kb.dma(sp, ki_sb[0:64, 0:Lk], KIs[:, 0:Lk], w=[ki_sb])
            kb.dma(sp, ki_sb[64:128, 0:Lk], KIs[:, 0:Lk], w=[ki_sb])
            rms_uT(kb, C, x_t, g1c, uT)
            psq = kb.psum(); psqi = kb.psum(); psw = kb.psum()
            for (pst, c0, n) in ((psq, 0, 512), (psqi, 512, 512), (psw, 1024, 8)):
                for kc in range(KC):
                    kb.mm(pst, pst[:, 0:n], uT[:, kc, :], Wqi[:, kc, c0:c0 + n], kc == 0, kc == KC - 1, r=[uT, Wqi])
            kb.acopy(w_sb[:, :], psw[:, 0:8], r=[psw], w=[w_sb])
            kb.acopy(C.ropeA[:, :], psq[:, :], r=[psq], w=[C.ropeA], func=AF.Square)
            kb.op(dve, lambda e: e.tensor_reduce(out=q8[:, :], in_=C.ropeA[:, :].rearrange("p (h d) -> p h d", d=64), axis=AX.X, op=ALU.add), r=[C.ropeA], w=[q8])
            kb.op(dve, lambda e: e.tensor_reduce(out=q1[:, :], in_=q8[:, :], axis=AX.X, op=ALU.max), r=[q8], w=[q1])
            pst_ = kb.psum()
            kb.tr(pst_, pst_[0:1, 0:128], q1[:, 0:1], identf[:, :], r=[q1, identf])
            kb.op(dve, lambda e, pst_=pst_: e.tensor_reduce(out=qg[:, :], in_=pst_[0:1, 0:128], axis=AX.X, op=ALU.max), r=[pst_], w=[qg])
            psg_ = kb.psum()
            kb.mm(psg_, psg_[:, 0:1], ones_r[:, :], qg[:, :], True, True, r=[ones_r, qg])
            kb.tt(dve, q1[:, :], psg_[:, 0:1], C.kmaxb[:, :], ALU.mult, r=[psg_, C.kmaxb], w=[q1])
            kb.acopy(mq[:, :], q1[:, :], r=[q1], w=[mq], func=AF.Sqrt, scale=1.0 / 64)
            kb.ts(dve, negm[par][:, :], mq[:, :], -1.0, ALU.mult, r=[mq], w=[negm[par]])
            rope_tm(kb, C, psq, 8, cst, q_bf[:, :], q_bf)
            rope_tm(kb, C, psqi, 8, cst, qi_bf[:, :], qi_bf)
            for src, dst in ((q_bf, qT[par]), (qi_bf, qiT)):
                ps = kb.psum(); psb_ = ps[:, :].bitcast(BF16)
                for m in range(4):
                    kb.tr(ps, psb_[:, m * 128:(m + 1) * 128], src[:, m * 128:(m + 1) * 128], C.ident[:, :], r=[src, C.ident])
                if dst is qiT:
                    kb.acopy(dst[:, :, :], psb_[:, 0:512].rearrange("p (m t) -> p m t", t=128), r=[ps], w=[dst])
                else:
                    kb.acopy(dst[0:64, :, 0, :], psb_[0:64, 0:512].rearrange("p (m t) -> p m t", t=128), r=[ps], w=[dst])
                    kb.acopy(dst[64:128, :, 1, :], psb_[64:128, 0:512].rearrange("p (m t) -> p m t", t=128), r=[ps], w=[dst])
            for ci, c0 in enumerate(range(0, Lk, 512)):
                n = min(512, Lk - c0)
                for h in range(8):
                    m, P = h // 2, slice((h % 2) * 64, (h % 2) * 64 + 64)
                    ps = kb.psum()
                    kb.mm(ps, ps[:, 0:n], qiT[P, m, :], ki_sb[P, c0:c0 + n], True, True, r=[qiT, ki_sb])
                    rlt = rl[cn.rl % 3]; cn.rl += 1
                    kb.acopy(rlt[:, 0:n], ps[:, 0:n], r=[ps], w=[rlt], func=AF.Relu)
                    if h == 0:
                        kb.ts(dve, score[:, c0:c0 + n], rlt[:, 0:n], w_sb[:, 0:1], ALU.mult, r=[rlt, w_sb], w=[score])
                    else:
                        kb.stt(dve, score[:, c0:c0 + n], rlt[:, 0:n], w_sb[:, h:h + 1], score[:, c0:c0 + n], ALU.mult, ALU.add,
                               r=[rlt, w_sb, score], w=[score])
            mb = m01
            kb.op(dve, lambda e, Lk=Lk: e.tensor_reduce(out=Bt[:, :], in_=score[:, 0:Lk], axis=AX.X, op=ALU.max, apply_absolute_value=True),
                  r=[score], w=[Bt])
            kb.tt(dve, score[:, Lk - 256:Lk], score[:, Lk - 256:Lk], maskadd[:, :], ALU.add, r=[score, maskadd], w=[score])
            kb.ts(dve, lo[:, :], Bt[:, :], -1.001, ALU.mult, r=[Bt], w=[lo], s2=-1e-30, op1=ALU.add)
            kb.ts(dve, w0t[:, :], Bt[:, :], 2.003, ALU.mult, r=[Bt], w=[w0t], s2=2e-30, op1=ALU.add)
            kb.ts(dve, Wtab[:, :], pow2[:, :], w0t[:, :], ALU.mult, r=[pow2, w0t], w=[Wtab])
            kb.ts(dve, W2tab[:, :], Wtab[:, :], 2.0, ALU.mult, r=[Wtab], w=[W2tab])
            kb.tt(dve, mid[:, :], lo[:, :], Wtab[:, 0:1], ALU.add, r=[lo, Wtab], w=[mid])
            for it in range(NBISECT):
                kb.ts(dve, mb[:, 0:Lk], score[:, 0:Lk], mid[:, :], ALU.is_ge, r=[score, mid], w=[mb, cnt],
                      s2=None, op1=ALU.add, accum=cnt[:, :])
                if it < NBISECT - 1:
                    kb.stt(dve, stp[:, :], cnt[:, :], 255.5, W2tab[:, it + 1:it + 2], ALU.is_ge, ALU.mult, r=[cnt, W2tab], w=[stp])
                    kb.stt(dve, mid[:, :], stp[:, :], Wtab[:, it + 1:it + 2], mid[:, :], ALU.subtract, ALU.add, r=[stp, Wtab, mid], w=[mid])
                else:
                    kb.stt(dve, stp[:, :], cnt[:, :], 255.5, Wtab[:, it:it + 1], ALU.is_ge, ALU.mult, r=[cnt, Wtab], w=[stp])
                    kb.stt(dve, lo[:, :], stp[:, :], Wtab[:, it:it + 1], mid[:, :], ALU.subtract, ALU.add, r=[stp, Wtab, mid], w=[lo])
            kb.ts(dve, mb[:, 0:Lk], score[:, 0:Lk], lo[:, :], ALU.is_ge, r=[score, lo], w=[mb])

        def front_b(j):
            par = j % 2
            Lk = 256 * (j + 1)
            nblk = Lk // 128
            mb = m01
            mT = maskT[par]
            for b0 in range(0, nblk, 8):
                nb = min(8, nblk - b0)
                ps = kb.psum(); psb_ = ps[:, :].bitcast(BF16)
                for bi in range(nb):
                    kb.tr(ps, psb_[:, bi * 128:(bi + 1) * 128], mb[:, (b0 + bi) * 128:(b0 + bi + 1) * 128], C.ident[:, :], r=[mb, C.ident])
                kb.acopy(mT[:, b0:b0 + nb, :], psb_[:, 0:nb * 128].rearrange("p (b t) -> p b t", t=128), r=[ps], w=[mT])

        def back(j):
            par = j % 2
            Lk = 256 * (j + 1)
            mT = maskT[par]
            nblk = Lk // 128
            for m in range(4):
                ktb = kt_sb[cn.kt % 2]; vbuf = v_sb[cn.kt % 2]; cn.kt += 1
                kb.dma(sp, ktb[:, 0:Lk], KTs[m, :, 0:Lk], w=[ktb])
                kb.dma(sp, vbuf[:, 0:nblk, :, :], Vs[m, :, 0:nblk, :].rearrange("p b (a d) -> p b a d", d=65), w=[vbuf])
                pso = [kb.psb[6], kb.psb[7]]
                pend = []

                def emit_pv(pm, blk0, nb, pso=pso, vbuf=vbuf, nblk=nblk):
                    for bi in range(nb):
                        blk = blk0 + bi
                        for hh in range(2):
                            kb.mm(pso[hh], pso[hh][:, 0:65], pm[:, bi * 256 + hh * 128:bi * 256 + (hh + 1) * 128], vbuf[:, blk, hh, :],
                                  blk == 0, blk == nblk - 1, r=[pm, vbuf])

                for blk0 in range(0, nblk, 2):
                    nb = 2
                    ps = kb.psum()
                    for bi in range(nb):
                        c0 = (blk0 + bi) * 128
                        kb.mm(ps, ps[:, bi * 256:(bi + 1) * 256], ktb[:, c0:c0 + 128], qT[par][:, m, :, :].rearrange("p a t -> p (a t)"), True, True,
                              r=[qT[par], ktb])
                    pu = pTu[cn.pb % 4]; pm = pTm[cn.pb % 4]; cn.pb += 1
                    kb.acopy(pu[:, :], ps[:, :], r=[ps, negm[par]], w=[pu], func=AF.Exp, scale=0.125, bias=negm[par][:, :])
                    kb.tt(pool, pm[:, :].rearrange("p (b a t) -> p b a t", a=2, t=128), pu[:, :].rearrange("p (b a t) -> p b a t", a=2, t=128),
                          bc(mT[:, blk0:blk0 + nb, :].unsqueeze(2), [128, nb, 2, 128]), ALU.mult, r=[pu, mT], w=[pm])
                    pend.append((pm, blk0, nb))
                    if len(pend) > 3:
                        emit_pv(*pend.pop(0))
                while pend:
                    emit_pv(*pend.pop(0))
                for hh in range(2):
                    kb.acopy(ybu[:, 2 * m + hh, :], pso[hh][:, 0:65], r=[pso[hh]], w=[ybu])
            kb.op(dve, lambda e: e.reciprocal(out=rinv8[:, :], in_=ybu[:, :, 64]), r=[ybu], w=[rinv8])
            kb.tt(dve, yb[:, :].rearrange("p (h d) -> p h d", d=64), ybu[:, :, 0:64],
                  bc(rinv8[:, :].unsqueeze(2), [128, 8, 64]), ALU.mult, r=[ybu, rinv8], w=[yb])
            kb.dma(pool, YBs[j * 128:(j + 1) * 128, :], yb[:, :], r=[yb])

        front(0)
        front_b(0)
        for j in range(NQ):
            if j + 1 < NQ:
                front(j + 1)
            back(j)
            if j + 1 < NQ:
                front_b(j + 1)
        kb.barrier()
        kb.nrot = 8
        kb.es = es_save

    NP = NQ // 2
    with ExitStack() as p3:
        es_save = kb.es
        kb.es = p3
        Wg = kb.sb([128, KC, 2048], BF16, "Wg")
        load_cast(kb, C, Wg, lambda c0, n: Wg[:, :, c0:c0 + n],
                  lambda c0, n: w_gate.rearrange("(k p) n -> p k n", p=128)[:, :, c0:c0 + n], 2048, None, step=128)
        Worw = kb.sb([128, 4, D], BF16, "Worw"); Woat = kb.sb([128, 4, D], BF16, "Woat"); Wout = kb.sb([128, KC, D], BF16, "Wout")
        load_cast(kb, C, Worw, lambda c0, n: Worw[:, :, c0:c0 + n],
                  lambda c0, n: w_orw.rearrange("(k p) n -> p k n", p=128)[:, :, c0:c0 + n], D, None, step=256)
        load_cast(kb, C, Woat, lambda c0, n: Woat[:, :, c0:c0 + n],
                  lambda c0, n: w_oatt.rearrange("(k p) n -> p k n", p=128)[:, :, c0:c0 + n], D, None, step=256)
        load_cast(kb, C, Wout, lambda c0, n: Wout[:, :, c0:c0 + n],
                  lambda c0, n: w_out.rearrange("(k p) n -> p k n", p=128)[:, :, c0:c0 + n], D, None, step=128)
        uT2 = kb.sb([128, KC, 256], BF16, "uT2")
        gT = kb.sb([128, 16, 256], F32, "gT")
        x4 = [kb.sb([128, D], F32, "x4") for _ in range(4)]
        yab = [kb.sb([128, 512], BF16) for _ in range(4)]; ybb = [kb.sb([128, 512], BF16) for _ in range(4)]
        yaT = kb.sb([128, 4, 256], BF16); ybT = kb.sb([128, 4, 256], BF16)
        t1 = kb.sb([128, 512], F32); t2 = kb.sb([128, 512], F32)
        mT = kb.sb([128, 8, 256], BF16)
        h1 = [kb.sb([128, D], F32) for _ in range(2)]
        for jp in range(NP):
            for t in range(2):
                j = 2 * jp + t
                sl_ = (jp % 2) * 2 + t
                kb.dma(sp, x4[sl_][:, :], xown[j * 128:(j + 1) * 128, :], w=[x4[sl_]])
                kb.dma(sp, yab[sl_][:, :], YAs[j * 128:(j + 1) * 128, :], w=[yab[sl_]])
                kb.dma(sp, ybb[sl_][:, :], YBs[j * 128:(j + 1) * 128, :], w=[ybb[sl_]])
            for t in range(2):
                sl_ = (jp % 2) * 2 + t
                rms_uT(kb, C, x4[sl_], g1c, uT2, dst=uT2[:, :, t * 128:(t + 1) * 128])
                for src, dst in ((yab[sl_], yaT), (ybb[sl_], ybT)):
                    ps = kb.psum(); psb_ = ps[:, :].bitcast(BF16)
                    for m in range(4):
                        kb.tr(ps, psb_[:, m * 128:(m + 1) * 128], src[:, m * 128:(m + 1) * 128], C.ident[:, :], r=[src, C.ident])
                    kb.acopy(dst[:, :, t * 128:(t + 1) * 128], psb_[:, 0:512].rearrange("p (m t) -> p m t", t=128), r=[ps], w=[dst])
            for g0 in range(0, 16, 2):
                ps = kb.psum()
                for n_ in range(2):
                    for kc in range(KC):
                        kb.mm(ps, ps[:, n_ * 256:(n_ + 1) * 256], Wg[:, kc, (g0 + n_) * 128:(g0 + n_ + 1) * 128], uT2[:, kc, :],
                              kc == 0, kc == KC - 1, r=[Wg, uT2])
                kb.acopy(gT[:, g0:g0 + 2, :], ps[:, :].rearrange("p (m t) -> p m t", t=256), r=[ps], w=[gT], func=AF.Sigmoid)
            for q4 in range(4):
                psa = kb.psum(); psb2 = kb.psum()
                for n_ in range(2):
                    nn = q4 * 2 + n_
                    for kc in range(4):
                        kb.mm(psa, psa[:, n_ * 256:(n_ + 1) * 256], Worw[:, kc, nn * 128:(nn + 1) * 128], yaT[:, kc, :], kc == 0, kc == 3, r=[Worw, yaT])
                    for kc in range(4):
                        kb.mm(psb2, psb2[:, n_ * 256:(n_ + 1) * 256], Woat[:, kc, nn * 128:(nn + 1) * 128], ybT[:, kc, :], kc == 0, kc == 3, r=[Woat, ybT])
                kb.tt(dve, t1[:, :], psa[:, :], gT[:, q4 * 2:q4 * 2 + 2, :].rearrange("p m t -> p (m t)"), ALU.mult, r=[psa, gT], w=[t1])
                kb.tt(dve, t2[:, :], psb2[:, :], gT[:, 8 + q4 * 2:8 + q4 * 2 + 2, :].rearrange("p m t -> p (m t)"), ALU.mult, r=[psb2, gT], w=[t2])
                kb.tt(dve, mT[:, q4 * 2:q4 * 2 + 2, :].rearrange("p m t -> p (m t)"), t1[:, :], t2[:, :], ALU.add, r=[t1, t2], w=[mT])
            for t in range(2):
                j = 2 * jp + t
                sl_ = (jp % 2) * 2 + t
                h1t = h1[t]
                for half in range(2):
                    ps = kb.psum()
                    for kc in range(KC):
                        kb.mm(ps, ps[:, :], mT[:, kc, t * 128:(t + 1) * 128], Wout[:, kc, half * 512:(half + 1) * 512], kc == 0, kc == KC - 1, r=[mT, Wout])
                    kb.tt(dve, h1t[:, half * 512:(half + 1) * 512], ps[:, :], x4[sl_][:, half * 512:(half + 1) * 512], ALU.add, r=[ps, x4[sl_]], w=[h1t])
                kb.dma(pool, H1s[j * 128:(j + 1) * 128, :], h1t[:, :], r=[h1t])
        kb.barrier()
        kb.es = es_save

    with ExitStack() as p4:
        es_save = kb.es
        kb.es = p4
        Wfi = kb.sb([128, KC, 5632], BF16, "Wfi")
        load_cast(kb, C, Wfi, lambda c0, n: Wfi[:, :, c0:c0 + n],
                  lambda c0, n: w_ffi.rearrange("(k p) n -> p k n", p=128)[:, :, c0:c0 + n], 5632, None, step=128)
        Wfo = kb.sb([128, 22, D], BF16, "Wfo")
        load_cast(kb, C, Wfo, lambda c0, n: Wfo[:, :, c0:c0 + n],
                  lambda c0, n: w_ffo.rearrange("(k p) n -> p k n", p=128)[:, :, c0:c0 + n], D, None, step=32)
        gfb = kb.sb([128, D], F32); kb.dma(sp, gfb[:, :], gfrow.partition_broadcast(128), w=[gfb])
        uT2 = kb.sb([128, KC, 256], BF16, "uT2")
        aT = kb.sb([128, 22, 256], BF16, "aT")
        sl = kb.sb([128, 512], F32)
        h1 = [kb.sb([128, D], F32) for _ in range(4)]
        h2 = kb.sb([128, D], F32); ot = kb.sb([128, D], F32)
        for jp in range(NP):
            for t in range(2):
                j = 2 * jp + t
                h1t = h1[(jp % 2) * 2 + t]
                kb.dma(sp, h1t[:, :], H1s[j * 128:(j + 1) * 128, :], w=[h1t])
            for t in range(2):
                h1t = h1[(jp % 2) * 2 + t]
                rms_uT(kb, C, h1t, g2c, uT2, dst=uT2[:, :, t * 128:(t + 1) * 128])
            for g0 in range(0, 22, 2):
                psg = kb.psum(); psu = kb.psum()
                for n_ in range(2):
                    nn = g0 + n_
                    for kc in range(KC):
                        kb.mm(psg, psg[:, n_ * 256:(n_ + 1) * 256], Wfi[:, kc, nn * 128:(nn + 1) * 128], uT2[:, kc, :], kc == 0, kc == KC - 1, r=[Wfi, uT2])
                    for kc in range(KC):
                        kb.mm(psu, psu[:, n_ * 256:(n_ + 1) * 256], Wfi[:, kc, 2816 + nn * 128:2816 + (nn + 1) * 128], uT2[:, kc, :], kc == 0, kc == KC - 1, r=[Wfi, uT2])
                kb.acopy(sl[:, :], psg[:, :], r=[psg], w=[sl], func=AF.Silu)
                kb.tt(dve, aT[:, g0:g0 + 2, :].rearrange("p m t -> p (m t)"), sl[:, :], psu[:, :], ALU.mult, r=[sl, psu], w=[aT])
            for t in range(2):
                j = 2 * jp + t
                h1t = h1[(jp % 2) * 2 + t]
                for half in range(2):
                    ps = kb.psum()
                    for n_ in range(22):
                        kb.mm(ps, ps[:, :], aT[:, n_, t * 128:(t + 1) * 128], Wfo[:, n_, half * 512:(half + 1) * 512], n_ == 0, n_ == 21, r=[aT, Wfo])
                    kb.tt(dve, h2[:, half * 512:(half + 1) * 512], ps[:, :], h1t[:, half * 512:(half + 1) * 512], ALU.add, r=[ps, h1t], w=[h2])
                kb.acopy(C.junk[:, 0:D], h2[:, :], r=[h2], w=[C.junk, C.ss], func=AF.Square, accum=C.ss[:, :])
                kb.acopy(C.rt[:, :], C.ss[:, :], r=[C.ss], w=[C.rt], func=AF.Sqrt, scale=1.0 / D, bias=C.epsn[:, :])
                kb.op(dve, lambda e: e.reciprocal(out=C.rstd[:, :], in_=C.rt[:, :]), r=[C.rt], w=[C.rstd])
                kb.stt(dve, ot[:, :], h2[:, :], C.rstd[:, :], gfb[:, :], ALU.mult, ALU.mult, r=[h2, C.rstd, gfb], w=[ot])
                kb.dma(pool, out[j * 128:(j + 1) * 128, :], ot[:, :], r=[ot])
        kb.barrier()
        kb.es = es_save
    kb.es.close()
    return nc


def _col(v, n):
    return np.ascontiguousarray(np.asarray(v, np.float32).reshape(n, 128).T)


def make_inputs(S, xb, p, W):
    NT = S // 128
    f = lambda a: np.ascontiguousarray(np.asarray(a, np.float32))
    w_in = W["w_in"]
    own_tiles = xb.reshape(NT // 2, 2, 128, D)[:, p].reshape(S // 2, D)
    half = 32
    inv = 1.0 / (10000.0 ** (np.arange(half, dtype=np.float32) * 2.0 / 64)).astype(np.float32)
    pos = np.arange(S, dtype=np.float32)
    ang = (pos[:, None] * inv[None, :]).astype(np.float32)
    cs = np.concatenate([np.cos(ang), np.sin(ang)], axis=1).astype(np.float32)
    cs_own = cs.reshape(NT // 2, 2, 128, 64)[:, p].reshape(S // 2, 64)
    t = np.arange(128)
    LL = np.concatenate([(t[:, None] <= t[None, :]), (t[:, None] < t[None, :])], axis=1).astype(np.float32) * np.float32(CDEC)
    maskU = np.concatenate([(t[:, None] < t[None, :]), (t[:, None] <= t[None, :])], axis=1).astype(np.float32)
    maskL = (t[:, None] > t[None, :]).astype(np.float32)
    bones = np.kron(np.eye(2, dtype=np.float32), np.ones((64, 64), np.float32))
    bones2 = np.kron(np.eye(2, dtype=np.float32), np.ones((64, 1), np.float32))
    s = np.arange(256)
    maskadd = np.where(s[None, :] <= (128 * p + t[:, None]), 0.0, NEG).astype(np.float32)
    par = np.zeros((128, 2), np.float32); par[:, 0] = p; par[:, 1] = 1 - p
    pow2 = np.tile((0.5 ** np.arange(1, NBISECT + 1, dtype=np.float64)).astype(np.float32)[None, :], (128, 1))
    m = {
        "xfull": f(xb), "xown": f(own_tiles),
        "g1col": _col(W["norm1_g"], 8), "g2col": _col(W["norm2_g"], 8), "gfrow": f(W["normf_g"]).reshape(1, D),
        "w_rw": f(w_in[:, 0:1792]),
        "w_kv": f(np.concatenate([w_in[:, 2304:2816], w_in[:, 2816:3328], w_in[:, 3840:3904]], axis=1)),
        "w_qi": f(np.concatenate([w_in[:, 1792:2304], w_in[:, 3328:3840], w_in[:, 3904:3912]], axis=1)),
        "w_gate": f(w_in[:, 3912:5960]),
        "mucol": _col(W["tshift_mu"], 14), "wdu": f(W["w_decay_up"]), "w0row": f(W["w0"]).reshape(1, 512),
        "aup": f(W["a_up"]), "a0col": _col(W["a0"], 4), "gup": f(W["g_up"]),
        "kkcol": _col(W["k_k"], 4), "kacol": _col(W["k_a"], 4), "rkcol": _col(np.asarray(W["r_k"]).reshape(512), 4),
        "lnxg": f(W["lnx_g"]).reshape(1, 512), "lnxb": f(W["lnx_b"]).reshape(1, 512),
        "w_orw": f(W["w_o_rwkv"]), "w_oatt": f(W["w_o_att"]), "w_out": f(W["w_out"]),
        "w_ffi": f(W["w_ffn_in"]), "w_ffo": f(W["w_ffn_out"]),
        "cs_full": cs, "cs_own": np.ascontiguousarray(cs_own),
        "identd": np.eye(128, dtype=np.float32), "LLd": LL, "maskUd": maskU, "maskLd": maskL,
        "bonesd": bones, "bones2d": bones2, "maskaddd": maskadd, "pard": par, "pow2d": pow2,
    }
    return m


_NC_CACHE = {}


def kernel(**inputs):
    x = np.asarray(inputs["x"], np.float32)
    B, S, _ = x.shape
    W = {k: np.asarray(v) for k, v in inputs.items() if k != "x"}
    if S not in _NC_CACHE:
        _NC_CACHE[S] = build(S)
    nc = _NC_CACHE[S]
    in_maps = []
    for c in range(2 * B):
        b, p = c // 2, c % 2
        in_maps.append(make_inputs(S, x[b], p, W))
    res = run_bass_kernel_spmd(nc, in_maps, core_ids=list(range(2 * B)))
    outp = np.zeros((B, S, D), np.float32)
    NT = S // 128
    for c in range(2 * B):
        b, p = c // 2, c % 2
        o = np.asarray(res.results[c]["out"], np.float32).reshape(NT // 2, 128, D)
        outp[b].reshape(NT // 2, 2, 128, D)[:, p] = o
    return outp
```

```python
import math
from contextlib import ExitStack
import numpy as np
import concourse.bass as bass
import concourse.mybir as mybir
from concourse.bass_utils import run_bass_kernel_spmd

F32 = mybir.dt.float32
BF16 = mybir.dt.bfloat16
ALU = mybir.AluOpType
AF = mybir.ActivationFunctionType
AX = mybir.AxisListType

D = 1024
KC = 8
NEG = -1.0e30
EPOCH = 4000
NDSLOT = 16
NBISECT = 12
CDEC = -math.exp(-0.5)


class T:
    def __init__(self, t):
        self.t = t
        self.w = None
        self.r = {}

    def __getitem__(self, k):
        return self.t[k]


class Eng:
    def __init__(self, kb, key, obj):
        self.kb, self.key, self.obj = kb, key, obj
        self.n = 0
        self.sems = []
        self.known = {}
        self.dn = 0
        self.dsems = None
        self.dcount = [0] * NDSLOT

    def sem_for(self, seq):
        ep = (seq - 1) // EPOCH
        while len(self.sems) <= ep:
            self.sems.append(self.kb.newsem())
        return self.sems[ep], (seq - 1) % EPOCH + 1


class KB:
    def __init__(self, nc):
        self.nc = nc
        self.es = ExitStack()
        self.nsem = 0
        self.pe = Eng(self, "pe", nc.tensor)
        self.act = Eng(self, "act", nc.scalar)
        self.dve = Eng(self, "dve", nc.vector)
        self.pool = Eng(self, "pool", nc.gpsimd)
        self.sp = Eng(self, "sp", nc.sync)
        self.engs = [self.pe, self.act, self.dve, self.pool, self.sp]
        self.dmatoks = {}
        self.psb = []
        self.psi = 0
        self.nrot = 8
        self.nm = 0

    def newsem(self):
        self.nsem += 1
        return self.es.enter_context(self.nc.semaphore("s%d" % self.nsem))

    def sb(self, shape, dt=F32, name=None):
        self.nm += 1
        return T(self.es.enter_context(self.nc.sbuf_tensor("%s_%d" % (name or "t", self.nm), list(shape), dt)))

    def init_psum(self):
        for i in range(8):
            self.psb.append(T(self.es.enter_context(self.nc.psum_tensor("ps%d" % i, [128, 512], F32))))
            self.psb[-1].excl = True

    def psum(self):
        b = self.psb[self.psi % self.nrot]
        self.psi += 1
        return b

    def psum_acc(self):
        self.pai = getattr(self, "pai", 0) + 1
        return self.psb[6 + self.pai % 2]

    def resolve(self, tok):
        key, seq = tok
        if isinstance(key, tuple):
            return self.dmatoks[key], seq
        e = getattr(self, key)
        return e.sem_for(seq)

    def _wait(self, E, tok):
        key, seq = tok
        if E.known.get(key, 0) >= seq:
            return
        E.known[key] = seq
        sem, val = self.resolve(tok)
        E.obj.wait_ge(sem, val)

    def _deps(self, E, r, w):
        for b in r:
            if b.w is not None:
                self._wait(E, b.w)
            if getattr(b, "excl", False):
                for k, tok in b.r.items():
                    if k != E.key:
                        self._wait(E, tok)
        for b in w:
            if b.w is not None and b.w[0] != E.key:
                self._wait(E, b.w)
            for k, tok in b.r.items():
                if k != E.key:
                    self._wait(E, tok)

    def op(self, E, fn, r=(), w=()):
        self._deps(E, r, w)
        ins = fn(E.obj)
        E.n += 1
        sem, val = E.sem_for(E.n)
        ins.then_inc(sem, 1)
        tok = (E.key, E.n)
        for b in r:
            b.r[E.key] = tok
        for b in w:
            b.w = tok
            b.r = {}

    def dma(self, Q, out, in_, r=(), w=()):
        self._deps(Q, r, w)
        if Q.dsems is None:
            Q.dsems = [self.newsem() for _ in range(NDSLOT)]
            for i in range(NDSLOT):
                self.dmatoks[("dma", Q.key, i)] = Q.dsems[i]
        slot = Q.dn % NDSLOT
        Q.dn += 1
        if Q.dcount[slot] > 0:
            self._wait(Q, (("dma", Q.key, slot), Q.dcount[slot] * 16))
        Q.dcount[slot] += 1
        Q.obj.dma_start(out=out, in_=in_).then_inc(Q.dsems[slot], 16)
        tok = (("dma", Q.key, slot), Q.dcount[slot] * 16)
        for b in r:
            b.r[tok[0]] = tok
        for b in w:
            b.w = tok
            b.r = {}

    def barrier(self):
        toks = []
        for e in self.engs:
            if e.n > 0:
                toks.append((e.key, e.n))
            if e.dsems is not None:
                for i in range(NDSLOT):
                    if e.dcount[i] > 0:
                        toks.append((("dma", e.key, i), e.dcount[i] * 16))
        for e in self.engs:
            for tok in toks:
                if tok[0] != e.key:
                    self._wait(e, tok)

    def mm(self, ps, out, lhsT, rhs, start, stop, r):
        self.op(self.pe, lambda e: e.matmul(out, lhsT, rhs, start=start, stop=stop), r=r, w=[ps])

    def tr(self, ps, out, in_, ident, r):
        self.op(self.pe, lambda e: e.transpose(out, in_, ident), r=r, w=[ps])

    def acopy(self, out, in_, r, w, func=AF.Copy, scale=1.0, bias=None, accum=None):
        def f(e):
            kw = {}
            if bias is not None:
                kw["bias"] = bias
            if accum is not None:
                kw["accum_out"] = accum
            return e.activation(out=out, in_=in_, func=func, scale=scale, **kw)
        self.op(self.act, f, r=r, w=w)

    def tt(self, E, out, a, b, op, r, w):
        self.op(E, lambda e: e.tensor_tensor(out=out, in0=a, in1=b, op=op), r=r, w=w)

    def ts(self, E, out, a, s1, op0, r, w, s2=None, op1=None, accum=None):
        def f(e):
            kw = {}
            if accum is not None:
                kw["accum_out"] = accum
            return e.tensor_scalar(out=out, in0=a, scalar1=s1, scalar2=s2, op0=op0,
                                   op1=(op1 if op1 is not None else ALU.bypass), **kw)
        self.op(E, f, r=r, w=w)

    def stt(self, E, out, a, s, b, op0, op1, r, w):
        self.op(E, lambda e: e.scalar_tensor_tensor(out=out, in0=a, scalar=s, in1=b, op0=op0, op1=op1), r=r, w=w)


def bc(ap, shape):
    return ap.to_broadcast(list(shape))


class Ctx:
    pass


def load_cast(kb, C, dst, dst_ap_fn, src_ap_fn, ncols, rows_shape, step=2048):
    sap, dap = src_ap_fn(0, ncols), dst_ap_fn(0, ncols)
    for k in range(sap.shape[1]):
        kb.dma(kb.pool, dap[:, k, :], sap[:, k, :], w=[dst])


def rms_uT(kb, C, xt, gcol, uT, dst=None):
    kb.acopy(C.junk[:, 0:D], xt[:, :], r=[xt], w=[C.junk, C.ss], func=AF.Square, accum=C.ss[:, :])
    kb.acopy(C.rt[:, :], C.ss[:, :], r=[C.ss], w=[C.rt], func=AF.Sqrt, scale=1.0 / D, bias=C.epsn[:, :])
    kb.op(kb.dve, lambda e: e.reciprocal(out=C.rstd[:, :], in_=C.rt[:, :]), r=[C.rt], w=[C.rstd])
    kb.ts(kb.dve, C.xn[:, :], xt[:, :], C.rstd[:, :], ALU.mult, r=[xt, C.rstd], w=[C.xn])
    ps = kb.psum()
    psv = ps[:, :].bitcast(BF16)
    for kc in range(KC):
        kb.tr(ps, psv[:, kc * 128:(kc + 1) * 128], C.xn[:, kc * 128:(kc + 1) * 128], C.ident[:, :], r=[C.xn, C.ident])
    kb.tt(kb.dve, (uT[:, :, :] if dst is None else dst), psv.rearrange("p (k t) -> p k t", t=128), bc(gcol[:, :].unsqueeze(2), [128, KC, 128]),
          ALU.mult, r=[ps, gcol], w=[uT])


def rope_tm(kb, C, ps, nh, cs, out, out_t):
    pv = ps[:, 0:nh * 64].rearrange("p (h two d) -> p h two d", two=2, d=32)
    cosb = bc(cs[:, 0:1, :].unsqueeze(1), [128, nh, 2, 32])
    sinb = bc(cs[:, 1:2, :].unsqueeze(1), [128, nh, 2, 32])
    A = C.ropeA[:, 0:nh * 64].rearrange("p (h two d) -> p h two d", two=2, d=32)
    Bm = C.ropeB[:, 0:nh * 64].rearrange("p (h two d) -> p h two d", two=2, d=32)
    ov = out.rearrange("p (h two d) -> p h two d", two=2, d=32)
    kb.tt(kb.dve, A, pv, cosb, ALU.mult, r=[ps, cs], w=[C.ropeA])
    kb.tt(kb.dve, Bm, pv, sinb, ALU.mult, r=[ps, cs], w=[C.ropeB])
    kb.tt(kb.pool, ov[:, :, 0:1, :], A[:, :, 0:1, :], Bm[:, :, 1:2, :], ALU.subtract, r=[C.ropeA, C.ropeB], w=[out_t])
    kb.tt(kb.pool, ov[:, :, 1:2, :], A[:, :, 1:2, :], Bm[:, :, 0:1, :], ALU.add, r=[C.ropeA, C.ropeB], w=[out_t])


def build(S, debug=False):
    NT = S // 128
    NQ = S // 256
    SO = S // 2
    nc = bass.Bass("TRN2", target_bir_lowering=False)
    kb = KB(nc)
    C = Ctx()

    def din(name, shape, dt=F32):
        return nc.dram_tensor(name, list(shape), dt, kind="ExternalInput").ap()

    def dscr(name, shape, dt):
        return nc.dram_tensor(name, list(shape), dt, kind=("ExternalOutput" if debug else "Internal")).ap()

    xfull = din("xfull", [S, D]); xown = din("xown", [SO, D])
    g1col = din("g1col", [128, 8]); g2col = din("g2col", [128, 8]); gfrow = din("gfrow", [1, D])
    w_rw = din("w_rw", [D, 1792]); w_kv = din("w_kv", [D, 1088]); w_qi = din("w_qi", [D, 1032]); w_gate = din("w_gate", [D, 2048])
    mucol = din("mucol", [128, 14]); wdu = din("wdu", [64, 512]); w0row = din("w0row", [1, 512])
    aup = din("aup", [64, 512]); a0col = din("a0col", [128, 4]); gup = din("gup", [128, 512])
    kkcol = din("kkcol", [128, 4]); kacol = din("kacol", [128, 4]); rkcol = din("rkcol", [128, 4])
    lnxg = din("lnxg", [1, 512]); lnxb = din("lnxb", [1, 512])
    w_orw = din("w_orw", [512, D]); w_oatt = din("w_oatt", [512, D]); w_out = din("w_out", [D, D])
    w_ffi = din("w_ffi", [D, 5632]); w_ffo = din("w_ffo", [2816, D])
    cs_full = din("cs_full", [S, 64]); cs_own = din("cs_own", [SO, 64])
    identd = din("identd", [128, 128]); LLd = din("LLd", [128, 256]); maskUd = din("maskUd", [128, 256]); maskLd = din("maskLd", [128, 128])
    bonesd = din("bonesd", [128, 128]); bones2d = din("bones2d", [128, 2]); maskaddd = din("maskaddd", [128, 256])
    pard = din("pard", [128, 2]); pow2d = din("pow2d", [128, NBISECT])
    out = nc.dram_tensor("out", [SO, D], F32, kind="ExternalOutput").ap()

    KTs = dscr("KTs", [4, 128, S], BF16)
    Vs = dscr("Vs", [4, 128, NT, 130], BF16)
    KIs = dscr("KIs", [64, S], BF16)
    YAs = dscr("YAs", [SO, 512], BF16)
    YBs = dscr("YBs", [SO, 512], BF16)
    H1s = dscr("H1s", [SO, D], F32)
    YAfull = dscr("YAfull", [S, 512], F32) if debug else None

    kb.init_psum()
    sp, pe, act, dve, pool = kb.sp, kb.pe, kb.act, kb.dve, kb.pool

    C.stgi = 0
    C.junk = kb.sb([128, 1024], BF16, "junk")
    C.ss = kb.sb([128, 1], F32); C.rt = kb.sb([128, 1], F32); C.rstd = kb.sb([128, 1], F32)
    C.xn = kb.sb([128, D], BF16, "xn")
    C.epsn = kb.sb([128, 1], F32)
    C.ident = kb.sb([128, 128], BF16, "ident")
    kb.op(dve, lambda e: e.memset(C.epsn[:, :], 1e-6), w=[C.epsn])
    C.kmaxb = kb.sb([128, 1], F32)

    def ld_const(dst, src_ap, shape, dt):
        if dt == F32:
            kb.dma(sp, dst[:, :], src_ap, w=[dst])
        else:
            kb.dma(pool, dst[:, :], src_ap, w=[dst])

    g1c = kb.sb([128, 8], F32); g2c = kb.sb([128, 8], F32); par = kb.sb([128, 2], F32)
    xt = [kb.sb([128, D], F32, "xt") for _ in range(2)]
    uT = kb.sb([128, KC, 128], BF16, "uT")
    STG = lambda: [kb.sb([128, 1024], F32, "stg") for _ in range(2)]
    with ExitStack() as t0:
        es_save = kb.es
        kb.es = t0
        ld_const(C.ident, identd[:, :], [128, 128], BF16)
        ld_const(g1c, g1col[:, :], [128, 8], F32)
        ld_const(g2c, g2col[:, :], [128, 8], F32)
        ld_const(par, pard[:, :], [128, 2], F32)
        kb.barrier()
        kb.es = es_save

    with ExitStack() as p1:
        es_save = kb.es
        kb.es = p1
        C.ropeA = kb.sb([128, 512], F32); C.ropeB = kb.sb([128, 512], F32)
        Wrw = kb.sb([128, KC, 1792], BF16, "Wrw")
        Wkv = kb.sb([128, KC, 1088], BF16, "Wkv")
        load_cast(kb, C, Wrw, lambda c0, n: Wrw[:, :, c0:c0 + n],
                  lambda c0, n: w_rw.rearrange("(k p) n -> p k n", p=128)[:, :, c0:c0 + n], 1792, None, step=128)
        load_cast(kb, C, Wkv, lambda c0, n: Wkv[:, :, c0:c0 + n],
                  lambda c0, n: w_kv.rearrange("(k p) n -> p k n", p=128)[:, :, c0:c0 + n], 1088, None, step=128)
        muc = kb.sb([128, 14], F32); ld_const(muc, mucol[:, :], [128, 14], F32)
        wdus = kb.sb([64, 512], BF16); ld_const(wdus, wdu[:, :], [64, 512], BF16)
        aups = kb.sb([128, 512], BF16)
        kb.dma(pool, aups[64:128, :], aup[:, :], w=[aups])
        gups = kb.sb([128, 512], BF16); ld_const(gups, gup[:, :], [128, 512], BF16)
        w0b = kb.sb([128, 512], F32); kb.dma(sp, w0b[:, :], w0row.partition_broadcast(128), w=[w0b])
        lgb = kb.sb([128, 512], F32); kb.dma(sp, lgb[:, :], lnxg.partition_broadcast(128), w=[lgb])
        lbb = kb.sb([128, 512], F32); kb.dma(sp, lbb[:, :], lnxb.partition_broadcast(128), w=[lbb])
        a0c = kb.sb([128, 4], F32); ld_const(a0c, a0col[:, :], [128, 4], F32)
        kkc = kb.sb([128, 4], F32); ld_const(kkc, kkcol[:, :], [128, 4], F32)
        kac = kb.sb([128, 4], F32); ld_const(kac, kacol[:, :], [128, 4], F32)
        rkc = kb.sb([128, 4], F32); ld_const(rkc, rkcol[:, :], [128, 4], F32)
        LL = kb.sb([128, 256], F32); ld_const(LL, LLd[:, :], [128, 256], F32)
        maskU = kb.sb([128, 256], F32); ld_const(maskU, maskUd[:, :], [128, 256], F32)
        maskL = kb.sb([128, 128], F32); ld_const(maskL, maskLd[:, :], [128, 128], F32)
        bones = kb.sb([128, 128], F32); ld_const(bones, bonesd[:, :], [128, 128], F32)
        bones2 = kb.sb([128, 2], BF16); ld_const(bones2, bones2d[:, :], [128, 2], BF16)

        pbuf = kb.sb([128, 14, 257], F32, "pbuf")
        kb.op(pool, lambda e: e.memset(pbuf[:, :, :], 0.0), w=[pbuf])
        psh = kb.sb([128, 14, 256], F32, "psh")
        uT2 = kb.sb([128, KC, 256], BF16, "uT2")
        x4 = xt + [kb.sb([128, D], F32, "x4") for _ in range(2)]
        wa_bf = kb.sb([128, 128], BF16); sg_bf = kb.sb([128, 128], BF16)
        zT = kb.sb([128, 512], F32); sigT = kb.sb([128, 512], F32)
        Epe = kb.sb([128, 4, 2, 128], F32); Eneg = kb.sb([128, 4, 128], F32)
        a_sb = kb.sb([128, 4, 128], F32); g_sb = kb.sb([128, 512], F32)
        kk = kb.sb([128, 4, 128], F32); kk2 = kb.sb([128, 4, 128], F32); sq = kb.sb([128, 512], F32)
        kap = kb.sb([128, 4, 128], F32); tmpa = kb.sb([128, 4, 128], F32); kmod = kb.sb([128, 4, 128], F32)
        bb = kb.sb([128, 4, 128], F32); rkr0 = kb.sb([128, 4, 128], F32)
        krt = kb.sb([128, 4, 2, 128], BF16); ktl = kb.sb([128, 4, 128], BF16); btl = kb.sb([128, 4, 128], BF16)
        rkr = kb.sb([128, 4, 128], BF16); v_bf = kb.sb([128, 4, 128], BF16)
        Vtm = kb.sb([128, 512], BF16); Ktm = kb.sb([128, 512], BF16); Btm = kb.sb([128, 512], BF16)
        s_sb = kb.sb([128, 8], F32)
        Tst = kb.sb([128, 4, 64], F32, "Tst"); Tbf = kb.sb([128, 4, 64], BF16, "Tbf")
        kb.op(pool, lambda e: e.memset(Tst[:, :, :], 0.0), w=[Tst])
        kb.op(pool, lambda e: e.memset(Tbf[:, :, :], 0.0), w=[Tbf])
        Ttmp = kb.sb([128, 4, 64], F32)
        Hh = []
        for h in range(8):
            o = Ctx()
            o.Akr = kb.sb([128, 256], BF16); o.Arb = kb.sb([128, 128], BF16)
            o.T3 = [kb.sb([128, 384], BF16) for _ in range(2)]
            o.X = kb.sb([128, 64], BF16)
            Hh.append(o)
        Uneg = kb.sb([128, 512], BF16)
        Yall = kb.sb([128, 512], F32); Ysq = kb.sb([128, 512], F32)
        st8 = [kb.sb([128, 8], F32) for _ in range(6)]
        Yn = kb.sb([128, 512], F32); tmpY = kb.sb([128, 512], F32)
        ya = [kb.sb([128, 512], F32) for _ in range(2)]
        ya_sel = kb.sb([128, 512], BF16)
        cs = [kb.sb([128, 2, 32], F32) for _ in range(2)]
        kr_bf = kb.sb([128, 512], BF16); krT = kb.sb([128, 4, 128], BF16); vb = kb.sb([128, 8, 65], BF16)
        kb.op(pool, lambda e: e.memset(vb[:, :, 64:65], 1.0), w=[vb])
        ki_bf = kb.sb([128, 64], BF16); kiT = kb.sb([64, 128], BF16)

        identf = kb.sb([128, 128], F32); ld_const(identf, identd[:, :], [128, 128], F32)
        ones_r = kb.sb([1, 128], F32)
        kb.op(pool, lambda e: e.memset(ones_r[:, :], 1.0), w=[ones_r])
        kmx = kb.sb([128, 1], F32); k8 = kb.sb([128, 8], F32); k1 = kb.sb([128, 1], F32)
        kb.op(pool, lambda e: e.memset(kmx[:, :], 0.0), w=[kmx])
        for i in range(NT):
            cst = cs[i % 2]
            tsl = slice((i % 2) * 128, (i % 2) * 128 + 128)
            kb.dma(sp, cst[:, :, :], cs_full[i * 128:(i + 1) * 128, :].rearrange("p (a d) -> p a d", d=32), w=[cst])
            if i % 2 == 0:
                for t in range(2):
                    x_t = x4[((i // 2) % 2) * 2 + t]
                    kb.dma(sp, x_t[:, :], xfull[(i + t) * 128:(i + t + 1) * 128, :], w=[x_t])
                for t in range(2):
                    x_t = x4[((i // 2) % 2) * 2 + t]
                    rms_uT(kb, C, x_t, g1c, uT2, dst=uT2[:, :, t * 128:(t + 1) * 128])
                for g0 in range(0, 14, 2):
                    ps = kb.psum()
                    for m in range(g0, g0 + 2):
                        for kc in range(KC):
                            kb.mm(ps, ps[:, (m - g0) * 256:(m - g0 + 1) * 256], Wrw[:, kc, m * 128:(m + 1) * 128], uT2[:, kc, :],
                                  kc == 0, kc == KC - 1, r=[Wrw, uT2])
                    kb.acopy(pbuf[:, g0:g0 + 2, 1:257], ps[:, :].rearrange("p (m t) -> p m t", t=256), r=[ps], w=[pbuf])
                kb.tt(dve, psh[:, :, :], pbuf[:, :, 0:256], pbuf[:, :, 1:257], ALU.subtract, r=[pbuf], w=[psh])
                kb.tt(dve, psh[:, :, :], psh[:, :, :], bc(muc[:, :].unsqueeze(2), [128, 14, 256]), ALU.mult, r=[psh, muc], w=[psh])
                kb.tt(dve, psh[:, :, :], psh[:, :, :], pbuf[:, :, 1:257], ALU.add, r=[psh, pbuf], w=[psh])
                kb.op(pool, lambda e: e.tensor_copy(out=pbuf[:, :, 0:1], in_=pbuf[:, :, 256:257]), r=[pbuf], w=[pbuf])
            r_ = psh[:, 0:4, tsl]; k_ = psh[:, 4:8, tsl]; v_ = psh[:, 8:12, tsl]
            kb.acopy(wa_bf[0:64, :], psh[0:64, 12, tsl], r=[psh], w=[wa_bf], func=AF.Tanh)
            kb.acopy(wa_bf[64:128, :], psh[64:128, 12, tsl], r=[psh], w=[wa_bf])
            kb.acopy(sg_bf[:, :], psh[:, 13, tsl], r=[psh], w=[sg_bf], func=AF.Sigmoid)
            psk = kb.psum(); psv_ = kb.psum(); psi = kb.psum()
            for (pst, c0, n) in ((psk, 0, 512), (psv_, 512, 512), (psi, 1024, 64)):
                for kc in range(KC):
                    kb.mm(pst, pst[:, 0:n], uT2[:, kc, tsl], Wkv[:, kc, c0:c0 + n], kc == 0, kc == KC - 1, r=[uT2, Wkv])
            kb.acopy(vb[:, :, 0:64], psv_[:, :].rearrange("p (h d) -> p h d", d=64), r=[psv_], w=[vb])
            kb.dma(pool, Vs[:, :, i, :].rearrange("m p f -> p m f"), vb[:, :, :].rearrange("p (m a) d -> p m (a d)", a=2), r=[vb])
            rope_tm(kb, C, psk, 8, cst, kr_bf[:, :], kr_bf)
            kb.tt(pool, tmpY[:, :], kr_bf[:, :], kr_bf[:, :], ALU.mult, r=[kr_bf], w=[tmpY])
            kb.op(dve, lambda e: e.tensor_reduce(out=k8[:, :], in_=tmpY[:, :].rearrange("p (h d) -> p h d", d=64), axis=AX.X, op=ALU.add), r=[tmpY], w=[k8])
            kb.op(dve, lambda e: e.tensor_reduce(out=k1[:, :], in_=k8[:, :], axis=AX.X, op=ALU.max), r=[k8], w=[k1])
            kb.tt(dve, kmx[:, :], kmx[:, :], k1[:, :], ALU.max, r=[kmx, k1], w=[kmx])
            ps = kb.psum(); psb_ = ps[:, :].bitcast(BF16)
            for m in range(4):
                kb.tr(ps, psb_[:, m * 128:(m + 1) * 128], kr_bf[:, m * 128:(m + 1) * 128], C.ident[:, :], r=[kr_bf, C.ident])
            kb.acopy(krT[:, :, :], psb_[:, 0:512].rearrange("p (m t) -> p m t", t=128), r=[ps], w=[krT])
            kb.dma(pool, KTs[:, :, i * 128:(i + 1) * 128].rearrange("m p s -> p m s"), krT[:, :, :], r=[krT])
            rope_tm(kb, C, psi, 1, cst, ki_bf[:, :], ki_bf)
            ps = kb.psum(); psb_ = ps[:, :].bitcast(BF16)
            kb.tr(ps, psb_[0:64, 0:128], ki_bf[:, 0:64], C.ident[:, :], r=[ki_bf, C.ident])
            kb.acopy(kiT[:, :], psb_[0:64, 0:128], r=[ps], w=[kiT])
            kb.dma(pool, KIs[:, i * 128:(i + 1) * 128], kiT[:, :], r=[kiT])
            ps = kb.psum()
            for m in range(4):
                kb.mm(ps, ps[:, m * 128:(m + 1) * 128], aups[64:128, m * 128:(m + 1) * 128], wa_bf[64:128, :], True, True, r=[aups, wa_bf])
            for m in range(4):
                kb.acopy(a_sb[:, m, :], ps[:, m * 128:(m + 1) * 128], r=[ps, a0c], w=[a_sb], func=AF.Sigmoid, bias=a0c[:, m:m + 1])
            ps = kb.psum()
            kb.mm(ps, ps[:, :], wa_bf[0:64, :], wdus[:, :], True, True, r=[wa_bf, wdus])
            kb.tt(dve, zT[:, :], ps[:, :], w0b[:, :], ALU.add, r=[ps, w0b], w=[zT])
            kb.acopy(sigT[:, :], zT[:, :], r=[zT], w=[sigT], func=AF.Sigmoid)
            for half in range(2):
                ps = kb.psum()
                for mm_ in range(2):
                    m = half * 2 + mm_
                    kb.mm(ps, ps[:, mm_ * 256:(mm_ + 1) * 256], sigT[:, m * 128:(m + 1) * 128], LL[:, :], True, True, r=[sigT, LL])
                pv = ps[:, :].rearrange("p (m a t) -> p m a t", a=2, t=128)
                kb.acopy(Epe[:, half * 2:half * 2 + 2, :, :], pv, r=[ps], w=[Epe], func=AF.Exp)
                kb.acopy(Eneg[:, half * 2:half * 2 + 2, :], pv[:, :, 0, :], r=[ps], w=[Eneg], func=AF.Exp, scale=-1.0)
            ps = kb.psum()
            kb.mm(ps, ps[:, :], sg_bf[:, :], gups[:, :], True, True, r=[sg_bf, gups])
            kb.acopy(g_sb[:, :], ps[:, :], r=[ps], w=[g_sb])
            kb.tt(dve, kk[:, :, :], k_, bc(kkc[:, :].unsqueeze(2), [128, 4, 128]), ALU.mult, r=[psh, kkc], w=[kk])
            kb.tt(dve, kk2[:, :, :], kk[:, :, :], kk[:, :, :], ALU.mult, r=[kk], w=[kk2])
            ps = kb.psum()
            for m in range(4):
                kb.mm(ps, ps[:, m * 128:(m + 1) * 128], bones[:, :], kk2[:, m, :], True, True, r=[bones, kk2])
            kb.acopy(sq[:, :], ps[:, :], r=[ps], w=[sq], func=AF.Sqrt)
            kb.ts(dve, sq[:, :], sq[:, :], 1e-12, ALU.max, r=[sq], w=[sq])
            kb.op(dve, lambda e: e.reciprocal(out=sq[:, :], in_=sq[:, :]), r=[sq], w=[sq])
            kb.tt(dve, kap[:, :, :], kk[:, :, :], sq[:, :].rearrange("p (m t) -> p m t", t=128), ALU.mult, r=[kk, sq], w=[kap])
            kb.stt(dve, tmpa[:, :, :], a_sb[:, :, :], -1.0, bc(kac[:, :].unsqueeze(2), [128, 4, 128]), ALU.add, ALU.mult, r=[a_sb, kac], w=[tmpa])
            kb.stt(dve, kmod[:, :, :], tmpa[:, :, :], 1.0, k_, ALU.add, ALU.mult, r=[tmpa, psh], w=[kmod])
            kb.tt(dve, bb[:, :, :], kap[:, :, :], a_sb[:, :, :], ALU.mult, r=[kap, a_sb], w=[bb])
            kb.tt(dve, krt[:, :, 1, :], r_, Epe[:, :, 0, :], ALU.mult, r=[psh, Epe], w=[krt])
            kb.tt(dve, krt[:, :, 0, :], kap[:, :, :], Epe[:, :, 1, :], ALU.mult, r=[kap, Epe], w=[krt])
            kb.tt(dve, ktl[:, :, :], kmod[:, :, :], Eneg[:, :, :], ALU.mult, r=[kmod, Eneg], w=[ktl])
            kb.tt(dve, btl[:, :, :], bb[:, :, :], Eneg[:, :, :], ALU.mult, r=[bb, Eneg], w=[btl])
            kb.acopy(v_bf[:, :, :], v_, r=[psh], w=[v_bf])
            for src, dst in ((v_bf, Vtm), (ktl, Ktm), (btl, Btm)):
                ps = kb.psum()
                psv = ps[:, :].bitcast(BF16)
                for m in range(4):
                    kb.tr(ps, psv[:, m * 128:(m + 1) * 128], src[:, m, :], C.ident[:, :], r=[src, C.ident])
                kb.acopy(dst[:, :], psv[:, 0:512], r=[ps], w=[dst])
            hp = lambda h: (h // 2, slice((h % 2) * 64, (h % 2) * 64 + 64))
            for h in range(8):
                m, P = hp(h); o = Hh[h]
                psA = kb.psum()
                kb.mm(psA, psA[:, 0:256], ktl[P, m, :], krt[P, m, :, :].rearrange("p a t -> p (a t)"), True, True, r=[ktl, krt])
                kb.tt(dve, o.Akr[:, :], psA[:, 0:256], maskU[:, :], ALU.mult, r=[psA, maskU], w=[o.Akr])
                psB = kb.psum()
                kb.mm(psB, psB[:, 0:256], btl[P, m, :], krt[P, m, :, :].rearrange("p a t -> p (a t)"), True, True, r=[btl, krt])
                kb.mm(psB, psB[:, 256:384], krt[P, m, 0, :], btl[P, m, :], True, True, r=[btl, krt])
                kb.stt(dve, o.T3[0][:, 128:256], psB[:, 0:128], -1.0, maskU[:, 0:128], ALU.mult, ALU.mult, r=[psB, maskU], w=[o.T3[0]])
                kb.stt(dve, o.T3[0][:, 256:384], psB[:, 256:384], -1.0, maskL[:, :], ALU.mult, ALU.mult, r=[psB, maskL], w=[o.T3[0]])
                kb.tt(dve, o.Arb[:, :], psB[:, 128:256], maskU[:, 128:256], ALU.mult, r=[psB, maskU], w=[o.Arb])
            for lev in range(7):
                for h in range(8):
                    o = Hh[h]
                    Tp = o.T3[lev % 2]; Tn = o.T3[(lev + 1) % 2]
                    ps = kb.psum()
                    if lev == 0:
                        kb.mm(ps, ps[:, 128:256], Tp[:, 256:384], Tp[:, 128:256], True, True, r=[Tp])
                        kb.mm(ps, ps[:, 256:384], Tp[:, 128:256], Tp[:, 256:384], True, True, r=[Tp])
                        kb.acopy(Tn[:, 128:384], ps[:, 128:384], r=[ps], w=[Tn])
                        kb.tt(dve, Tn[:, 0:128], Tp[:, 128:256], C.ident[:, :], ALU.add, r=[Tp, C.ident], w=[Tn])
                        continue
                    if lev < 6:
                        kb.mm(ps, ps[:, 0:256], Tp[:, 256:384], Tp[:, 0:256], True, True, r=[Tp])
                        kb.mm(ps, ps[:, 256:384], Tp[:, 128:256], Tp[:, 256:384], True, True, r=[Tp])
                        kb.acopy(Tn[:, 128:384], ps[:, 128:384], r=[ps], w=[Tn])
                    else:
                        kb.mm(ps, ps[:, 0:128], Tp[:, 256:384], Tp[:, 0:128], True, True, r=[Tp])
                    kb.tt(dve, Tn[:, 0:128], ps[:, 0:128], Tp[:, 0:128], ALU.add, r=[ps, Tp], w=[Tn])
            for h in range(8):
                m, P = hp(h); o = Hh[h]
                hs = slice(h * 64, (h + 1) * 64)
                ps = kb.psum()
                kb.mm(ps, ps[:, 0:64], o.Akr[:, 0:128], Vtm[:, hs], True, False, r=[o.Akr, Vtm])
                kb.mm(ps, ps[:, 0:64], krt[P, m, 0, :], Tbf[P, m, :], False, True, r=[krt, Tbf])
                kb.acopy(o.X[:, :], ps[:, 0:64], r=[ps], w=[o.X])
            for h in range(8):
                m, P = hp(h); o = Hh[h]
                Wf = o.T3[1]
                hs = slice(h * 64, (h + 1) * 64)
                ps = kb.psum()
                kb.mm(ps, ps[:, 0:64], Wf[:, 0:128], o.X[:, :], True, True, r=[Wf, o.X])
                kb.acopy(Uneg[:, hs], ps[:, 0:64], r=[ps], w=[Uneg], scale=-1.0)
            for h in range(8):
                m, P = hp(h); o = Hh[h]
                hs = slice(h * 64, (h + 1) * 64)
                ps = kb.psum()
                kb.mm(ps, ps[:, 0:64], o.Akr[:, 128:256], Vtm[:, hs], True, False, r=[o.Akr, Vtm])
                kb.mm(ps, ps[:, 0:64], krt[P, m, 1, :], Tbf[P, m, :], False, False, r=[krt, Tbf])
                kb.mm(ps, ps[:, 0:64], o.Arb[:, :], Uneg[:, hs], False, True, r=[o.Arb, Uneg])
                kb.acopy(Yall[:, hs], ps[:, 0:64], r=[ps], w=[Yall])
            for m in range(4):
                ms = slice(m * 128, (m + 1) * 128)
                ps = kb.psum()
                kb.mm(ps, ps[:, 0:128], Ktm[:, ms], Vtm[:, ms], True, False, r=[Ktm, Vtm])
                kb.mm(ps, ps[:, 0:128], Btm[:, ms], Uneg[:, ms], False, True, r=[Btm, Uneg])
                for hh in range(2):
                    P = slice(hh * 64, hh * 64 + 64)
                    kb.tt(dve, Ttmp[P, m, :], ps[P, hh * 64:hh * 64 + 64], Tst[P, m, :], ALU.add, r=[ps, Tst], w=[Ttmp])
                kb.ts(dve, Tst[:, m, :], Ttmp[:, m, :], Epe[:, m, 0, 127:128], ALU.mult, r=[Ttmp, Epe], w=[Tst])
                kb.op(pool, lambda e, m=m: e.tensor_copy(out=Tbf[:, m, :], in_=Tst[:, m, :]), r=[Tst], w=[Tbf])
            kb.tt(pool, rkr0[:, :, :], r_, kmod[:, :, :], ALU.mult, r=[psh, kmod], w=[rkr0])
            kb.tt(pool, rkr[:, :, :], rkr0[:, :, :], bc(rkc[:, :].unsqueeze(2), [128, 4, 128]), ALU.mult, r=[rkr0, rkc], w=[rkr])
            ps = kb.psum()
            for m in range(4):
                kb.mm(ps, ps[:, 2 * m:2 * m + 2], rkr[:, m, :], bones2[:, :], True, True, r=[rkr, bones2])
            kb.acopy(s_sb[:, :], ps[:, 0:8], r=[ps], w=[s_sb])
            Y3 = Yall[:, :].rearrange("p (h d) -> p h d", d=64)
            sm, sqs, mean, var, rstd8, msq = st8
            kb.op(dve, lambda e: e.tensor_reduce(out=sm[:, :], in_=Y3, axis=AX.X, op=ALU.add), r=[Yall], w=[sm])
            kb.tt(pool, Ysq[:, :], Yall[:, :], Yall[:, :], ALU.mult, r=[Yall], w=[Ysq])
            kb.op(dve, lambda e: e.tensor_reduce(out=sqs[:, :], in_=Ysq[:, :].rearrange("p (h d) -> p h d", d=64), axis=AX.X, op=ALU.add), r=[Ysq], w=[sqs])
            kb.ts(dve, mean[:, :], sm[:, :], 1.0 / 64, ALU.mult, r=[sm], w=[mean])
            kb.tt(dve, msq[:, :], mean[:, :], mean[:, :], ALU.mult, r=[mean], w=[msq])
            kb.stt(dve, var[:, :], sqs[:, :], 1.0 / 64, msq[:, :], ALU.mult, ALU.subtract, r=[sqs, msq], w=[var])
            kb.ts(dve, var[:, :], var[:, :], 64e-5, ALU.add, r=[var], w=[var])
            kb.acopy(rstd8[:, :], var[:, :], r=[var], w=[rstd8], func=AF.Sqrt)
            kb.op(dve, lambda e: e.reciprocal(out=rstd8[:, :], in_=rstd8[:, :]), r=[rstd8], w=[rstd8])
            Yn3 = Yn[:, :].rearrange("p (h d) -> p h d", d=64)
            kb.tt(dve, Yn3, Y3, bc(mean[:, :].unsqueeze(2), [128, 8, 64]), ALU.subtract, r=[Yall, mean], w=[Yn])
            kb.tt(dve, Yn3, Yn3, bc(rstd8[:, :].unsqueeze(2), [128, 8, 64]), ALU.mult, r=[Yn, rstd8], w=[Yn])
            kb.tt(pool, Yn[:, :], Yn[:, :], lgb[:, :], ALU.mult, r=[Yn, lgb], w=[Yn])
            kb.tt(pool, Yn[:, :], Yn[:, :], lbb[:, :], ALU.add, r=[Yn, lbb], w=[Yn])
            kb.tt(dve, tmpY[:, :].rearrange("p (h d) -> p h d", d=64), Vtm[:, :].rearrange("p (h d) -> p h d", d=64),
                  bc(s_sb[:, :].unsqueeze(2), [128, 8, 64]), ALU.mult, r=[Vtm, s_sb], w=[tmpY])
            kb.tt(dve, Yn[:, :], Yn[:, :], tmpY[:, :], ALU.add, r=[Yn, tmpY], w=[Yn])
            yat = ya[i % 2]
            kb.tt(dve, yat[:, :], Yn[:, :], g_sb[:, :], ALU.mult, r=[Yn, g_sb], w=[yat])
            if debug:
                kb.dma(pool, YAfull[i * 128:(i + 1) * 128, :], yat[:, :], r=[yat])
            if i % 2 == 1:
                kb.ts(dve, tmpY[:, :], ya[0][:, :], par[:, 1:2], ALU.mult, r=[ya[0], par], w=[tmpY])
                kb.stt(dve, ya_sel[:, :], ya[1][:, :], par[:, 0:1], tmpY[:, :], ALU.mult, ALU.add, r=[ya[1], par, tmpY], w=[ya_sel])
                kb.dma(pool, YAs[(i // 2) * 128:(i // 2 + 1) * 128, :], ya_sel[:, :], r=[ya_sel])
        ps = kb.psum()
        kb.tr(ps, ps[0:1, 0:128], kmx[:, 0:1], identf[:, :], r=[kmx, identf])
        kb.op(dve, lambda e: e.tensor_reduce(out=k1[0:1, :], in_=ps[0:1, 0:128], axis=AX.X, op=ALU.max), r=[ps], w=[k1])
        ps2 = kb.psum()
        kb.mm(ps2, ps2[:, 0:1], ones_r[:, :], k1[0:1, :], True, True, r=[ones_r, k1])
        kb.ts(dve, C.kmaxb[:, :], ps2[:, 0:1], 1.02, ALU.mult, r=[ps2], w=[C.kmaxb])
        kb.barrier()
        kb.es = es_save

    with ExitStack() as p2:
        es_save = kb.es
        kb.es = p2
        kb.nrot = 6
        C.ropeA = kb.sb([128, 512], F32); C.ropeB = kb.sb([128, 512], F32)
        Wqi = kb.sb([128, KC, 1032], BF16, "Wqi")
        maskadd = kb.sb([128, 256], F32); pow2 = kb.sb([128, NBISECT], F32)
        with ExitStack() as t2:
            kb.es = t2
            load_cast(kb, C, Wqi, lambda c0, n: Wqi[:, :, c0:c0 + n],
                      lambda c0, n: w_qi.rearrange("(k p) n -> p k n", p=128)[:, :, c0:c0 + n], 1032, None, step=128)
            ld_const(maskadd, maskaddd[:, :], [128, 256], F32)
            ld_const(pow2, pow2d[:, :], [128, NBISECT], F32)
            kb.barrier()
            kb.es = p2
        score = kb.sb([128, S], F32, "score")
        maskT = [kb.sb([128, NT, 128], BF16, "maskT")] * 2
        ki_sb = kb.sb([128, S], BF16, "ki_sb")
        m01 = kb.sb([128, S], BF16, "m01")
        pTu = [kb.sb([128, 512], BF16, "pTu") for _ in range(3)]
        pTm = [kb.sb([128, 512], BF16, "pTm") for _ in range(3)]
        kt_sb = [kb.sb([128, S], BF16, "kt_sb") for _ in range(2)]
        v_sb = [kb.sb([128, NT, 2, 65], BF16, "v_sb") for _ in range(2)]
        cso = [kb.sb([128, 2, 32], F32) for _ in range(2)]
        q_bf = kb.sb([128, 512], BF16); qi_bf = kb.sb([128, 512], BF16)
        qT = [kb.sb([128, 4, 2, 128], BF16) for _ in range(2)]; qiT = kb.sb([128, 4, 128], BF16)
        for qt_ in qT:
            kb.op(pool, lambda e, qt_=qt_: e.memset(qt_[:, :, :, :], 0.0), w=[qt_])
        w_sb = kb.sb([128, 8], F32)
        rl = [kb.sb([128, 512], F32) for _ in range(2)] + [C.ropeB]
        acc2 = [kb.sb([128, 512], F32)] * 2
        ptmp = [kb.sb([128, 512], F32)] * 2
        Bt = kb.sb([128, 1], F32); lo = kb.sb([128, 1], F32); mid = kb.sb([128, 1], F32); cnt = kb.sb([128, 1], F32)
        stp = kb.sb([128, 1], F32); Wtab = kb.sb([128, NBISECT], F32); W2tab = kb.sb([128, NBISECT], F32); w0t = kb.sb([128, 1], F32)
        q8 = kb.sb([128, 8], F32); q1 = kb.sb([128, 1], F32); mq = kb.sb([128, 1], F32); qg = kb.sb([1, 1], F32)
        negm = [kb.sb([128, 1], F32) for _ in range(2)]
        identf = kb.sb([128, 128], F32); kb.dma(sp, identf[:, :], identd[:, :], w=[identf])
        ones_r = kb.sb([1, 128], F32)
        kb.op(pool, lambda e: e.memset(ones_r[:, :], 1.0), w=[ones_r])
        rinv8 = kb.sb([128, 8], F32)
        ybu = kb.sb([128, 8, 65], F32)
        yb = kb.sb([128, 512], BF16)
        cn = Ctx(); cn.kt = 0; cn.rl = 0; cn.pb = 0; cn.pt = 0; cn.h = 0

        def front(j):
            par = j % 2
            Lk = 256 * (j + 1)
            nblk = Lk // 128
            x_t = xt[j % 2]; cst = cso[par]
            kb.dma(sp, x_t[:, :], xown[j * 128:(j + 1) * 128, :], w=[x_t])
            kb.dma(sp, cst[:, :, :], cs_own[j * 128:(j + 1) * 128, :].rearrange("p (a d) -> p a d", d=32), w=[cst])
            kb.dma(sp, ki_sb[0:64, 0:Lk], KIs[:, 0:Lk], w=[ki_sb])
            kb.dma(sp, ki_sb[64:128, 0:Lk], KIs[:, 0:Lk], w=[ki_sb])
            rms_uT(kb, C, x_t, g1c, uT)
            psq = kb.psum(); psqi = kb.psum(); psw = kb.psum()
            for (pst, c0, n) in ((psq, 0, 512), (psqi, 512, 512), (psw, 1024, 8)):
                for kc in range(KC):
                    kb.mm(pst, pst[:, 0:n], uT[:, kc, :], Wqi[:, kc, c0:c0 + n], kc == 0, kc == KC - 1, r=[uT, Wqi])
            kb.acopy(w_sb[:, :], psw[:, 0:8], r=[psw], w=[w_sb])
            kb.acopy(C.ropeA[:, :], psq[:, :], r=[psq], w=[C.ropeA], func=AF.Square)
            kb.op(dve, lambda e: e.tensor_reduce(out=q8[:, :], in_=C.ropeA[:, :].rearrange("p (h d) -> p h d", d=64), axis=AX.X, op=ALU.add), r=[C.ropeA], w=[q8])
            kb.op(dve, lambda e: e.tensor_reduce(out=q1[:, :], in_=q8[:, :], axis=AX.X, op=ALU.max), r=[q8], w=[q1])
            pst_ = kb.psum()
            kb.tr(pst_, pst_[0:1, 0:128], q1[:, 0:1], identf[:, :], r=[q1, identf])
            kb.op(dve, lambda e, pst_=pst_: e.tensor_reduce(out=qg[:, :], in_=pst_[0:1, 0:128], axis=AX.X, op=ALU.max), r=[pst_], w=[qg])
            psg_ = kb.psum()
            kb.mm(psg_, psg_[:, 0:1], ones_r[:, :], qg[:, :], True, True, r=[ones_r, qg])
            kb.tt(dve, q1[:, :], psg_[:, 0:1], C.kmaxb[:, :], ALU.mult, r=[psg_, C.kmaxb], w=[q1])
            kb.acopy(mq[:, :], q1[:, :], r=[q1], w=[mq], func=AF.Sqrt, scale=1.0 / 64)
            kb.ts(dve, negm[par][:, :], mq[:, :], -1.0, ALU.mult, r=[mq], w=[negm[par]])
            rope_tm(kb, C, psq, 8, cst, q_bf[:, :], q_bf)
            rope_tm(kb, C, psqi, 8, cst, qi_bf[:, :], qi_bf)
            for src, dst in ((q_bf, qT[par]), (qi_bf, qiT)):
                ps = kb.psum(); psb_ = ps[:, :].bitcast(BF16)
                for m in range(4):
                    kb.tr(ps, psb_[:, m * 128:(m + 1) * 128], src[:, m * 128:(m + 1) * 128], C.ident[:, :], r=[src, C.ident])
                if dst is qiT:
                    kb.acopy(dst[:, :, :], psb_[:, 0:512].rearrange("p (m t) -> p m t", t=128), r=[ps], w=[dst])
                else:
                    kb.acopy(dst[0:64, :, 0, :], psb_[0:64, 0:512].rearrange("p (m t) -> p m t", t=128), r=[ps], w=[dst])
                    kb.acopy(dst[64:128, :, 1, :], psb_[64:128, 0:512].rearrange("p (m t) -> p m t", t=128), r=[ps], w=[dst])
            for ci, c0 in enumerate(range(0, Lk, 512)):
                n = min(512, Lk - c0)
                a2 = acc2[ci % 2]
                for h in range(8):
                    m, P = h // 2, slice((h % 2) * 64, (h % 2) * 64 + 64)
                    ps = kb.psum()
                    kb.mm(ps, ps[:, 0:n], qiT[P, m, :], ki_sb[P, c0:c0 + n], True, True, r=[qiT, ki_sb])
                    rlt = rl[cn.rl % 3]; cn.rl += 1
                    kb.acopy(rlt[:, 0:n], ps[:, 0:n], r=[ps], w=[rlt], func=AF.Relu)
                    if h == 0:
                        kb.ts(dve, score[:, c0:c0 + n], rlt[:, 0:n], w_sb[:, 0:1], ALU.mult, r=[rlt, w_sb], w=[score])
                    else:
                        kb.stt(dve, score[:, c0:c0 + n], rlt[:, 0:n], w_sb[:, h:h + 1], score[:, c0:c0 + n], ALU.mult, ALU.add,
                               r=[rlt, w_sb, score], w=[score])
            mb = m01
            kb.op(dve, lambda e, Lk=Lk: e.tensor_reduce(out=Bt[:, :], in_=score[:, 0:Lk], axis=AX.X, op=ALU.max, apply_absolute_value=True),
                  r=[score], w=[Bt])
            kb.tt(dve, score[:, Lk - 256:Lk], score[:, Lk - 256:Lk], maskadd[:, :], ALU.add, r=[score, maskadd], w=[score])
            kb.ts(dve, lo[:, :], Bt[:, :], -1.001, ALU.mult, r=[Bt], w=[lo], s2=-1e-30, op1=ALU.add)
            kb.ts(dve, w0t[:, :], Bt[:, :], 2.003, ALU.mult, r=[Bt], w=[w0t], s2=2e-30, op1=ALU.add)
            kb.ts(dve, Wtab[:, :], pow2[:, :], w0t[:, :], ALU.mult, r=[pow2, w0t], w=[Wtab])
            kb.ts(dve, W2tab[:, :], Wtab[:, :], 2.0, ALU.mult, r=[Wtab], w=[W2tab])
            kb.tt(dve, mid[:, :], lo[:, :], Wtab[:, 0:1], ALU.add, r=[lo, Wtab], w=[mid])
            for it in range(NBISECT):
                kb.ts(dve, mb[:, 0:Lk], score[:, 0:Lk], mid[:, :], ALU.is_ge, r=[score, mid], w=[mb, cnt],
                      s2=None, op1=ALU.add, accum=cnt[:, :])
                if it < NBISECT - 1:
                    kb.stt(dve, stp[:, :], cnt[:, :], 255.5, W2tab[:, it + 1:it + 2], ALU.is_ge, ALU.mult, r=[cnt, W2tab], w=[stp])
                    kb.stt(dve, mid[:, :], stp[:, :], Wtab[:, it + 1:it + 2], mid[:, :], ALU.subtract, ALU.add, r=[stp, Wtab, mid], w=[mid])
                else:
                    kb.stt(dve, stp[:, :], cnt[:, :], 255.5, Wtab[:, it:it + 1], ALU.is_ge, ALU.mult, r=[cnt, Wtab], w=[stp])
                    kb.stt(dve, lo[:, :], stp[:, :], Wtab[:, it:it + 1], mid[:, :], ALU.subtract, ALU.add, r=[stp, Wtab, mid], w=[lo])
            kb.ts(dve, mb[:, 0:Lk], score[:, 0:Lk], lo[:, :], ALU.is_ge, r=[score, lo], w=[mb])

        def front_b(j):
            par = j % 2
            Lk = 256 * (j + 1)
            nblk = Lk // 128
            mb = m01
            mT = maskT[par]
            for b0 in range(0, nblk, 8):
                nb = min(8, nblk - b0)
                ps = kb.psum(); psb_ = ps[:, :].bitcast(BF16)
                for bi in range(nb):
                    kb.tr(ps, psb_[:, bi * 128:(bi + 1) * 128], mb[:, (b0 + bi) * 128:(b0 + bi + 1) * 128], C.ident[:, :], r=[mb, C.ident])
                kb.acopy(mT[:, b0:b0 + nb, :], psb_[:, 0:nb * 128].rearrange("p (b t) -> p b t", t=128), r=[ps], w=[mT])

        def back(j):
            par = j % 2
            Lk = 256 * (j + 1)
            mT = maskT[par]
            nblk = Lk // 128
            for m in range(4):
                ktb = kt_sb[cn.kt % 2]; vbuf = v_sb[cn.kt % 2]; cn.kt += 1
                kb.dma(sp, ktb[:, 0:Lk], KTs[m, :, 0:Lk], w=[ktb])
                kb.dma(sp, vbuf[:, 0:nblk, :, :], Vs[m, :, 0:nblk, :].rearrange("p b (a d) -> p b a d", d=65), w=[vbuf])
                pso = [kb.psb[6], kb.psb[7]]
                pend = []

                def emit_pv(pm, blk0, nb, pso=pso, vbuf=vbuf, nblk=nblk):
                    for bi in range(nb):
                        blk = blk0 + bi
                        for hh in range(2):
                            kb.mm(pso[hh], pso[hh][:, 0:65], pm[:, bi * 256 + hh * 128:bi * 256 + (hh + 1) * 128], vbuf[:, blk, hh, :],
                                  blk == 0, blk == nblk - 1, r=[pm, vbuf])

                for blk0 in range(0, nblk, 2):
                    nb = 2
                    ps = kb.psum()
                    for bi in range(nb):
                        c0 = (blk0 + bi) * 128
                        kb.mm(ps, ps[:, bi * 256:(bi + 1) * 256], ktb[:, c0:c0 + 128], qT[par][:, m, :, :].rearrange("p a t -> p (a t)"), True, True,
                              r=[qT[par], ktb])
                    pu = pTu[cn.pb % 3]; pm = pTm[cn.pb % 3]; cn.pb += 1
                    kb.acopy(pu[:, :], ps[:, :], r=[ps, negm[par]], w=[pu], func=AF.Exp, scale=0.125, bias=negm[par][:, :])
                    kb.tt(pool, pm[:, :].rearrange("p (b a t) -> p b a t", a=2, t=128), pu[:, :].rearrange("p (b a t) -> p b a t", a=2, t=128),
                          bc(mT[:, blk0:blk0 + nb, :].unsqueeze(2), [128, nb, 2, 128]), ALU.mult, r=[pu, mT], w=[pm])
                    pend.append((pm, blk0, nb))
                    if len(pend) > 2:
                        emit_pv(*pend.pop(0))
                while pend:
                    emit_pv(*pend.pop(0))
                for hh in range(2):
                    kb.acopy(ybu[:, 2 * m + hh, :], pso[hh][:, 0:65], r=[pso[hh]], w=[ybu])
            kb.op(dve, lambda e: e.reciprocal(out=rinv8[:, :], in_=ybu[:, :, 64]), r=[ybu], w=[rinv8])
            kb.tt(dve, yb[:, :].rearrange("p (h d) -> p h d", d=64), ybu[:, :, 0:64],
                  bc(rinv8[:, :].unsqueeze(2), [128, 8, 64]), ALU.mult, r=[ybu, rinv8], w=[yb])
            kb.dma(pool, YBs[j * 128:(j + 1) * 128, :], yb[:, :], r=[yb])

        front(0)
        front_b(0)
        for j in range(NQ):
            if j + 1 < NQ:
                front(j + 1)
            back(j)
            if j + 1 < NQ:
                front_b(j + 1)
        kb.barrier()
        kb.nrot = 8
        kb.es = es_save

    NP = NQ // 2
    with ExitStack() as p3:
        es_save = kb.es
        kb.es = p3
        Wg = kb.sb([128, KC, 2048], BF16, "Wg")
        load_cast(kb, C, Wg, lambda c0, n: Wg[:, :, c0:c0 + n],
                  lambda c0, n: w_gate.rearrange("(k p) n -> p k n", p=128)[:, :, c0:c0 + n], 2048, None, step=128)
        Worw = kb.sb([128, 4, D], BF16, "Worw"); Woat = kb.sb([128, 4, D], BF16, "Woat"); Wout = kb.sb([128, KC, D], BF16, "Wout")
        load_cast(kb, C, Worw, lambda c0, n: Worw[:, :, c0:c0 + n],
                  lambda c0, n: w_orw.rearrange("(k p) n -> p k n", p=128)[:, :, c0:c0 + n], D, None, step=256)
        load_cast(kb, C, Woat, lambda c0, n: Woat[:, :, c0:c0 + n],
                  lambda c0, n: w_oatt.rearrange("(k p) n -> p k n", p=128)[:, :, c0:c0 + n], D, None, step=256)
        load_cast(kb, C, Wout, lambda c0, n: Wout[:, :, c0:c0 + n],
                  lambda c0, n: w_out.rearrange("(k p) n -> p k n", p=128)[:, :, c0:c0 + n], D, None, step=128)
        uT2 = kb.sb([128, KC, 256], BF16, "uT2")
        gT = kb.sb([128, 16, 256], F32, "gT")
        x4 = [kb.sb([128, D], F32, "x4") for _ in range(4)]
        yab = [kb.sb([128, 512], BF16) for _ in range(4)]; ybb = [kb.sb([128, 512], BF16) for _ in range(4)]
        yaT = kb.sb([128, 4, 256], BF16); ybT = kb.sb([128, 4, 256], BF16)
        t1 = kb.sb([128, 512], F32); t2 = kb.sb([128, 512], F32)
        mT = kb.sb([128, 8, 256], BF16)
        h1 = [kb.sb([128, D], F32) for _ in range(2)]
        for jp in range(NP):
            for t in range(2):
                j = 2 * jp + t
                sl_ = (jp % 2) * 2 + t
                kb.dma(sp, x4[sl_][:, :], xown[j * 128:(j + 1) * 128, :], w=[x4[sl_]])
                kb.dma(sp, yab[sl_][:, :], YAs[j * 128:(j + 1) * 128, :], w=[yab[sl_]])
                kb.dma(sp, ybb[sl_][:, :], YBs[j * 128:(j + 1) * 128, :], w=[ybb[sl_]])
            for t in range(2):
                sl_ = (jp % 2) * 2 + t
                rms_uT(kb, C, x4[sl_], g1c, uT2, dst=uT2[:, :, t * 128:(t + 1) * 128])
                for src, dst in ((yab[sl_], yaT), (ybb[sl_], ybT)):
                    ps = kb.psum(); psb_ = ps[:, :].bitcast(BF16)
                    for m in range(4):
                        kb.tr(ps, psb_[:, m * 128:(m + 1) * 128], src[:, m * 128:(m + 1) * 128], C.ident[:, :], r=[src, C.ident])
                    kb.acopy(dst[:, :, t * 128:(t + 1) * 128], psb_[:, 0:512].rearrange("p (m t) -> p m t", t=128), r=[ps], w=[dst])
            for g0 in range(0, 16, 2):
                ps = kb.psum()
                for n_ in range(2):
                    for kc in range(KC):
                        kb.mm(ps, ps[:, n_ * 256:(n_ + 1) * 256], Wg[:, kc, (g0 + n_) * 128:(g0 + n_ + 1) * 128], uT2[:, kc, :],
                              kc == 0, kc == KC - 1, r=[Wg, uT2])
                kb.acopy(gT[:, g0:g0 + 2, :], ps[:, :].rearrange("p (m t) -> p m t", t=256), r=[ps], w=[gT], func=AF.Sigmoid)
            for q4 in range(4):
                psa = kb.psum(); psb2 = kb.psum()
                for n_ in range(2):
                    nn = q4 * 2 + n_
                    for kc in range(4):
                        kb.mm(psa, psa[:, n_ * 256:(n_ + 1) * 256], Worw[:, kc, nn * 128:(nn + 1) * 128], yaT[:, kc, :], kc == 0, kc == 3, r=[Worw, yaT])
                    for kc in range(4):
                        kb.mm(psb2, psb2[:, n_ * 256:(n_ + 1) * 256], Woat[:, kc, nn * 128:(nn + 1) * 128], ybT[:, kc, :], kc == 0, kc == 3, r=[Woat, ybT])
                kb.tt(dve, t1[:, :], psa[:, :], gT[:, q4 * 2:q4 * 2 + 2, :].rearrange("p m t -> p (m t)"), ALU.mult, r=[psa, gT], w=[t1])
                kb.tt(dve, t2[:, :], psb2[:, :], gT[:, 8 + q4 * 2:8 + q4 * 2 + 2, :].rearrange("p m t -> p (m t)"), ALU.mult, r=[psb2, gT], w=[t2])
                kb.tt(dve, mT[:, q4 * 2:q4 * 2 + 2, :].rearrange("p m t -> p (m t)"), t1[:, :], t2[:, :], ALU.add, r=[t1, t2], w=[mT])
            for t in range(2):
                j = 2 * jp + t
                sl_ = (jp % 2) * 2 + t
                h1t = h1[t]
                for half in range(2):
                    ps = kb.psum()
                    for kc in range(KC):
                        kb.mm(ps, ps[:, :], mT[:, kc, t * 128:(t + 1) * 128], Wout[:, kc, half * 512:(half + 1) * 512], kc == 0, kc == KC - 1, r=[mT, Wout])
                    kb.tt(dve, h1t[:, half * 512:(half + 1) * 512], ps[:, :], x4[sl_][:, half * 512:(half + 1) * 512], ALU.add, r=[ps, x4[sl_]], w=[h1t])
                kb.dma(pool, H1s[j * 128:(j + 1) * 128, :], h1t[:, :], r=[h1t])
        kb.barrier()
        kb.es = es_save

    with ExitStack() as p4:
        es_save = kb.es
        kb.es = p4
        Wfi = kb.sb([128, KC, 5632], BF16, "Wfi")
        load_cast(kb, C, Wfi, lambda c0, n: Wfi[:, :, c0:c0 + n],
                  lambda c0, n: w_ffi.rearrange("(k p) n -> p k n", p=128)[:, :, c0:c0 + n], 5632, None, step=128)
        Wfo = kb.sb([128, 22, D], BF16, "Wfo")
        load_cast(kb, C, Wfo, lambda c0, n: Wfo[:, :, c0:c0 + n],
                  lambda c0, n: w_ffo.rearrange("(k p) n -> p k n", p=128)[:, :, c0:c0 + n], D, None, step=32)
        gfb = kb.sb([128, D], F32); kb.dma(sp, gfb[:, :], gfrow.partition_broadcast(128), w=[gfb])
        uT2 = kb.sb([128, KC, 256], BF16, "uT2")
        aT = kb.sb([128, 22, 256], BF16, "aT")
        sl = kb.sb([128, 512], F32)
        h1 = [kb.sb([128, D], F32) for _ in range(4)]
        h2 = kb.sb([128, D], F32); ot = kb.sb([128, D], F32)
        for jp in range(NP):
            for t in range(2):
                j = 2 * jp + t
                h1t = h1[(jp % 2) * 2 + t]
                kb.dma(sp, h1t[:, :], H1s[j * 128:(j + 1) * 128, :], w=[h1t])
            for t in range(2):
                h1t = h1[(jp % 2) * 2 + t]
                rms_uT(kb, C, h1t, g2c, uT2, dst=uT2[:, :, t * 128:(t + 1) * 128])
            for g0 in range(0, 22, 2):
                psg = kb.psum(); psu = kb.psum()
                for n_ in range(2):
                    nn = g0 + n_
                    for kc in range(KC):
                        kb.mm(psg, psg[:, n_ * 256:(n_ + 1) * 256], Wfi[:, kc, nn * 128:(nn + 1) * 128], uT2[:, kc, :], kc == 0, kc == KC - 1, r=[Wfi, uT2])
                    for kc in range(KC):
                        kb.mm(psu, psu[:, n_ * 256:(n_ + 1) * 256], Wfi[:, kc, 2816 + nn * 128:2816 + (nn + 1) * 128], uT2[:, kc, :], kc == 0, kc == KC - 1, r=[Wfi, uT2])
                kb.acopy(sl[:, :], psg[:, :], r=[psg], w=[sl], func=AF.Silu)
                kb.tt(dve, aT[:, g0:g0 + 2, :].rearrange("p m t -> p (m t)"), sl[:, :], psu[:, :], ALU.mult, r=[sl, psu], w=[aT])
            for t in range(2):
                j = 2 * jp + t
                h1t = h1[(jp % 2) * 2 + t]
                for half in range(2):
                    ps = kb.psum()
                    for n_ in range(22):
                        kb.mm(ps, ps[:, :], aT[:, n_, t * 128:(t + 1) * 128], Wfo[:, n_, half * 512:(half + 1) * 512], n_ == 0, n_ == 21, r=[aT, Wfo])
                    kb.tt(dve, h2[:, half * 512:(half + 1) * 512], ps[:, :], h1t[:, half * 512:(half + 1) * 512], ALU.add, r=[ps, h1t], w=[h2])
                kb.acopy(C.junk[:, 0:D], h2[:, :], r=[h2], w=[C.junk, C.ss], func=AF.Square, accum=C.ss[:, :])
                kb.acopy(C.rt[:, :], C.ss[:, :], r=[C.ss], w=[C.rt], func=AF.Sqrt, scale=1.0 / D, bias=C.epsn[:, :])
                kb.op(dve, lambda e: e.reciprocal(out=C.rstd[:, :], in_=C.rt[:, :]), r=[C.rt], w=[C.rstd])
                kb.stt(dve, ot[:, :], h2[:, :], C.rstd[:, :], gfb[:, :], ALU.mult, ALU.mult, r=[h2, C.rstd, gfb], w=[ot])
                kb.dma(pool, out[j * 128:(j + 1) * 128, :], ot[:, :], r=[ot])
        kb.barrier()
        kb.es = es_save
    kb.es.close()
    return nc


def _col(v, n):
    return np.ascontiguousarray(np.asarray(v, np.float32).reshape(n, 128).T)


def make_inputs(S, xb, p, W):
    NT = S // 128
    f = lambda a: np.ascontiguousarray(np.asarray(a, np.float32))
    w_in = W["w_in"]
    own_tiles = xb.reshape(NT // 2, 2, 128, D)[:, p].reshape(S // 2, D)
    half = 32
    inv = 1.0 / (10000.0 ** (np.arange(half, dtype=np.float32) * 2.0 / 64)).astype(np.float32)
    pos = np.arange(S, dtype=np.float32)
    ang = (pos[:, None] * inv[None, :]).astype(np.float32)
    cs = np.concatenate([np.cos(ang), np.sin(ang)], axis=1).astype(np.float32)
    cs_own = cs.reshape(NT // 2, 2, 128, 64)[:, p].reshape(S // 2, 64)
    t = np.arange(128)
    LL = np.concatenate([(t[:, None] <= t[None, :]), (t[:, None] < t[None, :])], axis=1).astype(np.float32) * np.float32(CDEC)
    maskU = np.concatenate([(t[:, None] < t[None, :]), (t[:, None] <= t[None, :])], axis=1).astype(np.float32)
    maskL = (t[:, None] > t[None, :]).astype(np.float32)
    bones = np.kron(np.eye(2, dtype=np.float32), np.ones((64, 64), np.float32))
    bones2 = np.kron(np.eye(2, dtype=np.float32), np.ones((64, 1), np.float32))
    s = np.arange(256)
    maskadd = np.where(s[None, :] <= (128 * p + t[:, None]), 0.0, NEG).astype(np.float32)
    par = np.zeros((128, 2), np.float32); par[:, 0] = p; par[:, 1] = 1 - p
    pow2 = np.tile((0.5 ** np.arange(1, NBISECT + 1, dtype=np.float64)).astype(np.float32)[None, :], (128, 1))
    m = {
        "xfull": f(xb), "xown": f(own_tiles),
        "g1col": _col(W["norm1_g"], 8), "g2col": _col(W["norm2_g"], 8), "gfrow": f(W["normf_g"]).reshape(1, D),
        "w_rw": f(w_in[:, 0:1792]),
        "w_kv": f(np.concatenate([w_in[:, 2304:2816], w_in[:, 2816:3328], w_in[:, 3840:3904]], axis=1)),
        "w_qi": f(np.concatenate([w_in[:, 1792:2304], w_in[:, 3328:3840], w_in[:, 3904:3912]], axis=1)),
        "w_gate": f(w_in[:, 3912:5960]),
        "mucol": _col(W["tshift_mu"], 14), "wdu": f(W["w_decay_up"]), "w0row": f(W["w0"]).reshape(1, 512),
        "aup": f(W["a_up"]), "a0col": _col(W["a0"], 4), "gup": f(W["g_up"]),
        "kkcol": _col(W["k_k"], 4), "kacol": _col(W["k_a"], 4), "rkcol": _col(np.asarray(W["r_k"]).reshape(512), 4),
        "lnxg": f(W["lnx_g"]).reshape(1, 512), "lnxb": f(W["lnx_b"]).reshape(1, 512),
        "w_orw": f(W["w_o_rwkv"]), "w_oatt": f(W["w_o_att"]), "w_out": f(W["w_out"]),
        "w_ffi": f(W["w_ffn_in"]), "w_ffo": f(W["w_ffn_out"]),
        "cs_full": cs, "cs_own": np.ascontiguousarray(cs_own),
        "identd": np.eye(128, dtype=np.float32), "LLd": LL, "maskUd": maskU, "maskLd": maskL,
        "bonesd": bones, "bones2d": bones2, "maskaddd": maskadd, "pard": par, "pow2d": pow2,
    }
    return m


_NC_CACHE = {}


def kernel(**inputs):
    x = np.asarray(inputs["x"], np.float32)
    B, S, _ = x.shape
    W = {k: np.asarray(v) for k, v in inputs.items() if k != "x"}
    if S not in _NC_CACHE:
        _NC_CACHE[S] = build(S)
    nc = _NC_CACHE[S]
    in_maps = []
    for c in range(2 * B):
        b, p = c // 2, c % 2
        in_maps.append(make_inputs(S, x[b], p, W))
    res = run_bass_kernel_spmd(nc, in_maps, core_ids=list(range(2 * B)))
    outp = np.zeros((B, S, D), np.float32)
    NT = S // 128
    for c in range(2 * B):
        b, p = c // 2, c % 2
        o = np.asarray(res.results[c]["out"], np.float32).reshape(NT // 2, 128, D)
        outp[b].reshape(NT // 2, 2, 128, D)[:, p] = o
    return outp
```

```python
import math
from contextlib import ExitStack
import numpy as np
import concourse.bass as bass
import concourse.mybir as mybir
from concourse.bass_utils import run_bass_kernel_spmd

F32 = mybir.dt.float32
BF16 = mybir.dt.bfloat16
ALU = mybir.AluOpType
AF = mybir.ActivationFunctionType
AX = mybir.AxisListType

D = 1024
KC = 8
NEG = -1.0e30
EPOCH = 4000
NDSLOT = 16
NBISECT = 12
CDEC = -math.exp(-0.5)


class T:
    def __init__(self, t):
        self.t = t
        self.w = None
        self.r = {}

    def __getitem__(self, k):
        return self.t[k]


class Eng:
    def __init__(self, kb, key, obj):
        self.kb, self.key, self.obj = kb, key, obj
        self.n = 0
        self.sems = []
        self.known = {}
        self.dn = 0
        self.dsems = None
        self.dcount = [0] * NDSLOT

    def sem_for(self, seq):
        ep = (seq - 1) // EPOCH
        while len(self.sems) <= ep:
            self.sems.append(self.kb.newsem())
        return self.sems[ep], (seq - 1) % EPOCH + 1


class KB:
    def __init__(self, nc):
        self.nc = nc
        self.es = ExitStack()
        self.nsem = 0
        self.pe = Eng(self, "pe", nc.tensor)
        self.act = Eng(self, "act", nc.scalar)
        self.dve = Eng(self, "dve", nc.vector)
        self.pool = Eng(self, "pool", nc.gpsimd)
        self.sp = Eng(self, "sp", nc.sync)
        self.engs = [self.pe, self.act, self.dve, self.pool, self.sp]
        self.dmatoks = {}
        self.psb = []
        self.psi = 0
        self.nrot = 8
        self.nm = 0

    def newsem(self):
        self.nsem += 1
        return self.es.enter_context(self.nc.semaphore("s%d" % self.nsem))

    def sb(self, shape, dt=F32, name=None):
        self.nm += 1
        return T(self.es.enter_context(self.nc.sbuf_tensor("%s_%d" % (name or "t", self.nm), list(shape), dt)))

    def init_psum(self):
        for i in range(8):
            self.psb.append(T(self.es.enter_context(self.nc.psum_tensor("ps%d" % i, [128, 512], F32))))
            self.psb[-1].excl = True

    def psum(self):
        b = self.psb[self.psi % self.nrot]
        self.psi += 1
        return b

    def psum_acc(self):
        self.pai = getattr(self, "pai", 0) + 1
        return self.psb[6 + self.pai % 2]

    def resolve(self, tok):
        key, seq = tok
        if isinstance(key, tuple):
            return self.dmatoks[key], seq
        e = getattr(self, key)
        return e.sem_for(seq)

    def _wait(self, E, tok):
        key, seq = tok
        if E.known.get(key, 0) >= seq:
            return
        E.known[key] = seq
        sem, val = self.resolve(tok)
        E.obj.wait_ge(sem, val)

    def _deps(self, E, r, w):
        for b in r:
            if b.w is not None:
                self._wait(E, b.w)
            if getattr(b, "excl", False):
                for k, tok in b.r.items():
                    if k != E.key:
                        self._wait(E, tok)
        for b in w:
            if b.w is not None and b.w[0] != E.key:
                self._wait(E, b.w)
            for k, tok in b.r.items():
                if k != E.key:
                    self._wait(E, tok)

    def op(self, E, fn, r=(), w=()):
        self._deps(E, r, w)
        ins = fn(E.obj)
        E.n += 1
        sem, val = E.sem_for(E.n)
        ins.then_inc(sem, 1)
        tok = (E.key, E.n)
        for b in r:
            b.r[E.key] = tok
        for b in w:
            b.w = tok
            b.r = {}

    def dma(self, Q, out, in_, r=(), w=()):
        self._deps(Q, r, w)
        if Q.dsems is None:
            Q.dsems = [self.newsem() for _ in range(NDSLOT)]
            for i in range(NDSLOT):
                self.dmatoks[("dma", Q.key, i)] = Q.dsems[i]
        slot = Q.dn % NDSLOT
        Q.dn += 1
        if Q.dcount[slot] > 0:
            self._wait(Q, (("dma", Q.key, slot), Q.dcount[slot] * 16))
        Q.dcount[slot] += 1
        Q.obj.dma_start(out=out, in_=in_).then_inc(Q.dsems[slot], 16)
        tok = (("dma", Q.key, slot), Q.dcount[slot] * 16)
        for b in r:
            b.r[tok[0]] = tok
        for b in w:
            b.w = tok
            b.r = {}

    def barrier(self):
        toks = []
        for e in self.engs:
            if e.n > 0:
                toks.append((e.key, e.n))
            if e.dsems is not None:
                for i in range(NDSLOT):
                    if e.dcount[i] > 0:
                        toks.append((("dma", e.key, i), e.dcount[i] * 16))
        for e in self.engs:
            for tok in toks:
                if tok[0] != e.key:
                    self._wait(e, tok)

    def mm(self, ps, out, lhsT, rhs, start, stop, r):
        self.op(self.pe, lambda e: e.matmul(out, lhsT, rhs, start=start, stop=stop), r=r, w=[ps])

    def tr(self, ps, out, in_, ident, r):
        self.op(self.pe, lambda e: e.transpose(out, in_, ident), r=r, w=[ps])

    def acopy(self, out, in_, r, w, func=AF.Copy, scale=1.0, bias=None, accum=None):
        def f(e):
            kw = {}
            if bias is not None:
                kw["bias"] = bias
            if accum is not None:
                kw["accum_out"] = accum
            return e.activation(out=out, in_=in_, func=func, scale=scale, **kw)
        self.op(self.act, f, r=r, w=w)

    def tt(self, E, out, a, b, op, r, w):
        self.op(E, lambda e: e.tensor_tensor(out=out, in0=a, in1=b, op=op), r=r, w=w)

    def ts(self, E, out, a, s1, op0, r, w, s2=None, op1=None, accum=None):
        def f(e):
            kw = {}
            if accum is not None:
                kw["accum_out"] = accum
            return e.tensor_scalar(out=out, in0=a, scalar1=s1, scalar2=s2, op0=op0,
                                   op1=(op1 if op1 is not None else ALU.bypass), **kw)
        self.op(E, f, r=r, w=w)

    def stt(self, E, out, a, s, b, op0, op1, r, w):
        self.op(E, lambda e: e.scalar_tensor_tensor(out=out, in0=a, scalar=s, in1=b, op0=op0, op1=op1), r=r, w=w)


def bc(ap, shape):
    return ap.to_broadcast(list(shape))


class Ctx:
    pass


def load_cast(kb, C, dst, dst_ap_fn, src_ap_fn, ncols, rows_shape, step=2048):
    sap, dap = src_ap_fn(0, ncols), dst_ap_fn(0, ncols)
    for k in range(sap.shape[1]):
        kb.dma(kb.pool, dap[:, k, :], sap[:, k, :], w=[dst])


def rms_uT(kb, C, xt, gcol, uT, dst=None):
    kb.acopy(C.junk[:, 0:D], xt[:, :], r=[xt], w=[C.junk, C.ss], func=AF.Square, accum=C.ss[:, :])
    kb.acopy(C.rt[:, :], C.ss[:, :], r=[C.ss], w=[C.rt], func=AF.Sqrt, scale=1.0 / D, bias=C.epsn[:, :])
    kb.op(kb.dve, lambda e: e.reciprocal(out=C.rstd[:, :], in_=C.rt[:, :]), r=[C.rt], w=[C.rstd])
    kb.ts(kb.dve, C.xn[:, :], xt[:, :], C.rstd[:, :], ALU.mult, r=[xt, C.rstd], w=[C.xn])
    ps = kb.psum()
    psv = ps[:, :].bitcast(BF16)
    for kc in range(KC):
        kb.tr(ps, psv[:, kc * 128:(kc + 1) * 128], C.xn[:, kc * 128:(kc + 1) * 128], C.ident[:, :], r=[C.xn, C.ident])
    kb.tt(kb.dve, (uT[:, :, :] if dst is None else dst), psv.rearrange("p (k t) -> p k t", t=128), bc(gcol[:, :].unsqueeze(2), [128, KC, 128]),
          ALU.mult, r=[ps, gcol], w=[uT])


def rope_tm(kb, C, ps, nh, cs, out, out_t):
    pv = ps[:, 0:nh * 64].rearrange("p (h two d) -> p h two d", two=2, d=32)
    cosb = bc(cs[:, 0:1, :].unsqueeze(1), [128, nh, 2, 32])
    sinb = bc(cs[:, 1:2, :].unsqueeze(1), [128, nh, 2, 32])
    A = C.ropeA[:, 0:nh * 64].rearrange("p (h two d) -> p h two d", two=2, d=32)
    Bm = C.ropeB[:, 0:nh * 64].rearrange("p (h two d) -> p h two d", two=2, d=32)
    ov = out.rearrange("p (h two d) -> p h two d", two=2, d=32)
    kb.tt(kb.dve, A, pv, cosb, ALU.mult, r=[ps, cs], w=[C.ropeA])
    kb.tt(kb.dve, Bm, pv, sinb, ALU.mult, r=[ps, cs], w=[C.ropeB])
    kb.tt(kb.pool, ov[:, :, 0:1, :], A[:, :, 0:1, :], Bm[:, :, 1:2, :], ALU.subtract, r=[C.ropeA, C.ropeB], w=[out_t])
    kb.tt(kb.pool, ov[:, :, 1:2, :], A[:, :, 1:2, :], Bm[:, :, 0:1, :], ALU.add, r=[C.ropeA, C.ropeB], w=[out_t])


def build(S, debug=False):
    NT = S // 128
    NQ = S // 256
    SO = S // 2
    nc = bass.Bass("TRN2", target_bir_lowering=False)
    kb = KB(nc)
    C = Ctx()

    def din(name, shape, dt=F32):
        return nc.dram_tensor(name, list(shape), dt, kind="ExternalInput").ap()

    def dscr(name, shape, dt):
        return nc.dram_tensor(name, list(shape), dt, kind=("ExternalOutput" if debug else "Internal")).ap()

    xfull = din("xfull", [S, D]); xown = din("xown", [SO, D])
    g1col = din("g1col", [128, 8]); g2col = din("g2col", [128, 8]); gfrow = din("gfrow", [1, D])
    w_rw = din("w_rw", [D, 1792]); w_kv = din("w_kv", [D, 1088]); w_qi = din("w_qi", [D, 1032]); w_gate = din("w_gate", [D, 2048])
    mucol = din("mucol", [128, 14]); wdu = din("wdu", [64, 512]); w0row = din("w0row", [1, 512])
    aup = din("aup", [64, 512]); a0col = din("a0col", [128, 4]); gup = din("gup", [128, 512])
    kkcol = din("kkcol", [128, 4]); kacol = din("kacol", [128, 4]); rkcol = din("rkcol", [128, 4])
    lnxg = din("lnxg", [1, 512]); lnxb = din("lnxb", [1, 512])
    w_orw = din("w_orw", [512, D]); w_oatt = din("w_oatt", [512, D]); w_out = din("w_out", [D, D])
    w_ffi = din("w_ffi", [D, 5632]); w_ffo = din("w_ffo", [2816, D])
    cs_full = din("cs_full", [S, 64]); cs_own = din("cs_own", [SO, 64])
    identd = din("identd", [128, 128]); LLd = din("LLd", [128, 256]); maskUd = din("maskUd", [128, 256]); maskLd = din("maskLd", [128, 128])
    bonesd = din("bonesd", [128, 128]); bones2d = din("bones2d", [128, 2]); maskaddd = din("maskaddd", [128, 256])
    pard = din("pard", [128, 2]); pow2d = din("pow2d", [128, NBISECT])
    out = nc.dram_tensor("out", [SO, D], F32, kind="ExternalOutput").ap()

    KTs = dscr("KTs", [4, 128, S], BF16)
    Vs = dscr("Vs", [4, 128, NT, 130], BF16)
    KIs = dscr("KIs", [64, S], BF16)
    YAs = dscr("YAs", [SO, 512], BF16)
    YBs = dscr("YBs", [SO, 512], BF16)
    H1s = dscr("H1s", [SO, D], F32)
    YAfull = dscr("YAfull", [S, 512], F32) if debug else None

    kb.init_psum()
    sp, pe, act, dve, pool = kb.sp, kb.pe, kb.act, kb.dve, kb.pool

    C.stgi = 0
    C.junk = kb.sb([128, 1024], BF16, "junk")
    C.ss = kb.sb([128, 1], F32); C.rt = kb.sb([128, 1], F32); C.rstd = kb.sb([128, 1], F32)
    C.xn = kb.sb([128, D], BF16, "xn")
    C.epsn = kb.sb([128, 1], F32)
    C.ident = kb.sb([128, 128], BF16, "ident")
    kb.op(dve, lambda e: e.memset(C.epsn[:, :], 1e-6), w=[C.epsn])
    C.kmaxb = kb.sb([128, 1], F32)

    def ld_const(dst, src_ap, shape, dt):
        if dt == F32:
            kb.dma(sp, dst[:, :], src_ap, w=[dst])
        else:
            kb.dma(pool, dst[:, :], src_ap, w=[dst])

    g1c = kb.sb([128, 8], F32); g2c = kb.sb([128, 8], F32); par = kb.sb([128, 2], F32)
    xt = [kb.sb([128, D], F32, "xt") for _ in range(2)]
    uT = kb.sb([128, KC, 128], BF16, "uT")
    STG = lambda: [kb.sb([128, 1024], F32, "stg") for _ in range(2)]
    with ExitStack() as t0:
        es_save = kb.es
        kb.es = t0
        ld_const(C.ident, identd[:, :], [128, 128], BF16)
        ld_const(g1c, g1col[:, :], [128, 8], F32)
        ld_const(g2c, g2col[:, :], [128, 8], F32)
        ld_const(par, pard[:, :], [128, 2], F32)
        kb.barrier()
        kb.es = es_save

    with ExitStack() as p1:
        es_save = kb.es
        kb.es = p1
        C.ropeA = kb.sb([128, 512], F32); C.ropeB = kb.sb([128, 512], F32)
        Wrw = kb.sb([128, KC, 1792], BF16, "Wrw")
        Wkv = kb.sb([128, KC, 1088], BF16, "Wkv")
        load_cast(kb, C, Wrw, lambda c0, n: Wrw[:, :, c0:c0 + n],
                  lambda c0, n: w_rw.rearrange("(k p) n -> p k n", p=128)[:, :, c0:c0 + n], 1792, None, step=128)
        load_cast(kb, C, Wkv, lambda c0, n: Wkv[:, :, c0:c0 + n],
                  lambda c0, n: w_kv.rearrange("(k p) n -> p k n", p=128)[:, :, c0:c0 + n], 1088, None, step=128)
        muc = kb.sb([128, 14], F32); ld_const(muc, mucol[:, :], [128, 14], F32)
        wdus = kb.sb([64, 512], BF16); ld_const(wdus, wdu[:, :], [64, 512], BF16)
        aups = kb.sb([128, 512], BF16)
        kb.dma(pool, aups[64:128, :], aup[:, :], w=[aups])
        gups = kb.sb([128, 512], BF16); ld_const(gups, gup[:, :], [128, 512], BF16)
        w0b = kb.sb([128, 512], F32); kb.dma(sp, w0b[:, :], w0row.partition_broadcast(128), w=[w0b])
        lgb = kb.sb([128, 512], F32); kb.dma(sp, lgb[:, :], lnxg.partition_broadcast(128), w=[lgb])
        lbb = kb.sb([128, 512], F32); kb.dma(sp, lbb[:, :], lnxb.partition_broadcast(128), w=[lbb])
        a0c = kb.sb([128, 4], F32); ld_const(a0c, a0col[:, :], [128, 4], F32)
        kkc = kb.sb([128, 4], F32); ld_const(kkc, kkcol[:, :], [128, 4], F32)
        kac = kb.sb([128, 4], F32); ld_const(kac, kacol[:, :], [128, 4], F32)
        rkc = kb.sb([128, 4], F32); ld_const(rkc, rkcol[:, :], [128, 4], F32)
        LL = kb.sb([128, 256], F32); ld_const(LL, LLd[:, :], [128, 256], F32)
        maskU = kb.sb([128, 256], F32); ld_const(maskU, maskUd[:, :], [128, 256], F32)
        maskL = kb.sb([128, 128], F32); ld_const(maskL, maskLd[:, :], [128, 128], F32)
        bones = kb.sb([128, 128], F32); ld_const(bones, bonesd[:, :], [128, 128], F32)
        bones2 = kb.sb([128, 2], BF16); ld_const(bones2, bones2d[:, :], [128, 2], BF16)

        pbuf = kb.sb([128, 14, 257], F32, "pbuf")
        kb.op(pool, lambda e: e.memset(pbuf[:, :, :], 0.0), w=[pbuf])
        psh = kb.sb([128, 14, 256], F32, "psh")
        uT2 = kb.sb([128, KC, 256], BF16, "uT2")
        x4 = xt + [kb.sb([128, D], F32, "x4") for _ in range(2)]
        wa_bf = kb.sb([128, 128], BF16); sg_bf = kb.sb([128, 128], BF16)
        zT = kb.sb([128, 512], F32); sigT = kb.sb([128, 512], F32)
        Epe = kb.sb([128, 4, 2, 128], F32); Eneg = kb.sb([128, 4, 128], F32)
        a_sb = kb.sb([128, 4, 128], F32); g_sb = kb.sb([128, 512], F32)
        kk = kb.sb([128, 4, 128], F32); kk2 = kb.sb([128, 4, 128], F32); sq = kb.sb([128, 512], F32)
        kap = kb.sb([128, 4, 128], F32); tmpa = kb.sb([128, 4, 128], F32); kmod = kb.sb([128, 4, 128], F32)
        bb = kb.sb([128, 4, 128], F32); rkr0 = kb.sb([128, 4, 128], F32)
        krt = kb.sb([128, 4, 2, 128], BF16); ktl = kb.sb([128, 4, 128], BF16); btl = kb.sb([128, 4, 128], BF16)
        rkr = kb.sb([128, 4, 128], BF16); v_bf = kb.sb([128, 4, 128], BF16)
        Vtm = kb.sb([128, 512], BF16); Ktm = kb.sb([128, 512], BF16); Btm = kb.sb([128, 512], BF16)
        s_sb = kb.sb([128, 8], F32)
        Tst = kb.sb([128, 4, 64], F32, "Tst"); Tbf = kb.sb([128, 4, 64], BF16, "Tbf")
        kb.op(pool, lambda e: e.memset(Tst[:, :, :], 0.0), w=[Tst])
        kb.op(pool, lambda e: e.memset(Tbf[:, :, :], 0.0), w=[Tbf])
        Ttmp = kb.sb([128, 4, 64], F32)
        Hh = []
        for h in range(8):
            o = Ctx()
            o.Akr = kb.sb([128, 256], BF16); o.Arb = kb.sb([128, 128], BF16)
            o.T3 = [kb.sb([128, 384], BF16) for _ in range(2)]
            o.X = kb.sb([128, 64], BF16)
            Hh.append(o)
        Uneg = kb.sb([128, 512], BF16)
        Yall = kb.sb([128, 512], F32); Ysq = kb.sb([128, 512], F32)
        st8 = [kb.sb([128, 8], F32) for _ in range(6)]
        Yn = kb.sb([128, 512], F32); tmpY = kb.sb([128, 512], F32)
        ya = [kb.sb([128, 512], F32) for _ in range(2)]
        ya_sel = kb.sb([128, 512], BF16)
        cs = [kb.sb([128, 2, 32], F32) for _ in range(2)]
        kr_bf = kb.sb([128, 512], BF16); krT = kb.sb([128, 4, 128], BF16); vb = kb.sb([128, 8, 65], BF16)
        kb.op(pool, lambda e: e.memset(vb[:, :, 64:65], 1.0), w=[vb])
        ki_bf = kb.sb([128, 64], BF16); kiT = kb.sb([64, 128], BF16)

        identf = kb.sb([128, 128], F32); ld_const(identf, identd[:, :], [128, 128], F32)
        ones_r = kb.sb([1, 128], F32)
        kb.op(pool, lambda e: e.memset(ones_r[:, :], 1.0), w=[ones_r])
        kmx = kb.sb([128, 1], F32); k8 = kb.sb([128, 8], F32); k1 = kb.sb([128, 1], F32)
        kb.op(pool, lambda e: e.memset(kmx[:, :], 0.0), w=[kmx])
        for i in range(NT):
            cst = cs[i % 2]
            tsl = slice((i % 2) * 128, (i % 2) * 128 + 128)
            kb.dma(sp, cst[:, :, :], cs_full[i * 128:(i + 1) * 128, :].rearrange("p (a d) -> p a d", d=32), w=[cst])
            if i % 2 == 0:
                for t in range(2):
                    x_t = x4[((i // 2) % 2) * 2 + t]
                    kb.dma(sp, x_t[:, :], xfull[(i + t) * 128:(i + t + 1) * 128, :], w=[x_t])
                for t in range(2):
                    x_t = x4[((i // 2) % 2) * 2 + t]
                    rms_uT(kb, C, x_t, g1c, uT2, dst=uT2[:, :, t * 128:(t + 1) * 128])
                for g0 in range(0, 14, 2):
                    ps = kb.psum()
                    for m in range(g0, g0 + 2):
                        for kc in range(KC):
                            kb.mm(ps, ps[:, (m - g0) * 256:(m - g0 + 1) * 256], Wrw[:, kc, m * 128:(m + 1) * 128], uT2[:, kc, :],
                                  kc == 0, kc == KC - 1, r=[Wrw, uT2])
                    kb.acopy(pbuf[:, g0:g0 + 2, 1:257], ps[:, :].rearrange("p (m t) -> p m t", t=256), r=[ps], w=[pbuf])
                kb.tt(dve, psh[:, :, :], pbuf[:, :, 0:256], pbuf[:, :, 1:257], ALU.subtract, r=[pbuf], w=[psh])
                kb.tt(dve, psh[:, :, :], psh[:, :, :], bc(muc[:, :].unsqueeze(2), [128, 14, 256]), ALU.mult, r=[psh, muc], w=[psh])
                kb.tt(dve, psh[:, :, :], psh[:, :, :], pbuf[:, :, 1:257], ALU.add, r=[psh, pbuf], w=[psh])
                kb.op(pool, lambda e: e.tensor_copy(out=pbuf[:, :, 0:1], in_=pbuf[:, :, 256:257]), r=[pbuf], w=[pbuf])
            r_ = psh[:, 0:4, tsl]; k_ = psh[:, 4:8, tsl]; v_ = psh[:, 8:12, tsl]
            kb.acopy(wa_bf[0:64, :], psh[0:64, 12, tsl], r=[psh], w=[wa_bf], func=AF.Tanh)
            kb.acopy(wa_bf[64:128, :], psh[64:128, 12, tsl], r=[psh], w=[wa_bf])
            kb.acopy(sg_bf[:, :], psh[:, 13, tsl], r=[psh], w=[sg_bf], func=AF.Sigmoid)
            psk = kb.psum(); psv_ = kb.psum(); psi = kb.psum()
            for (pst, c0, n) in ((psk, 0, 512), (psv_, 512, 512), (psi, 1024, 64)):
                for kc in range(KC):
                    kb.mm(pst, pst[:, 0:n], uT2[:, kc, tsl], Wkv[:, kc, c0:c0 + n], kc == 0, kc == KC - 1, r=[uT2, Wkv])
            kb.acopy(vb[:, :, 0:64], psv_[:, :].rearrange("p (h d) -> p h d", d=64), r=[psv_], w=[vb])
            kb.dma(pool, Vs[:, :, i, :].rearrange("m p f -> p m f"), vb[:, :, :].rearrange("p (m a) d -> p m (a d)", a=2), r=[vb])
            rope_tm(kb, C, psk, 8, cst, kr_bf[:, :], kr_bf)
            kb.tt(pool, tmpY[:, :], kr_bf[:, :], kr_bf[:, :], ALU.mult, r=[kr_bf], w=[tmpY])
            kb.op(dve, lambda e: e.tensor_reduce(out=k8[:, :], in_=tmpY[:, :].rearrange("p (h d) -> p h d", d=64), axis=AX.X, op=ALU.add), r=[tmpY], w=[k8])
            kb.op(dve, lambda e: e.tensor_reduce(out=k1[:, :], in_=k8[:, :], axis=AX.X, op=ALU.max), r=[k8], w=[k1])
            kb.tt(dve, kmx[:, :], kmx[:, :], k1[:, :], ALU.max, r=[kmx, k1], w=[kmx])
            ps = kb.psum(); psb_ = ps[:, :].bitcast(BF16)
            for m in range(4):
                kb.tr(ps, psb_[:, m * 128:(m + 1) * 128], kr_bf[:, m * 128:(m + 1) * 128], C.ident[:, :], r=[kr_bf, C.ident])
            kb.acopy(krT[:, :, :], psb_[:, 0:512].rearrange("p (m t) -> p m t", t=128), r=[ps], w=[krT])
            kb.dma(pool, KTs[:, :, i * 128:(i + 1) * 128].rearrange("m p s -> p m s"), krT[:, :, :], r=[krT])
            rope_tm(kb, C, psi, 1, cst, ki_bf[:, :], ki_bf)
            ps = kb.psum(); psb_ = ps[:, :].bitcast(BF16)
            kb.tr(ps, psb_[0:64, 0:128], ki_bf[:, 0:64], C.ident[:, :], r=[ki_bf, C.ident])
            kb.acopy(kiT[:, :], psb_[0:64, 0:128], r=[ps], w=[kiT])
            kb.dma(pool, KIs[:, i * 128:(i + 1) * 128], kiT[:, :], r=[kiT])
            ps = kb.psum()
            for m in range(4):
                kb.mm(ps, ps[:, m * 128:(m + 1) * 128], aups[64:128, m * 128:(m + 1) * 128], wa_bf[64:128, :], True, True, r=[aups, wa_bf])
            for m in range(4):
                kb.acopy(a_sb[:, m, :], ps[:, m * 128:(m + 1) * 128], r=[ps, a0c], w=[a_sb], func=AF.Sigmoid, bias=a0c[:, m:m + 1])
            ps = kb.psum()
            kb.mm(ps, ps[:, :], wa_bf[0:64, :], wdus[:, :], True, True, r=[wa_bf, wdus])
            kb.tt(dve, zT[:, :], ps[:, :], w0b[:, :], ALU.add, r=[ps, w0b], w=[zT])
            kb.acopy(sigT[:, :], zT[:, :], r=[zT], w=[sigT], func=AF.Sigmoid)
            for half in range(2):
                ps = kb.psum()
                for mm_ in range(2):
                    m = half * 2 + mm_
                    kb.mm(ps, ps[:, mm_ * 256:(mm_ + 1) * 256], sigT[:, m * 128:(m + 1) * 128], LL[:, :], True, True, r=[sigT, LL])
                pv = ps[:, :].rearrange("p (m a t) -> p m a t", a=2, t=128)
                kb.acopy(Epe[:, half * 2:half * 2 + 2, :, :], pv, r=[ps], w=[Epe], func=AF.Exp)
                kb.acopy(Eneg[:, half * 2:half * 2 + 2, :], pv[:, :, 0, :], r=[ps], w=[Eneg], func=AF.Exp, scale=-1.0)
            ps = kb.psum()
            kb.mm(ps, ps[:, :], sg_bf[:, :], gups[:, :], True, True, r=[sg_bf, gups])
            kb.acopy(g_sb[:, :], ps[:, :], r=[ps], w=[g_sb])
            kb.tt(dve, kk[:, :, :], k_, bc(kkc[:, :].unsqueeze(2), [128, 4, 128]), ALU.mult, r=[psh, kkc], w=[kk])
            kb.tt(dve, kk2[:, :, :], kk[:, :, :], kk[:, :, :], ALU.mult, r=[kk], w=[kk2])
            ps = kb.psum()
            for m in range(4):
                kb.mm(ps, ps[:, m * 128:(m + 1) * 128], bones[:, :], kk2[:, m, :], True, True, r=[bones, kk2])
            kb.acopy(sq[:, :], ps[:, :], r=[ps], w=[sq], func=AF.Sqrt)
            kb.ts(dve, sq[:, :], sq[:, :], 1e-12, ALU.max, r=[sq], w=[sq])
            kb.op(dve, lambda e: e.reciprocal(out=sq[:, :], in_=sq[:, :]), r=[sq], w=[sq])
            kb.tt(dve, kap[:, :, :], kk[:, :, :], sq[:, :].rearrange("p (m t) -> p m t", t=128), ALU.mult, r=[kk, sq], w=[kap])
            kb.stt(dve, tmpa[:, :, :], a_sb[:, :, :], -1.0, bc(kac[:, :].unsqueeze(2), [128, 4, 128]), ALU.add, ALU.mult, r=[a_sb, kac], w=[tmpa])
            kb.stt(dve, kmod[:, :, :], tmpa[:, :, :], 1.0, k_, ALU.add, ALU.mult, r=[tmpa, psh], w=[kmod])
            kb.tt(dve, bb[:, :, :], kap[:, :, :], a_sb[:, :, :], ALU.mult, r=[kap, a_sb], w=[bb])
            kb.tt(dve, krt[:, :, 1, :], r_, Epe[:, :, 0, :], ALU.mult, r=[psh, Epe], w=[krt])
            kb.tt(dve, krt[:, :, 0, :], kap[:, :, :], Epe[:, :, 1, :], ALU.mult, r=[kap, Epe], w=[krt])
            kb.tt(dve, ktl[:, :, :], kmod[:, :, :], Eneg[:, :, :], ALU.mult, r=[kmod, Eneg], w=[ktl])
            kb.tt(dve, btl[:, :, :], bb[:, :, :], Eneg[:, :, :], ALU.mult, r=[bb, Eneg], w=[btl])
            kb.acopy(v_bf[:, :, :], v_, r=[psh], w=[v_bf])
            for src, dst in ((v_bf, Vtm), (ktl, Ktm), (btl, Btm)):
                ps = kb.psum()
                psv = ps[:, :].bitcast(BF16)
                for m in range(4):
                    kb.tr(ps, psv[:, m * 128:(m + 1) * 128], src[:, m, :], C.ident[:, :], r=[src, C.ident])
                kb.acopy(dst[:, :], psv[:, 0:512], r=[ps], w=[dst])
            hp = lambda h: (h // 2, slice((h % 2) * 64, (h % 2) * 64 + 64))
            for h in range(8):
                m, P = hp(h); o = Hh[h]
                psA = kb.psum()
                kb.mm(psA, psA[:, 0:256], ktl[P, m, :], krt[P, m, :, :].rearrange("p a t -> p (a t)"), True, True, r=[ktl, krt])
                kb.tt(dve, o.Akr[:, :], psA[:, 0:256], maskU[:, :], ALU.mult, r=[psA, maskU], w=[o.Akr])
                psB = kb.psum()
                kb.mm(psB, psB[:, 0:256], btl[P, m, :], krt[P, m, :, :].rearrange("p a t -> p (a t)"), True, True, r=[btl, krt])
                kb.mm(psB, psB[:, 256:384], krt[P, m, 0, :], btl[P, m, :], True, True, r=[btl, krt])
                kb.stt(dve, o.T3[0][:, 128:256], psB[:, 0:128], -1.0, maskU[:, 0:128], ALU.mult, ALU.mult, r=[psB, maskU], w=[o.T3[0]])
                kb.stt(dve, o.T3[0][:, 256:384], psB[:, 256:384], -1.0, maskL[:, :], ALU.mult, ALU.mult, r=[psB, maskL], w=[o.T3[0]])
                kb.tt(dve, o.Arb[:, :], psB[:, 128:256], maskU[:, 128:256], ALU.mult, r=[psB, maskU], w=[o.Arb])
            for lev in range(7):
                for h in range(8):
                    o = Hh[h]
                    Tp = o.T3[lev % 2]; Tn = o.T3[(lev + 1) % 2]
                    ps = kb.psum()
                    if lev == 0:
                        kb.mm(ps, ps[:, 128:256], Tp[:, 256:384], Tp[:, 128:256], True, True, r=[Tp])
                        kb.mm(ps, ps[:, 256:384], Tp[:, 128:256], Tp[:, 256:384], True, True, r=[Tp])
                        kb.acopy(Tn[:, 128:384], ps[:, 128:384], r=[ps], w=[Tn])
                        kb.tt(dve, Tn[:, 0:128], Tp[:, 128:256], C.ident[:, :], ALU.add, r=[Tp, C.ident], w=[Tn])
                        continue
                    if lev < 6:
                        kb.mm(ps, ps[:, 0:256], Tp[:, 256:384], Tp[:, 0:256], True, True, r=[Tp])
                        kb.mm(ps, ps[:, 256:384], Tp[:, 128:256], Tp[:, 256:384], True, True, r=[Tp])
                        kb.acopy(Tn[:, 128:384], ps[:, 128:384], r=[ps], w=[Tn])
                    else:
                        kb.mm(ps, ps[:, 0:128], Tp[:, 256:384], Tp[:, 0:128], True, True, r=[Tp])
                    kb.tt(dve, Tn[:, 0:128], ps[:, 0:128], Tp[:, 0:128], ALU.add, r=[ps, Tp], w=[Tn])
            for h in range(8):
                m, P = hp(h); o = Hh[h]
                hs = slice(h * 64, (h + 1) * 64)
                ps = kb.psum()
                kb.mm(ps, ps[:, 0:64], o.Akr[:, 0:128], Vtm[:, hs], True, False, r=[o.Akr, Vtm])
                kb.mm(ps, ps[:, 0:64], krt[P, m, 0, :], Tbf[P, m, :], False, True, r=[krt, Tbf])
                kb.acopy(o.X[:, :], ps[:, 0:64], r=[ps], w=[o.X])
            for h in range(8):
                m, P = hp(h); o = Hh[h]
                Wf = o.T3[1]
                hs = slice(h * 64, (h + 1) * 64)
                ps = kb.psum()
                kb.mm(ps, ps[:, 0:64], Wf[:, 0:128], o.X[:, :], True, True, r=[Wf, o.X])
                kb.acopy(Uneg[:, hs], ps[:, 0:64], r=[ps], w=[Uneg], scale=-1.0)
            for h in range(8):
                m, P = hp(h); o = Hh[h]
                hs = slice(h * 64, (h + 1) * 64)
                ps = kb.psum()
                kb.mm(ps, ps[:, 0:64], o.Akr[:, 128:256], Vtm[:, hs], True, False, r=[o.Akr, Vtm])
                kb.mm(ps, ps[:, 0:64], krt[P, m, 1, :], Tbf[P, m, :], False, False, r=[krt, Tbf])
                kb.mm(ps, ps[:, 0:64], o.Arb[:, :], Uneg[:, hs], False, True, r=[o.Arb, Uneg])
                kb.acopy(Yall[:, hs], ps[:, 0:64], r=[ps], w=[Yall])
            for m in range(4):
                ms = slice(m * 128, (m + 1) * 128)
                ps = kb.psum()
                kb.mm(ps, ps[:, 0:128], Ktm[:, ms], Vtm[:, ms], True, False, r=[Ktm, Vtm])
                kb.mm(ps, ps[:, 0:128], Btm[:, ms], Uneg[:, ms], False, True, r=[Btm, Uneg])
                for hh in range(2):
                    P = slice(hh * 64, hh * 64 + 64)
                    kb.tt(dve, Ttmp[P, m, :], ps[P, hh * 64:hh * 64 + 64], Tst[P, m, :], ALU.add, r=[ps, Tst], w=[Ttmp])
                kb.ts(dve, Tst[:, m, :], Ttmp[:, m, :], Epe[:, m, 0, 127:128], ALU.mult, r=[Ttmp, Epe], w=[Tst])
                kb.op(pool, lambda e, m=m: e.tensor_copy(out=Tbf[:, m, :], in_=Tst[:, m, :]), r=[Tst], w=[Tbf])
            kb.tt(pool, rkr0[:, :, :], r_, kmod[:, :, :], ALU.mult, r=[psh, kmod], w=[rkr0])
            kb.tt(pool, rkr[:, :, :], rkr0[:, :, :], bc(rkc[:, :].unsqueeze(2), [128, 4, 128]), ALU.mult, r=[rkr0, rkc], w=[rkr])
            ps = kb.psum()
            for m in range(4):
                kb.mm(ps, ps[:, 2 * m:2 * m + 2], rkr[:, m, :], bones2[:, :], True, True, r=[rkr, bones2])
            kb.acopy(s_sb[:, :], ps[:, 0:8], r=[ps], w=[s_sb])
            Y3 = Yall[:, :].rearrange("p (h d) -> p h d", d=64)
            sm, sqs, mean, var, rstd8, msq = st8
            kb.op(dve, lambda e: e.tensor_reduce(out=sm[:, :], in_=Y3, axis=AX.X, op=ALU.add), r=[Yall], w=[sm])
            kb.tt(pool, Ysq[:, :], Yall[:, :], Yall[:, :], ALU.mult, r=[Yall], w=[Ysq])
            kb.op(dve, lambda e: e.tensor_reduce(out=sqs[:, :], in_=Ysq[:, :].rearrange("p (h d) -> p h d", d=64), axis=AX.X, op=ALU.add), r=[Ysq], w=[sqs])
            kb.ts(dve, mean[:, :], sm[:, :], 1.0 / 64, ALU.mult, r=[sm], w=[mean])
            kb.tt(dve, msq[:, :], mean[:, :], mean[:, :], ALU.mult, r=[mean], w=[msq])
            kb.stt(dve, var[:, :], sqs[:, :], 1.0 / 64, msq[:, :], ALU.mult, ALU.subtract, r=[sqs, msq], w=[var])
            kb.ts(dve, var[:, :], var[:, :], 64e-5, ALU.add, r=[var], w=[var])
            kb.acopy(rstd8[:, :], var[:, :], r=[var], w=[rstd8], func=AF.Sqrt)
            kb.op(dve, lambda e: e.reciprocal(out=rstd8[:, :], in_=rstd8[:, :]), r=[rstd8], w=[rstd8])
            Yn3 = Yn[:, :].rearrange("p (h d) -> p h d", d=64)
            kb.tt(dve, Yn3, Y3, bc(mean[:, :].unsqueeze(2), [128, 8, 64]), ALU.subtract, r=[Yall, mean], w=[Yn])
            kb.tt(dve, Yn3, Yn3, bc(rstd8[:, :].unsqueeze(2), [128, 8, 64]), ALU.mult, r=[Yn, rstd8], w=[Yn])
            kb.tt(pool, Yn[:, :], Yn[:, :], lgb[:, :], ALU.mult, r=[Yn, lgb], w=[Yn])
            kb.tt(pool, Yn[:, :], Yn[:, :], lbb[:, :], ALU.add, r=[Yn, lbb], w=[Yn])
            kb.tt(dve, tmpY[:, :].rearrange("p (h d) -> p h d", d=64), Vtm[:, :].rearrange("p (h d) -> p h d", d=64),
                  bc(s_sb[:, :].unsqueeze(2), [128, 8, 64]), ALU.mult, r=[Vtm, s_sb], w=[tmpY])
            kb.tt(dve, Yn[:, :], Yn[:, :], tmpY[:, :], ALU.add, r=[Yn, tmpY], w=[Yn])
            yat = ya[i % 2]
            kb.tt(dve, yat[:, :], Yn[:, :], g_sb[:, :], ALU.mult, r=[Yn, g_sb], w=[yat])
            if debug:
                kb.dma(pool, YAfull[i * 128:(i + 1) * 128, :], yat[:, :], r=[yat])
            if i % 2 == 1:
                kb.ts(dve, tmpY[:, :], ya[0][:, :], par[:, 1:2], ALU.mult, r=[ya[0], par], w=[tmpY])
                kb.stt(dve, ya_sel[:, :], ya[1][:, :], par[:, 0:1], tmpY[:, :], ALU.mult, ALU.add, r=[ya[1], par, tmpY], w=[ya_sel])
                kb.dma(pool, YAs[(i // 2) * 128:(i // 2 + 1) * 128, :], ya_sel[:, :], r=[ya_sel])
        ps = kb.psum()
        kb.tr(ps, ps[0:1, 0:128], kmx[:, 0:1], identf[:, :], r=[kmx, identf])
        kb.op(dve, lambda e: e.tensor_reduce(out=k1[0:1, :], in_=ps[0:1, 0:128], axis=AX.X, op=ALU.max), r=[ps], w=[k1])
        ps2 = kb.psum()
        kb.mm(ps2, ps2[:, 0:1], ones_r[:, :], k1[0:1, :], True, True, r=[ones_r, k1])
        kb.ts(dve, C.kmaxb[:, :], ps2[:, 0:1], 1.02, ALU.mult, r=[ps2], w=[C.kmaxb])
        kb.barrier()
        kb.es = es_save

    with ExitStack() as p2:
        es_save = kb.es
        kb.es = p2
        kb.nrot = 6
        C.ropeA = kb.sb([128, 512], F32); C.ropeB = kb.sb([128, 512], F32)
        Wqi = kb.sb([128, KC, 1032], BF16, "Wqi")
        maskadd = kb.sb([128, 256], F32); pow2 = kb.sb([128, NBISECT], F32)
        with ExitStack() as t2:
            kb.es = t2
            load_cast(kb, C, Wqi, lambda c0, n: Wqi[:, :, c0:c0 + n],
                      lambda c0, n: w_qi.rearrange("(k p) n -> p k n", p=128)[:, :, c0:c0 + n], 1032, None, step=128)
            ld_const(maskadd, maskaddd[:, :], [128, 256], F32)
            ld_const(pow2, pow2d[:, :], [128, NBISECT], F32)
            kb.barrier()
            kb.es = p2
        score = kb.sb([128, S], F32, "score")
        maskT = [kb.sb([128, NT, 128], BF16, "maskT")] * 2
        ki_sb = kb.sb([128, S], BF16, "ki_sb")
        m01 = kb.sb([128, S], BF16, "m01")
        pTu = [kb.sb([128, 512], BF16, "pTu") for _ in range(3)]
        pTm = [kb.sb([128, 512], BF16, "pTm") for _ in range(3)]
        kt_sb = [kb.sb([128, S], BF16, "kt_sb") for _ in range(2)]
        v_sb = [kb.sb([128, NT, 2, 65], BF16, "v_sb") for _ in range(2)]
        cso = [kb.sb([128, 2, 32], F32) for _ in range(2)]
        q_bf = kb.sb([128, 512], BF16); qi_bf = kb.sb([128, 512], BF16)
        qT = [kb.sb([128, 4, 2, 128], BF16) for _ in range(2)]; qiT = kb.sb([128, 4, 128], BF16)
        for qt_ in qT:
            kb.op(pool, lambda e, qt_=qt_: e.memset(qt_[:, :, :, :], 0.0), w=[qt_])
        w_sb = kb.sb([128, 8], F32)
        rl = [kb.sb([128, 512], F32) for _ in range(2)] + [C.ropeB]
        acc2 = [kb.sb([128, 512], F32)] * 2
        ptmp = [kb.sb([128, 512], F32)] * 2
        Bt = kb.sb([128, 1], F32); lo = kb.sb([128, 1], F32); mid = kb.sb([128, 1], F32); cnt = kb.sb([128, 1], F32)
        stp = kb.sb([128, 1], F32); Wtab = kb.sb([128, NBISECT], F32); W2tab = kb.sb([128, NBISECT], F32); w0t = kb.sb([128, 1], F32)
        q8 = kb.sb([128, 8], F32); q1 = kb.sb([128, 1], F32); mq = kb.sb([128, 1], F32); qg = kb.sb([1, 1], F32)
        negm = [kb.sb([128, 1], F32) for _ in range(2)]
        identf = kb.sb([128, 128], F32); kb.dma(sp, identf[:, :], identd[:, :], w=[identf])
        ones_r = kb.sb([1, 128], F32)
        kb.op(pool, lambda e: e.memset(ones_r[:, :], 1.0), w=[ones_r])
        rinv8 = kb.sb([128, 8], F32)
        ybu = kb.sb([128, 8, 65], F32)
        yb = kb.sb([128, 512], BF16)
        cn = Ctx(); cn.kt = 0; cn.rl = 0; cn.pb = 0; cn.pt = 0; cn.h = 0

        def front(j):
            par = j % 2
            Lk = 256 * (j + 1)
            nblk = Lk // 128
            x_t = xt[j % 2]; cst = cso[par]
            kb.dma(sp, x_t[:, :], xown[j * 128:(j + 1) * 128, :], w=[x_t])
            kb.dma(sp, cst[:, :, :], cs_own[j * 128:(j + 1) * 128, :].rearrange("p (a d) -> p a d", d=32), w=[cst])
            kb.dma(sp, ki_sb[0:64, 0:Lk], KIs[:, 0:Lk], w=[ki_sb])
            kb.dma(sp, ki_sb[64:128, 0:Lk], KIs[:, 0:Lk], w=[ki_sb])
            rms_uT(kb, C, x_t, g1c, uT)
            psq = kb.psum(); psqi = kb.psum(); psw = kb.psum()
            for (pst, c0, n) in ((psq, 0, 512), (psqi, 512, 512), (psw, 1024, 8)):
                for kc in range(KC):
                    kb.mm(pst, pst[:, 0:n], uT[:, kc, :], Wqi[:, kc, c0:c0 + n], kc == 0, kc == KC - 1, r=[uT, Wqi])
            kb.acopy(w_sb[:, :], psw[:, 0:8], r=[psw], w=[w_sb])
            kb.acopy(C.ropeA[:, :], psq[:, :], r=[psq], w=[C.ropeA], func=AF.Square)
            kb.op(dve, lambda e: e.tensor_reduce(out=q8[:, :], in_=C.ropeA[:, :].rearrange("p (h d) -> p h d", d=64), axis=AX.X, op=ALU.add), r=[C.ropeA], w=[q8])
            kb.op(dve, lambda e: e.tensor_reduce(out=q1[:, :], in_=q8[:, :], axis=AX.X, op=ALU.max), r=[q8], w=[q1])
            pst_ = kb.psum()
            kb.tr(pst_, pst_[0:1, 0:128], q1[:, 0:1], identf[:, :], r=[q1, identf])
            kb.op(dve, lambda e, pst_=pst_: e.tensor_reduce(out=qg[:, :], in_=pst_[0:1, 0:128], axis=AX.X, op=ALU.max), r=[pst_], w=[qg])
            psg_ = kb.psum()
            kb.mm(psg_, psg_[:, 0:1], ones_r[:, :], qg[:, :], True, True, r=[ones_r, qg])
            kb.tt(dve, q1[:, :], psg_[:, 0:1], C.kmaxb[:, :], ALU.mult, r=[psg_, C.kmaxb], w=[q1])
            kb.acopy(mq[:, :], q1[:, :], r=[q1], w=[mq], func=AF.Sqrt, scale=1.0 / 64)
            kb.ts(dve, negm[par][:, :], mq[:, :], -1.0, ALU.mult, r=[mq], w=[negm[par]])
            rope_tm(kb, C, psq, 8, cst, q_bf[:, :], q_bf)
            rope_tm(kb, C, psqi, 8, cst, qi_bf[:, :], qi_bf)
            for src, dst in ((q_bf, qT[par]), (qi_bf, qiT)):
                ps = kb.psum(); psb_ = ps[:, :].bitcast(BF16)
                for m in range(4):
                    kb.tr(ps, psb_[:, m * 128:(m + 1) * 128], src[:, m * 128:(m + 1) * 128], C.ident[:, :], r=[src, C.ident])
                if dst is qiT:
                    kb.acopy(dst[:, :, :], psb_[:, 0:512].rearrange("p (m t) -> p m t", t=128), r=[ps], w=[dst])
                else:
                    kb.acopy(dst[0:64, :, 0, :], psb_[0:64, 0:512].rearrange("p (m t) -> p m t", t=128), r=[ps], w=[dst])
                    kb.acopy(dst[64:128, :, 1, :], psb_[64:128, 0:512].rearrange("p (m t) -> p m t", t=128), r=[ps], w=[dst])
            for ci, c0 in enumerate(range(0, Lk, 512)):
                n = min(512, Lk - c0)
                a2 = acc2[ci % 2]
                for h in range(8):
                    m, P = h // 2, slice((h % 2) * 64, (h % 2) * 64 + 64)
                    ps = kb.psum()
                    kb.mm(ps, ps[:, 0:n], qiT[P, m, :], ki_sb[P, c0:c0 + n], True, True, r=[qiT, ki_sb])
                    rlt = rl[cn.rl % 3]; cn.rl += 1
                    kb.acopy(rlt[:, 0:n], ps[:, 0:n], r=[ps], w=[rlt], func=AF.Relu)
                    if h == 0:
                        kb.ts(dve, score[:, c0:c0 + n], rlt[:, 0:n], w_sb[:, 0:1], ALU.mult, r=[rlt, w_sb], w=[score])
                    else:
                        kb.stt(dve, score[:, c0:c0 + n], rlt[:, 0:n], w_sb[:, h:h + 1], score[:, c0:c0 + n], ALU.mult, ALU.add,
                               r=[rlt, w_sb, score], w=[score])
            mb = m01
            kb.op(dve, lambda e, Lk=Lk: e.tensor_reduce(out=Bt[:, :], in_=score[:, 0:Lk], axis=AX.X, op=ALU.max, apply_absolute_value=True),
                  r=[score], w=[Bt])
            kb.tt(dve, score[:, Lk - 256:Lk], score[:, Lk - 256:Lk], maskadd[:, :], ALU.add, r=[score, maskadd], w=[score])
            kb.ts(dve, lo[:, :], Bt[:, :], -1.001, ALU.mult, r=[Bt], w=[lo], s2=-1e-30, op1=ALU.add)
            kb.ts(dve, w0t[:, :], Bt[:, :], 2.003, ALU.mult, r=[Bt], w=[w0t], s2=2e-30, op1=ALU.add)
            kb.ts(dve, Wtab[:, :], pow2[:, :], w0t[:, :], ALU.mult, r=[pow2, w0t], w=[Wtab])
            kb.ts(dve, W2tab[:, :], Wtab[:, :], 2.0, ALU.mult, r=[Wtab], w=[W2tab])
            kb.tt(dve, mid[:, :], lo[:, :], Wtab[:, 0:1], ALU.add, r=[lo, Wtab], w=[mid])
            for it in range(NBISECT):
                kb.ts(dve, mb[:, 0:Lk], score[:, 0:Lk], mid[:, :], ALU.is_ge, r=[score, mid], w=[mb, cnt],
                      s2=None, op1=ALU.add, accum=cnt[:, :])
                if it < NBISECT - 1:
                    kb.stt(dve, stp[:, :], cnt[:, :], 255.5, W2tab[:, it + 1:it + 2], ALU.is_ge, ALU.mult, r=[cnt, W2tab], w=[stp])
                    kb.stt(dve, mid[:, :], stp[:, :], Wtab[:, it + 1:it + 2], mid[:, :], ALU.subtract, ALU.add, r=[stp, Wtab, mid], w=[mid])
                else:
                    kb.stt(dve, stp[:, :], cnt[:, :], 255.5, Wtab[:, it:it + 1], ALU.is_ge, ALU.mult, r=[cnt, Wtab], w=[stp])
                    kb.stt(dve, lo[:, :], stp[:, :], Wtab[:, it:it + 1], mid[:, :], ALU.subtract, ALU.add, r=[stp, Wtab, mid], w=[lo])
            kb.ts(dve, mb[:, 0:Lk], score[:, 0:Lk], lo[:, :], ALU.is_ge, r=[score, lo], w=[mb])

        def front_b(j):
            par = j % 2
            Lk = 256 * (j + 1)
            nblk = Lk // 128
            mb = m01
            mT = maskT[par]
            for b0 in range(0, nblk, 8):
                nb = min(8, nblk - b0)
                ps = kb.psum(); psb_ = ps[:, :].bitcast(BF16)
                for bi in range(nb):
                    kb.tr(ps, psb_[:, bi * 128:(bi + 1) * 128], mb[:, (b0 + bi) * 128:(b0 + bi + 1) * 128], C.ident[:, :], r=[mb, C.ident])
                kb.acopy(mT[:, b0:b0 + nb, :], psb_[:, 0:nb * 128].rearrange("p (b t) -> p b t", t=128), r=[ps], w=[mT])

        def back(j):
            par = j % 2
            Lk = 256 * (j + 1)
            mT = maskT[par]
            nblk = Lk // 128
            for m in range(4):
                ktb = kt_sb[cn.kt % 2]; vbuf = v_sb[cn.kt % 2]; cn.kt += 1
                kb.dma(sp, ktb[:, 0:Lk], KTs[m, :, 0:Lk], w=[ktb])
                kb.dma(sp, vbuf[:, 0:nblk, :, :], Vs[m, :, 0:nblk, :].rearrange("p b (a d) -> p b a d", d=65), w=[vbuf])
                pso = [kb.psb[6], kb.psb[7]]
                pend = []

                def emit_pv(pm, blk0, nb, pso=pso, vbuf=vbuf, nblk=nblk):
                    for bi in range(nb):
                        blk = blk0 + bi
                        for hh in range(2):
                            kb.mm(pso[hh], pso[hh][:, 0:65], pm[:, bi * 256 + hh * 128:bi * 256 + (hh + 1) * 128], vbuf[:, blk, hh, :],
                                  blk == 0, blk == nblk - 1, r=[pm, vbuf])

                for blk0 in range(0, nblk, 2):
                    nb = 2
                    ps = kb.psum()
                    for bi in range(nb):
                        c0 = (blk0 + bi) * 128
                        kb.mm(ps, ps[:, bi * 256:(bi + 1) * 256], ktb[:, c0:c0 + 128], qT[par][:, m, :, :].rearrange("p a t -> p (a t)"), True, True,
                              r=[qT[par], ktb])
                    pu = pTu[cn.pb % 3]; pm = pTm[cn.pb % 3]; cn.pb += 1
                    kb.acopy(pu[:, :], ps[:, :], r=[ps, negm[par]], w=[pu], func=AF.Exp, scale=0.125, bias=negm[par][:, :])
                    kb.tt(pool, pm[:, :].rearrange("p (b a t) -> p b a t", a=2, t=128), pu[:, :].rearrange("p (b a t) -> p b a t", a=2, t=128),
                          bc(mT[:, blk0:blk0 + nb, :].unsqueeze(2), [128, nb, 2, 128]), ALU.mult, r=[pu, mT], w=[pm])
                    pend.append((pm, blk0, nb))
                    if len(pend) > 2:
                        emit_pv(*pend.pop(0))
                while pend:
                    emit_pv(*pend.pop(0))
                for hh in range(2):
                    kb.acopy(ybu[:, 2 * m + hh, :], pso[hh][:, 0:65], r=[pso[hh]], w=[ybu])
            kb.op(dve, lambda e: e.reciprocal(out=rinv8[:, :], in_=ybu[:, :, 64]), r=[ybu], w=[rinv8])
            kb.tt(dve, yb[:, :].rearrange("p (h d) -> p h d", d=64), ybu[:, :, 0:64],
                  bc(rinv8[:, :].unsqueeze(2), [128, 8, 64]), ALU.mult, r=[ybu, rinv8], w=[yb])
            kb.dma(pool, YBs[j * 128:(j + 1) * 128, :], yb[:, :], r=[yb])

        front(0)
        front_b(0)
        for j in range(NQ):
            if j + 1 < NQ:
                front(j + 1)
            back(j)
            if j + 1 < NQ:
                front_b(j + 1)
        kb.barrier()
        kb.nrot = 8
        kb.es = es_save

    NP = NQ // 2
    with ExitStack() as p3:
        es_save = kb.es
        kb.es = p3
        Wg = kb.sb([128, KC, 2048], BF16, "Wg")
        load_cast(kb, C, Wg, lambda c0, n: Wg[:, :, c0:c0 + n],
                  lambda c0, n: w_gate.rearrange("(k p) n -> p k n", p=128)[:, :, c0:c0 + n], 2048, None, step=128)
        Worw = kb.sb([128, 4, D], BF16, "Worw"); Woat = kb.sb([128, 4, D], BF16, "Woat"); Wout = kb.sb([128, KC, D], BF16, "Wout")
        load_cast(kb, C, Worw, lambda c0, n: Worw[:, :, c0:c0 + n],
                  lambda c0, n: w_orw.rearrange("(k p) n -> p k n", p=128)[:, :, c0:c0 + n], D, None, step=256)
        load_cast(kb, C, Woat, lambda c0, n: Woat[:, :, c0:c0 + n],
                  lambda c0, n: w_oatt.rearrange("(k p) n -> p k n", p=128)[:, :, c0:c0 + n], D, None, step=256)
        load_cast(kb, C, Wout, lambda c0, n: Wout[:, :, c0:c0 + n],
                  lambda c0, n: w_out.rearrange("(k p) n -> p k n", p=128)[:, :, c0:c0 + n], D, None, step=128)
        uT2 = kb.sb([128, KC, 256], BF16, "uT2")
        gT = kb.sb([128, 16, 256], F32, "gT")
        x4 = [kb.sb([128, D], F32, "x4") for _ in range(4)]
        yab = [kb.sb([128, 512], BF16) for _ in range(4)]; ybb = [kb.sb([128, 512], BF16) for _ in range(4)]
        yaT = kb.sb([128, 4, 256], BF16); ybT = kb.sb([128, 4, 256], BF16)
        t1 = kb.sb([128, 512], F32); t2 = kb.sb([128, 512], F32)
        mT = kb.sb([128, 8, 256], BF16)
        h1 = [kb.sb([128, D], F32) for _ in range(2)]
        for jp in range(NP):
            for t in range(2):
                j = 2 * jp + t
                sl_ = (jp % 2) * 2 + t
                kb.dma(sp, x4[sl_][:, :], xown[j * 128:(j + 1) * 128, :], w=[x4[sl_]])
                kb.dma(sp, yab[sl_][:, :], YAs[j * 128:(j + 1) * 128, :], w=[yab[sl_]])
                kb.dma(sp, ybb[sl_][:, :], YBs[j * 128:(j + 1) * 128, :], w=[ybb[sl_]])
            for t in range(2):
                sl_ = (jp % 2) * 2 + t
                rms_uT(kb, C, x4[sl_], g1c, uT2, dst=uT2[:, :, t * 128:(t + 1) * 128])
                for src, dst in ((yab[sl_], yaT), (ybb[sl_], ybT)):
                    ps = kb.psum(); psb_ = ps[:, :].bitcast(BF16)
                    for m in range(4):
                        kb.tr(ps, psb_[:, m * 128:(m + 1) * 128], src[:, m * 128:(m + 1) * 128], C.ident[:, :], r=[src, C.ident])
                    kb.acopy(dst[:, :, t * 128:(t + 1) * 128], psb_[:, 0:512].rearrange("p (m t) -> p m t", t=128), r=[ps], w=[dst])
            for g0 in range(0, 16, 2):
                ps = kb.psum()
                for n_ in range(2):
                    for kc in range(KC):
                        kb.mm(ps, ps[:, n_ * 256:(n_ + 1) * 256], Wg[:, kc, (g0 + n_) * 128:(g0 + n_ + 1) * 128], uT2[:, kc, :],
                              kc == 0, kc == KC - 1, r=[Wg, uT2])
                kb.acopy(gT[:, g0:g0 + 2, :], ps[:, :].rearrange("p (m t) -> p m t", t=256), r=[ps], w=[gT], func=AF.Sigmoid)
            for q4 in range(4):
                psa = kb.psum(); psb2 = kb.psum()
                for n_ in range(2):
                    nn = q4 * 2 + n_
                    for kc in range(4):
                        kb.mm(psa, psa[:, n_ * 256:(n_ + 1) * 256], Worw[:, kc, nn * 128:(nn + 1) * 128], yaT[:, kc, :], kc == 0, kc == 3, r=[Worw, yaT])
                    for kc in range(4):
                        kb.mm(psb2, psb2[:, n_ * 256:(n_ + 1) * 256], Woat[:, kc, nn * 128:(nn + 1) * 128], ybT[:, kc, :], kc == 0, kc == 3, r=[Woat, ybT])
                kb.tt(dve, t1[:, :], psa[:, :], gT[:, q4 * 2:q4 * 2 + 2, :].rearrange("p m t -> p (m t)"), ALU.mult, r=[psa, gT], w=[t1])
                kb.tt(dve, t2[:, :], psb2[:, :], gT[:, 8 + q4 * 2:8 + q4 * 2 + 2, :].rearrange("p m t -> p (m t)"), ALU.mult, r=[psb2, gT], w=[t2])
                kb.tt(dve, mT[:, q4 * 2:q4 * 2 + 2, :].rearrange("p m t -> p (m t)"), t1[:, :], t2[:, :], ALU.add, r=[t1, t2], w=[mT])
            for t in range(2):
                j = 2 * jp + t
                sl_ = (jp % 2) * 2 + t
                h1t = h1[t]
                for half in range(2):
                    ps = kb.psum()
                    for kc in range(KC):
                        kb.mm(ps, ps[:, :], mT[:, kc, t * 128:(t + 1) * 128], Wout[:, kc, half * 512:(half + 1) * 512], kc == 0, kc == KC - 1, r=[mT, Wout])
                    kb.tt(dve, h1t[:, half * 512:(half + 1) * 512], ps[:, :], x4[sl_][:, half * 512:(half + 1) * 512], ALU.add, r=[ps, x4[sl_]], w=[h1t])
                kb.dma(pool, H1s[j * 128:(j + 1) * 128, :], h1t[:, :], r=[h1t])
        kb.barrier()
        kb.es = es_save

    with ExitStack() as p4:
        es_save = kb.es
        kb.es = p4
        Wfi = kb.sb([128, KC, 5632], BF16, "Wfi")
        load_cast(kb, C, Wfi, lambda c0, n: Wfi[:, :, c0:c0 + n],
                  lambda c0, n: w_ffi.rearrange("(k p) n -> p k n", p=128)[:, :, c0:c0 + n], 5632, None, step=128)
        Wfo = kb.sb([128, 22, D], BF16, "Wfo")
        load_cast(kb, C, Wfo, lambda c0, n: Wfo[:, :, c0:c0 + n],
                  lambda c0, n: w_ffo.rearrange("(k p) n -> p k n", p=128)[:, :, c0:c0 + n], D, None, step=32)
        gfb = kb.sb([128, D], F32); kb.dma(sp, gfb[:, :], gfrow.partition_broadcast(128), w=[gfb])
        uT2 = kb.sb([128, KC, 256], BF16, "uT2")
        aT = kb.sb([128, 22, 256], BF16, "aT")
        sl2 = [kb.sb([128, 512], F32) for _ in range(2)]
        h1 = [kb.sb([128, D], F32) for _ in range(4)]
        h2 = kb.sb([128, D], F32); ot = kb.sb([128, D], F32)
        for jp in range(NP):
            for t in range(2):
                j = 2 * jp + t
                h1t = h1[(jp % 2) * 2 + t]
                kb.dma(sp, h1t[:, :], H1s[j * 128:(j + 1) * 128, :], w=[h1t])
            for t in range(2):
                h1t = h1[(jp % 2) * 2 + t]
                rms_uT(kb, C, h1t, g2c, uT2, dst=uT2[:, :, t * 128:(t + 1) * 128])
            for g0 in range(0, 22, 2):
                psg = kb.psum(); psu = kb.psum()
                for n_ in range(2):
                    nn = g0 + n_
                    for kc in range(KC):
                        kb.mm(psg, psg[:, n_ * 256:(n_ + 1) * 256], Wfi[:, kc, nn * 128:(nn + 1) * 128], uT2[:, kc, :], kc == 0, kc == KC - 1, r=[Wfi, uT2])
                    for kc in range(KC):
                        kb.mm(psu, psu[:, n_ * 256:(n_ + 1) * 256], Wfi[:, kc, 2816 + nn * 128:2816 + (nn + 1) * 128], uT2[:, kc, :], kc == 0, kc == KC - 1, r=[Wfi, uT2])
                sl = sl2[(g0 // 2) % 2]
                kb.acopy(sl[:, :], psg[:, :], r=[psg], w=[sl], func=AF.Silu)
                kb.tt(dve, aT[:, g0:g0 + 2, :].rearrange("p m t -> p (m t)"), sl[:, :], psu[:, :], ALU.mult, r=[sl, psu], w=[aT])
            for t in range(2):
                j = 2 * jp + t
                h1t = h1[(jp % 2) * 2 + t]
                for half in range(2):
                    ps = kb.psum()
                    for n_ in range(22):
                        kb.mm(ps, ps[:, :], aT[:, n_, t * 128:(t + 1) * 128], Wfo[:, n_, half * 512:(half + 1) * 512], n_ == 0, n_ == 21, r=[aT, Wfo])
                    kb.tt(dve, h2[:, half * 512:(half + 1) * 512], ps[:, :], h1t[:, half * 512:(half + 1) * 512], ALU.add, r=[ps, h1t], w=[h2])
                kb.acopy(C.junk[:, 0:D], h2[:, :], r=[h2], w=[C.junk, C.ss], func=AF.Square, accum=C.ss[:, :])
                kb.acopy(C.rt[:, :], C.ss[:, :], r=[C.ss], w=[C.rt], func=AF.Sqrt, scale=1.0 / D, bias=C.epsn[:, :])
                kb.op(dve, lambda e: e.reciprocal(out=C.rstd[:, :], in_=C.rt[:, :]), r=[C.rt], w=[C.rstd])
                kb.stt(dve, ot[:, :], h2[:, :], C.rstd[:, :], gfb[:, :], ALU.mult, ALU.mult, r=[h2, C.rstd, gfb], w=[ot])
                kb.dma(pool, out[j * 128:(j + 1) * 128, :], ot[:, :], r=[ot])
        kb.barrier()
        kb.es = es_save
    kb.es.close()
    return nc


def _col(v, n):
    return np.ascontiguousarray(np.asarray(v, np.float32).reshape(n, 128).T)


def make_inputs(S, xb, p, W):
    NT = S // 128
    f = lambda a: np.ascontiguousarray(np.asarray(a, np.float32))
    w_in = W["w_in"]
    own_tiles = xb.reshape(NT // 2, 2, 128, D)[:, p].reshape(S // 2, D)
    half = 32
    inv = 1.0 / (10000.0 ** (np.arange(half, dtype=np.float32) * 2.0 / 64)).astype(np.float32)
    pos = np.arange(S, dtype=np.float32)
    ang = (pos[:, None] * inv[None, :]).astype(np.float32)
    cs = np.concatenate([np.cos(ang), np.sin(ang)], axis=1).astype(np.float32)
    cs_own = cs.reshape(NT // 2, 2, 128, 64)[:, p].reshape(S // 2, 64)
    t = np.arange(128)
    LL = np.concatenate([(t[:, None] <= t[None, :]), (t[:, None] < t[None, :])], axis=1).astype(np.float32) * np.float32(CDEC)
    maskU = np.concatenate([(t[:, None] < t[None, :]), (t[:, None] <= t[None, :])], axis=1).astype(np.float32)
    maskL = (t[:, None] > t[None, :]).astype(np.float32)
    bones = np.kron(np.eye(2, dtype=np.float32), np.ones((64, 64), np.float32))
    bones2 = np.kron(np.eye(2, dtype=np.float32), np.ones((64, 1), np.float32))
    s = np.arange(256)
    maskadd = np.where(s[None, :] <= (128 * p + t[:, None]), 0.0, NEG).astype(np.float32)
    par = np.zeros((128, 2), np.float32); par[:, 0] = p; par[:, 1] = 1 - p
    pow2 = np.tile((0.5 ** np.arange(1, NBISECT + 1, dtype=np.float64)).astype(np.float32)[None, :], (128, 1))
    m = {
        "xfull": f(xb), "xown": f(own_tiles),
        "g1col": _col(W["norm1_g"], 8), "g2col": _col(W["norm2_g"], 8), "gfrow": f(W["normf_g"]).reshape(1, D),
        "w_rw": f(w_in[:, 0:1792]),
        "w_kv": f(np.concatenate([w_in[:, 2304:2816], w_in[:, 2816:3328], w_in[:, 3840:3904]], axis=1)),
        "w_qi": f(np.concatenate([w_in[:, 1792:2304], w_in[:, 3328:3840], w_in[:, 3904:3912]], axis=1)),
        "w_gate": f(w_in[:, 3912:5960]),
        "mucol": _col(W["tshift_mu"], 14), "wdu": f(W["w_decay_up"]), "w0row": f(W["w0"]).reshape(1, 512),
        "aup": f(W["a_up"]), "a0col": _col(W["a0"], 4), "gup": f(W["g_up"]),
        "kkcol": _col(W["k_k"], 4), "kacol": _col(W["k_a"], 4), "rkcol": _col(np.asarray(W["r_k"]).reshape(512), 4),
        "lnxg": f(W["lnx_g"]).reshape(1, 512), "lnxb": f(W["lnx_b"]).reshape(1, 512),
        "w_orw": f(W["w_o_rwkv"]), "w_oatt": f(W["w_o_att"]), "w_out": f(W["w_out"]),
        "w_ffi": f(W["w_ffn_in"]), "w_ffo": f(W["w_ffn_out"]),
        "cs_full": cs, "cs_own": np.ascontiguousarray(cs_own),
        "identd": np.eye(128, dtype=np.float32), "LLd": LL, "maskUd": maskU, "maskLd": maskL,
        "bonesd": bones, "bones2d": bones2, "maskaddd": maskadd, "pard": par, "pow2d": pow2,
    }
    return m


_NC_CACHE = {}


def kernel(**inputs):
    x = np.asarray(inputs["x"], np.float32)
    B, S, _ = x.shape
    W = {k: np.asarray(v) for k, v in inputs.items() if k != "x"}
    if S not in _NC_CACHE:
        _NC_CACHE[S] = build(S)
    nc = _NC_CACHE[S]
    in_maps = []
    for c in range(2 * B):
        b, p = c // 2, c % 2
        in_maps.append(make_inputs(S, x[b], p, W))
    res = run_bass_kernel_spmd(nc, in_maps, core_ids=list(range(2 * B)))
    outp = np.zeros((B, S, D), np.float32)
    NT = S // 128
    for c in range(2 * B):
        b, p = c // 2, c % 2
        o = np.asarray(res.results[c]["out"], np.float32).reshape(NT // 2, 128, D)
        outp[b].reshape(NT // 2, 2, 128, D)[:, p] = o
    return outp
```
